# Optimizing a Trainium2 kernel written in Bass

```python
import math
import jax, jax.numpy as jnp
from jax import lax
import numpy as np

D_MODEL = 1024
BATCH = 8
SEQ = 4096
DEPTH = 1

N_META = 16
CHUNK = 128
Q_BLOCK = 128
PAD = CHUNK - N_META
D_SSD = 2 * D_MODEL
SSD_HEAD_DIM = 64
H_SSD = D_SSD // SSD_HEAD_DIM
SSD_GROUPS = 4
D_STATE = 128
CONV_K = 4
CONV_DIM = D_SSD + 2 * SSD_GROUPS * D_STATE
H_ATT = 16
ATT_HEAD_DIM = 64
D_ATT = H_ATT * ATT_HEAD_DIM
N_COLS = D_SSD + CONV_DIM + H_SSD + D_ATT + 3 * D_ATT + H_ATT + 2 * D_MODEL
EPS = 1e-6

kernel_name = "hybrid_ssd_fox_gated_merge"


def rmsnorm(x, g):
    xf = x.astype(jnp.float32)
    y = xf * lax.rsqrt(jnp.mean(xf * xf, axis=-1, keepdims=True) + EPS)
    return (y * g.astype(jnp.float32)).astype(x.dtype)


def gated_group_rmsnorm(y, z, g):
    u = (y * jax.nn.silu(z)).astype(jnp.float32)
    shp = u.shape
    u = u.reshape(shp[:-1] + (SSD_GROUPS, shp[-1] // SSD_GROUPS))
    u = u * lax.rsqrt(jnp.mean(u * u, axis=-1, keepdims=True) + EPS)
    return (u.reshape(shp) * g.astype(jnp.float32)).astype(y.dtype)


def causal_depthwise_conv(u, w, b):
    C = u.shape[-1]
    out = lax.conv_general_dilated(u, w[:, None, :].astype(u.dtype), window_strides=(1,),
                                   padding=[(CONV_K - 1, 0)],
                                   dimension_numbers=("NWC", "WIO", "NWC"),
                                   feature_group_count=C)
    return out + b


def ssd_chunked(xh, dt, a, bmat, cmat):
    Bsz, Lp, H, P = xh.shape
    G, N = bmat.shape[-2:]
    R = H // G
    nc = Lp // CHUNK
    xdt = (xh.astype(jnp.float32) * dt[..., None]).reshape(Bsz, nc, CHUNK, G, R, P)
    adt = (dt * a).reshape(Bsz, nc, CHUNK, G, R).transpose(0, 1, 3, 4, 2)
    a_cs = jnp.cumsum(adt, axis=-1)
    bm = bmat.astype(jnp.float32).reshape(Bsz, nc, CHUNK, G, N)
    cm = cmat.astype(jnp.float32).reshape(Bsz, nc, CHUNK, G, N)
    causal = jnp.tril(jnp.ones((CHUNK, CHUNK), dtype=bool))
    seg = a_cs[..., :, None] - a_cs[..., None, :]
    decay = jnp.exp(jnp.where(causal, seg, -jnp.inf))
    cb = jnp.einsum("bclgn,bcsgn->bcgls", cm, bm)
    y_diag = jnp.einsum("bcgls,bcgrls,bcsgrp->bclgrp", cb, decay, xdt)
    decay_states = jnp.exp(a_cs[..., -1:] - a_cs)
    states = jnp.einsum("bclgn,bcgrl,bclgrp->bcgrpn", bm, decay_states, xdt)
    chunk_decay = jnp.exp(a_cs[..., -1])

    def step(h, inp):
        s, d = inp
        return h * d[..., None, None] + s, h

    h0 = jnp.zeros((Bsz, G, R, P, N), jnp.float32)
    _, h_in = lax.scan(step, h0, (jnp.swapaxes(states, 0, 1), jnp.swapaxes(chunk_decay, 0, 1)))
    h_in = jnp.swapaxes(h_in, 0, 1)
    y_off = jnp.einsum("bclgn,bcgrpn,bcgrl->bclgrp", cm, h_in, jnp.exp(a_cs))
    return (y_diag + y_off).reshape(Bsz, Lp, H, P)


def forgetting_attention(q, k, v, logf):
    Bsz, Lp, H, Dh = q.shape
    scale = 1.0 / math.sqrt(Dh)
    c = jnp.cumsum(logf, axis=1)
    c_k = jnp.transpose(c, (0, 2, 1))
    nb = Lp // Q_BLOCK
    qb = q.reshape(Bsz, nb, Q_BLOCK, H, Dh).transpose(1, 0, 2, 3, 4)
    cqb = c.reshape(Bsz, nb, Q_BLOCK, H).transpose(1, 0, 3, 2)
    kpos = jnp.arange(Lp)

    def block(args):
        i, qi, cqi = args
        qpos = i * Q_BLOCK + jnp.arange(Q_BLOCK)
        s = jnp.einsum("bthd,bshd->bhts", qi, k).astype(jnp.float32) * scale
        s = s + (cqi[..., :, None] - c_k[..., None, :])
        s = jnp.where(kpos[None, :] <= qpos[:, None], s, -jnp.inf)
        p = jax.nn.softmax(s, axis=-1).astype(v.dtype)
        return jnp.einsum("bhts,bshd->bthd", p, v)

    out = lax.map(block, (jnp.arange(nb), qb, cqb))
    return out.transpose(1, 0, 2, 3, 4).reshape(Bsz, Lp, H, Dh)


def hybrid_layer(h, norm_pre, w_in, conv_w, conv_b, dt_bias, a_log, d_skip, ssd_norm,
                 fgate_bias, gate_bias, w_proj_ssd, w_proj_att, w_out, norm_post):
    Bsz, L, _ = h.shape
    Lp = L + PAD
    u = rmsnorm(h, norm_pre)
    proj = u @ w_in
    cuts = [D_SSD, D_SSD + CONV_DIM, D_SSD + CONV_DIM + H_SSD,
            D_SSD + CONV_DIM + H_SSD + D_ATT,
            D_SSD + CONV_DIM + H_SSD + 2 * D_ATT,
            D_SSD + CONV_DIM + H_SSD + 3 * D_ATT,
            D_SSD + CONV_DIM + H_SSD + 4 * D_ATT,
            D_SSD + CONV_DIM + H_SSD + 4 * D_ATT + H_ATT]
    z_ssd, xbc, dt_raw, z_att, q, k, v, f_raw, g_raw = jnp.split(proj, cuts, axis=-1)

    xbc = jax.nn.silu(causal_depthwise_conv(xbc, conv_w, conv_b))
    xs, bm, cm = jnp.split(xbc, [D_SSD, D_SSD + SSD_GROUPS * D_STATE], axis=-1)
    dt = jax.nn.softplus(dt_raw.astype(jnp.float32) + dt_bias.astype(jnp.float32))
    a = -jnp.exp(a_log.astype(jnp.float32))
    front = ((0, 0), (PAD, 0), (0, 0))
    xs_p = jnp.pad(xs, front).reshape(Bsz, Lp, H_SSD, SSD_HEAD_DIM)
    bm_p = jnp.pad(bm, front).reshape(Bsz, Lp, SSD_GROUPS, D_STATE)
    cm_p = jnp.pad(cm, front).reshape(Bsz, Lp, SSD_GROUPS, D_STATE)
    dt_p = jnp.pad(dt, front)
    y = ssd_chunked(xs_p, dt_p, a, bm_p, cm_p)
    y = y + d_skip.astype(jnp.float32)[:, None] * xs_p.astype(jnp.float32)
    y = y[:, PAD:].reshape(Bsz, L, D_SSD).astype(h.dtype)
    y_ssd = gated_group_rmsnorm(y, z_ssd, ssd_norm)

    logf = jax.nn.log_sigmoid(f_raw.astype(jnp.float32) + fgate_bias.astype(jnp.float32))
    back = ((0, 0), (0, PAD), (0, 0))
    qh = jnp.pad(q, back).reshape(Bsz, Lp, H_ATT, ATT_HEAD_DIM)
    kh = jnp.pad(k, back).reshape(Bsz, Lp, H_ATT, ATT_HEAD_DIM)
    vh = jnp.pad(v, back).reshape(Bsz, Lp, H_ATT, ATT_HEAD_DIM)
    o = forgetting_attention(qh, kh, vh, jnp.pad(logf, back))
    o = o[:, :L].reshape(Bsz, L, D_ATT)
    y_att = o * jax.nn.silu(z_att)

    gates = jax.nn.sigmoid(g_raw + gate_bias)
    g_ssd, g_att = jnp.split(gates, 2, axis=-1)
    merged = g_ssd * (y_ssd @ w_proj_ssd) + g_att * (y_att @ w_proj_att)
    return h + rmsnorm(merged @ w_out, norm_post)


def setup_inputs(seed: int = 0) -> dict:
    key = jax.random.key(seed)
    ks = jax.random.split(key, 16)
    D = D_MODEL
    nrm = jax.random.normal
    x = nrm(ks[0], (BATCH, SEQ, D), jnp.float32)
    meta_tokens = nrm(ks[1], (N_META, D), jnp.float32)
    norm_pre = 1.0 + 0.05 * nrm(ks[2], (DEPTH, D), jnp.float32)
    w_in = nrm(ks[3], (DEPTH, D, N_COLS), jnp.float32) * D ** -0.5
    conv_w = jax.random.uniform(ks[4], (DEPTH, CONV_K, CONV_DIM), jnp.float32, -0.5, 0.5)
    conv_b = 0.05 * nrm(ks[5], (DEPTH, CONV_DIM), jnp.float32)
    dt0 = jnp.exp(jax.random.uniform(ks[6], (DEPTH, H_SSD), jnp.float32,
                                     math.log(1e-3), math.log(1e-1)))
    dt_bias = dt0 + jnp.log(-jnp.expm1(-dt0))
    a_log = jnp.log(jax.random.uniform(ks[7], (DEPTH, H_SSD), jnp.float32, 1.0, 16.0))
    d_skip = 1.0 + 0.1 * nrm(ks[8], (DEPTH, H_SSD), jnp.float32)
    ssd_norm = 1.0 + 0.05 * nrm(ks[9], (DEPTH, D_SSD), jnp.float32)
    fgate_bias = jax.random.uniform(ks[10], (DEPTH, H_ATT), jnp.float32, 1.0, 6.0)
    gate_bias = 0.1 * nrm(ks[11], (DEPTH, 2 * D), jnp.float32)
    w_proj_ssd = nrm(ks[12], (DEPTH, D_SSD, D), jnp.float32) * D_SSD ** -0.5
    w_proj_att = nrm(ks[13], (DEPTH, D_ATT, D), jnp.float32) * D_ATT ** -0.5
    w_out = nrm(ks[14], (DEPTH, D, D), jnp.float32) * D ** -0.5
    norm_post = 1.0 + 0.05 * nrm(ks[15], (DEPTH, D), jnp.float32)
    return {"x": x, "meta_tokens": meta_tokens, "norm_pre": norm_pre, "w_in": w_in,
            "conv_w": conv_w, "conv_b": conv_b, "dt_bias": dt_bias, "a_log": a_log,
            "d_skip": d_skip, "ssd_norm": ssd_norm, "fgate_bias": fgate_bias,
            "gate_bias": gate_bias, "w_proj_ssd": w_proj_ssd, "w_proj_att": w_proj_att,
            "w_out": w_out, "norm_post": norm_post}


def reference(x, meta_tokens, norm_pre, w_in, conv_w, conv_b, dt_bias, a_log, d_skip,
              ssd_norm, fgate_bias, gate_bias, w_proj_ssd, w_proj_att, w_out, norm_post):
    Bsz = x.shape[0]
    meta = jnp.broadcast_to(meta_tokens[None].astype(x.dtype), (Bsz, N_META, D_MODEL))
    h = jnp.concatenate([meta, x], axis=1)
    for i in range(DEPTH):
        h = hybrid_layer(h, norm_pre[i], w_in[i], conv_w[i], conv_b[i], dt_bias[i], a_log[i],
                         d_skip[i], ssd_norm[i], fgate_bias[i], gate_bias[i], w_proj_ssd[i],
                         w_proj_att[i], w_out[i], norm_post[i])
    return h[:, N_META:]
```

```python
import numpy as np
from contextlib import ExitStack
import concourse.bass as bass
import concourse.mybir as mybir
from concourse.bass_utils import run_bass_kernel_spmd

F32 = mybir.dt.float32
BF16 = mybir.dt.bfloat16
AF = mybir.ActivationFunctionType
ALU = mybir.AluOpType

ENGS = ("pe", "act", "dve", "pool", "sp")

D = 1024
KC = 8
NCOLS = 11312
C_Z, C_XS, C_B, C_C, C_DT, C_ZA, C_Q, C_K, C_V, C_F, C_G = (0, 2048, 4096, 4608, 5120, 5152, 6176, 7200,
                                                          8224, 9248, 9264)
EPS = 1e-6
NMETA = 16


class Res:
    __slots__ = ("w", "r", "excl")

    def __init__(self, excl=False):
        self.w = None
        self.r = []
        self.excl = excl


class Prog:
    def __init__(self, nc, n_dma_sems=32):
        self.nc = nc
        self.q = {e: [] for e in ENGS}
        self.cnt = {e: 0 for e in ENGS}
        self.seen = {e: {} for e in ENGS}
        self.n_dma = n_dma_sems
        self.dma_cnt = [0] * n_dma_sems
        self.dma_rr = 0
        self.pending = {e: False for e in ENGS}

    def _deps(self, eng, reads, writes):
        toks = []
        for r in reads:
            if r.w is not None:
                toks.append(r.w)
            if r.excl:
                toks.extend(t for t in r.r if t[2] != eng)
        for w in writes:
            if w.w is not None:
                toks.append(w.w)
            toks.extend(w.r)
        need = {}
        for (k, v, e) in toks:
            if e == eng and eng == "pe":
                continue
            if need.get(k, 0) < v:
                need[k] = v
        out = []
        seen = self.seen[eng]
        for k, v in need.items():
            if seen.get(k, 0) >= v:
                continue
            seen[k] = v
            out.append((k, v))
        return out

    def _commit(self, tok, reads, writes):
        for r in reads:
            r.r.append(tok)
            if len(r.r) > 64:
                best = {}
                for t in r.r:
                    if best.get(t[0], (None, 0, None))[1] < t[1]:
                        best[t[0]] = t
                r.r = list(best.values())
        for w in writes:
            w.w = tok
            w.r = []

    def begin_capture(self):
        self.cap = []

    carry = ()

    def end_capture(self, hop=600.0, keep=1.0):
        ops = list(self.carry) + self.cap
        self.carry = ()
        self.cap = None
        n = len(ops)
        lastw = {}
        readers = {}
        preds = [set() for _ in range(n)]
        for i, (kind, eng, fn, reads, writes, sig, cost, lat) in enumerate(ops):
            for r in reads:
                k = id(r)
                if k in lastw:
                    preds[i].add(lastw[k])
            for w in writes:
                k = id(w)
                if k in lastw:
                    preds[i].add(lastw[k])
                for j in readers.get(k, ()):
                    preds[i].add(j)
            for r in reads:
                readers.setdefault(id(r), []).append(i)
            for w in writes:
                lastw[id(w)] = i
                readers[id(w)] = []
            preds[i].discard(i)
        succs = [[] for _ in range(n)]
        npred = [0] * n
        for i in range(n):
            npred[i] = len(preds[i])
            for j in preds[i]:
                succs[j].append(i)
        efree = {e: 0.0 for e in ENGS}
        fin = [0.0] * n
        rdy = [0.0] * n
        ready = [i for i in range(n) if npred[i] == 0]
        order = []
        import heapq
        while ready:
            best = None
            bt = None
            for i in ready:
                eng = ops[i][1]
                t = max(efree[eng], rdy[i])
                if bt is None or t < bt - 1e-9 or (abs(t - bt) <= 1e-9 and i < best):
                    bt = t
                    best = i
            i = best
            ready.remove(i)
            kind, eng, fn, reads, writes, sig, cost, lat = ops[i]
            efree[eng] = bt + cost
            fin[i] = bt + lat
            order.append(i)
            for j in succs[i]:
                npred[j] -= 1
                dly = fin[i] + (hop if ops[j][1] != eng or kind == "dma" else 0.0)
                if dly > rdy[j]:
                    rdy[j] = dly
                if npred[j] == 0:
                    ready.append(j)
        assert len(order) == n
        if keep < 1.0:
            n_emit = int(keep * n)
            left = sorted(order[n_emit:])
            self.carry = [ops[i] for i in left]
            order = order[:n_emit]
        for i in order:
            kind, eng, fn, reads, writes, sig, cost, lat = ops[i]
            if kind == "dma":
                self.dma(eng, fn, reads, writes, sem=sig)
            else:
                self.op(eng, fn, reads, writes, True)

    cap = None

    def op(self, eng, fn, reads=(), writes=(), sig=True, cost=300.0):
        if self.cap is not None:
            self.cap.append(("op", eng, fn, tuple(reads), tuple(writes), sig, cost, cost))
            return None
        waits = self._deps(eng, reads, writes)
        if sig:
            self.cnt[eng] += 1
            tok = (('e', eng), self.cnt[eng], eng)
            self.q[eng].append((waits, fn, ('e', eng), 1))
            self.pending[eng] = False
        else:
            tok = (('e', eng), self.cnt[eng] + 1, eng)
            self.q[eng].append((waits, fn, None, 0))
            self.pending[eng] = True
        self._commit(tok, reads, writes)
        return tok

    def new_sem(self):
        self.dma_cnt.append(0)
        return len(self.dma_cnt) - 1

    def dma(self, eng, fn, reads=(), writes=(), sem=None, cost=3000.0):
        if self.cap is not None:
            self.cap.append(("dma", eng, fn, tuple(reads), tuple(writes), sem, 100.0, cost))
            return None
        waits = self._deps(eng, reads, writes)
        if sem is None:
            i = self.dma_rr
            self.dma_rr = (self.dma_rr + 1) % self.n_dma
        else:
            i = sem
        self.dma_cnt[i] += 16
        tok = (('d', i), self.dma_cnt[i], 'dma')
        self.q[eng].append((waits, fn, ('d', i), 16))
        self._commit(tok, reads, writes)
        return tok

    def all_tokens(self):
        toks = []
        for e in ENGS:
            if self.cnt[e] > 0:
                toks.append((('e', e), self.cnt[e], e))
        for i in range(len(self.dma_cnt)):
            if self.dma_cnt[i] > 0:
                toks.append((('d', i), self.dma_cnt[i], 'dma'))
        return toks

    def wait_all(self, eng, toks):
        waits = []
        for (k, v, e) in toks:
            if self.seen[eng].get(k, 0) >= v:
                continue
            self.seen[eng][k] = v
            waits.append((k, v))
        if waits:
            self.q[eng].append((waits, None, None, 0))

    def barrier(self):
        for e in ENGS:
            assert not self.pending[e], "barrier with unsignalled op on " + e
        toks = self.all_tokens()
        for e in ENGS:
            self.wait_all(e, toks)

    def emit(self, sems):
        nc = self.nc
        q = self.q

        def run(e, name):
            for (waits, fn, sk, n) in q[name]:
                for (k, v) in waits:
                    e.wait_ge(sems[k], v)
                if fn is None:
                    continue
                inst = fn(e)
                if sk is not None:
                    inst.then_inc(sems[sk], n)

        with nc.Block() as block:
            @block.tensor
            def _(e):
                run(e, "pe")

            @block.scalar
            def _(e):
                run(e, "act")

            @block.vector
            def _(e):
                run(e, "dve")

            @block.gpsimd
            def _(e):
                run(e, "pool")

            @block.sync
            def _(e):
                run(e, "sp")


class Arena:
    def __init__(self, t, nbytes):
        self.t = t
        self.cap = nbytes
        self.off = 0
        self.peak = 0
        self.top = nbytes

    def alloc(self, shape, dt, parts=128):
        n = 1
        for s in shape:
            n *= s
        esz = 2 if dt == BF16 else 4
        nb = (n * esz + 63) // 64 * 64
        off = self.off
        self.off += nb
        self.peak = max(self.peak, self.off)
        assert self.off <= self.cap, ("arena overflow", self.off, self.cap)
        ap = self.t[0:parts, off // 2: off // 2 + (n * esz) // 2]
        if dt == F32:
            ap = ap.bitcast(F32)
        if len(shape) == 2:
            ap = ap.rearrange("p (a b) -> p a b", a=shape[0])
        elif len(shape) == 3:
            ap = ap.rearrange("p (a b c) -> p a b c", a=shape[0], b=shape[1])
        return ap

    def alloc_top(self, shape, dt, parts=128):
        n = 1
        for s_ in shape:
            n *= s_
        esz = 2 if dt == BF16 else 4
        nb = (n * esz + 63) // 64 * 64
        self.top -= nb
        save = self.off
        self.off = self.top
        ap = self.alloc(shape, dt, parts)
        self.off = save
        self.peak = max(self.peak, save)
        return ap

    def mark(self):
        return self.off

    def reset(self, m):
        self.off = m


def bc_mid(ap2, n):
    p, f = ap2.shape
    return ap2.unsqueeze(1).to_broadcast([p, n, f])


def bc_last(ap2, n):
    p, f = ap2.shape
    return ap2.unsqueeze(2).to_broadcast([p, f, n])


class StopBuild(Exception):
    pass


_FIN = [None]


def build(nc, SEQ, dbg=False, stop=None):
    try:
        return _build(nc, SEQ, dbg, stop)
    except StopBuild:
        return _FIN[0]()


def _build(nc, SEQ, dbg=False, stop=None):
    assert SEQ % 128 == 0
    T = SEQ + 128
    NB = T // 128
    TP = T
    NCH = NB
    NQT = (TP + 511) // 512
    NFILL = 128 - NMETA

    dram = {}

    def din(name, shape, dt=F32):
        dram[name] = nc.dram_tensor(name, shape, dt, kind="ExternalInput").ap()
        return dram[name]

    x_d = din("x", [SEQ, D])
    meta_d = din("meta", [NMETA, D])
    npre_d = din("norm_pre", [KC, 128])
    win_d = din("w_in", [D, NCOLS])
    convwb_d = din("conv_wb", [5, 3072])
    dtb_d = din("dt_bias", [1, 32])
    alog_d = din("a_log", [1, 32])
    dsk_d = din("d_skip", [1, 32])
    snrm_d = din("ssd_norm", [16, 128])
    fgb_d = din("fgate_bias", [16, 1])
    gb_d = din("gate_bias", [1, 2048])
    wps_d = din("w_proj_ssd", [2048, D])
    wpa_d = din("w_proj_att", [D, D])
    wo_d = din("w_out", [D, D])
    npost_d = din("norm_post", [1, D])
    out_d = nc.dram_tensor("out", [SEQ, D], F32, kind="ExternalOutput").ap()

    c3_d = nc.dram_tensor("c3_s", [16, 3, TP], BF16, kind="Internal").ap()
    gates_d = nc.dram_tensor("gates_s", [TP, 2048], BF16, kind="Internal").ap()
    yatt_d = nc.dram_tensor("yatt_s", [NB, 128, KC, 128], BF16, kind="Internal").ap()
    yssd_d = nc.dram_tensor("yssd_s", [NB, 128, 16, 128], BF16, kind="Internal").ap()
    wssd_d = nc.dram_tensor("wssd_s", [10, 128, KC * 512], BF16, kind="Internal").ap()
    dbg_d = {}
    if dbg:
        dbg_d["uT"] = nc.dram_tensor("dbg_uT", [128, KC, TP], BF16, kind="ExternalOutput").ap()
        dbg_d["yatt"] = nc.dram_tensor("dbg_yatt", [NB, 128, KC, 128], BF16, kind="ExternalOutput").ap()
        dbg_d["yssd"] = nc.dram_tensor("dbg_yssd", [NB, 128, 16, 128], BF16, kind="ExternalOutput").ap()
        dbg_d["gates"] = nc.dram_tensor("dbg_gates", [TP, 2048], BF16, kind="ExternalOutput").ap()
        dbg_d["c3"] = nc.dram_tensor("dbg_c3", [16, 3, TP], BF16, kind="ExternalOutput").ap()

    win_v = win_d.rearrange("(kc p) n -> p kc n", p=128)

    es = ExitStack()
    ARENA_BYTES = 206 * 1024
    arena_t = es.enter_context(nc.sbuf_tensor("arena", [128, ARENA_BYTES // 2], BF16))
    A = Arena(arena_t, ARENA_BYTES)
    PTs = [es.enter_context(nc.psum_tensor("ps%d" % i, [128, 1024], F32)) for i in range(4)]
    PR = [Res(excl=True) for _ in range(8)]

    def pbank(b):
        return PTs[b // 2][:, (b % 2) * 512:(b % 2) * 512 + 512]

    def pbank_bf(b):
        return pbank(b).bitcast(BF16)

    P = Prog(nc)
    sems = {}
    for e in ENGS:
        sems[('e', e)] = es.enter_context(nc.semaphore("s_" + e))
    for i in range(P.n_dma + 16):
        sems[('d', i)] = es.enter_context(nc.semaphore("d_%d" % i))

    def finish():
        if P.cap is not None:
            P.end_capture()
        P.barrier()
        P.emit(sems)
        es.close()
        return nc, A.peak

    _FIN[0] = finish

    def ck(name):
        if stop == name:
            raise StopBuild()

    def fsz(ap):
        n = 1
        for d in ap.shape[1:]:
            n *= d
        return n

    def mm(out, lhsT, rhs, start=True, stop=True, rd=(), wr=(), sig=True):
        c = 64.0 + max(fsz(rhs), 64) / 2.0
        if rhs.dtype == F32:
            c *= 4
        return P.op("pe", lambda e, o=out, l=lhsT, r=rhs, s=start, t=stop:
                    e.matmul(o, lhsT=l, rhs=r, start=s, stop=t), rd, wr, sig, cost=c)

    def tr(out, in_, ident, rd=(), wr=(), sig=True):
        return P.op("pe", lambda e, o=out, i=in_, d=ident: e.transpose(o, i, d), rd, wr, sig, cost=130.0)

    def act(out, in_, func, rd=(), wr=(), bias=None, scale=None, accum=None):
        kw = {}
        if bias is not None:
            kw["bias"] = bias
        if scale is not None:
            kw["scale"] = scale
        if accum is not None:
            kw["accum_out"] = accum
        c = 220.0 + max(fsz(in_), 64) / 1.2 + (100.0 if accum is not None else 0.0)
        return P.op("act", lambda e, o=out, i=in_, f=func, kw=kw: e.activation(out=o, in_=i, func=f, **kw), rd, wr,
                    cost=c)

    def ecost(eng, ap):
        n = max(fsz(ap), 64)
        return (120.0 + n / 0.96) if eng == "dve" else (300.0 + n / 0.6)

    def tt(eng, out, in0, in1, op, rd=(), wr=()):
        return P.op(eng, lambda e, o=out, a=in0, b=in1, p=op: e.tensor_tensor(out=o, in0=a, in1=b, op=p), rd, wr,
                    cost=ecost(eng, out))

    def ts(eng, out, in0, s1, s2, op0, op1, rd=(), wr=()):
        return P.op(eng, lambda e, o=out, a=in0, x=s1, y=s2, p0=op0, p1=op1:
                    e.tensor_scalar(out=o, in0=a, scalar1=x, scalar2=y, op0=p0, op1=p1), rd, wr,
                    cost=ecost(eng, out))

    def stt(out, in0, scalar, in1, op0, op1, rd=(), wr=()):
        return P.op("dve", lambda e, o=out, a=in0, s=scalar, b=in1, p0=op0, p1=op1:
                    e.scalar_tensor_tensor(out=o, in0=a, scalar=s, in1=b, op0=p0, op1=p1), rd, wr,
                    cost=ecost("dve", out))

    def cp(eng, out, in_, rd=(), wr=()):
        if eng == "act":
            return act(out, in_, AF.Copy, rd, wr)
        return P.op(eng, lambda e, o=out, i=in_: e.tensor_copy(out=o, in_=i), rd, wr, cost=ecost(eng, out))

    def memset(eng, ap, val, rd=(), wr=()):
        return P.op(eng, lambda e, a=ap, v=val: e.memset(a, v), rd, wr, cost=ecost(eng, ap))

    def dma(eng, out, in_, rd=(), wr=(), sem=None):
        nb = fsz(out) * out.shape[0] * (2 if out.dtype == BF16 else 4)
        return P.dma(eng, lambda e, o=out, i=in_: e.dma_start(out=o, in_=i), rd, wr, sem=sem,
                     cost=2500.0 + nb / 150.0)

    def recip(out, in_, rd=(), wr=()):
        return P.op("dve", lambda e, o=out, i=in_: e.reciprocal(out=o, in_=i), rd, wr,
                    cost=120.0 + 8.0 * max(fsz(out), 8))

    def asel(out, in_, pattern, cmp, fill, base, cm, rd=(), wr=()):
        return P.op("pool", lambda e, o=out, i=in_, p=pattern, c=cmp, f=fill, b=base, m=cm:
                    e.affine_select(out=o, in_=i, pattern=p, compare_op=c, fill=f, base=b, channel_multiplier=m),
                    rd, wr)

    Rc = Res()
    ones_f = A.alloc([128], F32)
    zeros_bf = A.alloc([4, 128], BF16)
    ones_bf = A.alloc([128], BF16)
    ident_bf = A.alloc([128], BF16)
    ident_f = A.alloc([128], F32)
    tri_bf = A.alloc([128], BF16)
    tri_f = A.alloc([128], F32)
    maskneg = A.alloc([4, 128], BF16)
    lsel128 = A.alloc([128], F32)
    negones_f = A.alloc([128], F32)
    memset("pool", ones_f, 1.0, wr=[Rc])
    memset("pool", ones_bf, 1.0, wr=[Rc])
    memset("pool", zeros_bf, 0.0, wr=[Rc])
    asel(ident_bf, ones_bf, [[-1, 128]], ALU.is_equal, 0.0, 0, 1, rd=[Rc], wr=[Rc])
    asel(ident_f, ones_f, [[-1, 128]], ALU.is_equal, 0.0, 0, 1, rd=[Rc], wr=[Rc])
    asel(tri_bf, ones_bf, [[1, 128]], ALU.is_ge, 0.0, 0, -1, rd=[Rc], wr=[Rc])
    asel(tri_f, ones_f, [[1, 128]], ALU.is_ge, 0.0, 0, -1, rd=[Rc], wr=[Rc])
    asel(maskneg, zeros_bf, [[0, 4], [1, 128]], ALU.is_ge, -30000.0, 0, -1, rd=[Rc], wr=[Rc])
    asel(lsel128, ones_f, [[0, 128]], ALU.is_equal, 0.0, -127, 1, rd=[Rc], wr=[Rc])
    memset("pool", negones_f, -1.0, wr=[Rc])
    epsc = A.alloc([1], F32)
    memset("pool", epsc, EPS, wr=[Rc])

    gpre = A.alloc([KC], F32)
    snrm = A.alloc([16], F32)
    cw = A.alloc([24, 5], F32)
    fbneg = A.alloc([1], F32, parts=16)
    dtb_bc = A.alloc([32], F32)
    a_bc = A.alloc([32], F32)
    dsk_bc = A.alloc([32], F32)
    Rp = Res()
    m0 = A.mark()
    t_np = A.alloc([128], F32, parts=KC)
    t_sn = A.alloc([128], F32, parts=16)
    t_cw = A.alloc([3072], F32, parts=5)
    dma("sp", t_np, npre_d, wr=[Rp], sem=P.new_sem())
    dma("sp", t_sn, snrm_d, wr=[Rp], sem=P.new_sem())
    dma("sp", t_cw, convwb_d, wr=[Rp], sem=P.new_sem())
    dma("sp", fbneg, fgb_d, wr=[Rp], sem=P.new_sem())
    dma("sp", dtb_bc, dtb_d[0].partition_broadcast(128), wr=[Rp])
    dma("sp", a_bc, alog_d[0].partition_broadcast(128), wr=[Rp])
    dma("sp", dsk_bc, dsk_d[0].partition_broadcast(128), wr=[Rp])
    tr(pbank(0)[:, 0:KC], t_np, ident_f[0:KC, 0:KC], rd=[Rp, Rc], wr=[PR[0]])
    cp("dve", gpre, pbank(0)[:, 0:KC], rd=[PR[0]], wr=[Rp])
    tr(pbank(1)[:, 0:16], t_sn, ident_f[0:16, 0:16], rd=[Rp, Rc], wr=[PR[1]])
    cp("dve", snrm, pbank(1)[:, 0:16], rd=[PR[1]], wr=[Rp])
    for cg in range(24):
        tr(pbank(2)[:, cg * 5:cg * 5 + 5], t_cw[:, cg * 128:(cg + 1) * 128], ident_f[0:5, 0:5],
           rd=[Rp, Rc], wr=[PR[2]], sig=(cg == 23))
    cp("dve", cw, pbank(2)[:, 0:120].rearrange("p (a b) -> p a b", a=24), rd=[PR[2]], wr=[Rp])
    ts("dve", fbneg, fbneg, -1.0, None, ALU.mult, ALU.bypass, rd=[Rp], wr=[Rp])
    act(a_bc, a_bc, AF.Exp, rd=[Rp], wr=[Rp])
    ts("dve", a_bc, a_bc, -1.0, None, ALU.mult, ALU.bypass, rd=[Rp], wr=[Rp])
    P.barrier()
    A.reset(m0)

    if stop == "C":
        return finish()
    m_const = A.mark()
    uT = A.alloc([KC, TP], BF16)
    Ru = [Res() for _ in range(NB)]
    negc = A.alloc([NB, 16], F32)
    Rnegc = Res()

    def ru(t0, t1):
        return [Ru[b] for b in range(t0 // 128, (t1 - 1) // 128 + 1)]

    m_u = A.mark()

    stg = []
    Rstg = []
    stg_i = [0]

    def new_stg(width):
        stg.clear()
        Rstg.clear()
        for _ in range(2):
            stg.append(A.alloc([KC, width], F32))
            Rstg.append(Res())

    def load_w(src_v, c0, ncols, dst, dstR, scl):
        nk = dst.shape[1]
        SW = stg[0].shape[2]
        for c in range(0, ncols, SW):
            w = min(SW, ncols - c)
            for k0 in range(0, nk, KC):
                k1 = min(nk, k0 + KC)
                i = stg_i[0] % 2
                stg_i[0] += 1
                dma("sp", stg[i][:, 0:k1 - k0, 0:w], src_v[:, k0:k1, c0 + c:c0 + c + w], wr=[Rstg[i]])
                for k in range(k0, k1):
                    if scl is not None:
                        ts("pool", dst[:, k, c:c + w], stg[i][:, k - k0, 0:w], scl[:, k:k + 1], 1.0,
                           ALU.mult, ALU.mult, rd=[Rstg[i], Rp], wr=[dstR])
                    else:
                        cp("pool", dst[:, k, c:c + w], stg[i][:, k - k0, 0:w], rd=[Rstg[i]], wr=[dstR])

    P.begin_capture()
    Rwssd = [Res() for _ in range(10)]

    xt = [A.alloc([D], F32) for _ in range(2)]
    Rxt = [Res(), Res()]
    xn = [A.alloc([D], BF16) for _ in range(2)]
    Rxn = [Res(), Res()]
    junk = A.alloc([D], BF16)
    ssq = [A.alloc([4], F32) for _ in range(2)]
    Rss = [Res(), Res()]

    def load_x_tile(m, buf, R):
        if m == 0:
            memset("pool", buf[0:NFILL, :], 0.0, wr=[R])
            dma("sp", buf[NFILL:128, :], meta_d, wr=[R])
        else:
            dma("sp", buf[:, :], x_d[128 * (m - 1):128 * m, :], wr=[R])
        return 128

    load_x_tile(0, xt[0], Rxt[0])
    for m in range(NB):
        i = m % 2
        t0 = 128 * m
        n = 128
        if m + 1 < NB:
            load_x_tile(m + 1, xt[1 - i], Rxt[1 - i])
        act(junk[0:n, :], xt[i][0:n, :], AF.Square, rd=[Rxt[i]], wr=[Rss[i]], accum=ssq[i][0:n, 0:1])
        act(ssq[i][0:n, 1:2], ssq[i][0:n, 0:1], AF.Sqrt, rd=[Rss[i]], wr=[Rss[i]], bias=EPS, scale=1.0 / D)
        recip(ssq[i][0:n, 2:3], ssq[i][0:n, 1:2], rd=[Rss[i]], wr=[Rss[i]])
        ts("dve", xn[i][0:n, :], xt[i][0:n, :], ssq[i][0:n, 2:3], None, ALU.mult, ALU.bypass,
           rd=[Rxt[i], Rss[i]], wr=[Rxn[i]])
        pb = pbank_bf(i).rearrange("p (a b) -> p a b", a=KC)
        for kc in range(KC):
            tr(pb[:, kc, 0:n], xn[i][0:n, kc * 128:(kc + 1) * 128], ident_bf[0:n, 0:n],
               rd=[Rxn[i], Rc], wr=[PR[i]], sig=(kc == KC - 1))
        cp("act" if m % 2 == 0 else "dve", uT[:, :, t0:t0 + n], pb[:, :, 0:n], rd=[PR[i]], wr=[Ru[m]])
    new_stg(256)
    wg = A.alloc([KC, 2048], BF16)
    Rwg = Res()
    gb_bc = A.alloc([2048], F32)
    Rgb = Res()
    dma("sp", gb_bc, gb_d[0].partition_broadcast(128), wr=[Rgb])
    load_w(win_v, C_G, 2048, wg, Rwg, gpre)
    gtmp = [A.alloc([512], F32) for _ in range(2)]
    Rgt = [Res(), Res()]
    gsb = [A.alloc([2048], BF16) for _ in range(2)]
    Rgs = [Res(), Res()]
    Rgates = [Res() for _ in range(NB)]
    k = 0
    for m in range(1, NB):
        t0 = 128 * m
        n = 128
        i = m % 2
        for qd in range(4):
            b = 4 + k % 4
            j = k % 2
            k += 1
            for kc in range(KC):
                mm(pbank(b)[0:n, :], uT[:, kc, t0:t0 + n], wg[:, kc, qd * 512:(qd + 1) * 512],
                   start=(kc == 0), stop=(kc == KC - 1), rd=[Rwg, Ru[m]], wr=[PR[b]], sig=(kc == KC - 1))
            tt("dve", gtmp[j][0:n, :], pbank(b)[0:n, :], gb_bc[0:n, qd * 512:(qd + 1) * 512], ALU.add,
               rd=[PR[b], Rgb], wr=[Rgt[j]])
            act(gsb[i][0:n, qd * 512:(qd + 1) * 512], gtmp[j][0:n, :], AF.Sigmoid, rd=[Rgt[j]], wr=[Rgs[i]])
        dma("sp", gates_d[t0:t0 + n, :], gsb[i][0:n, :], rd=[Rgs[i]], wr=[Rgates[m]])

    P.end_capture()
    P.barrier()
    A.reset(m_u)
    if dbg:
        dma("sp", dbg_d["uT"], uT, rd=Ru)
        dma("sp", dbg_d["gates"][128:T, :], gates_d[128:T, :], rd=Rgates)

    if stop == "P0":
        return finish()
    stg128 = [A.alloc_top([KC, 128], F32) for _ in range(2)]
    Rstg128 = [Res(), Res()]
    wv = A.alloc_top([KC, 512], BF16)
    Rwv = Res()
    wp2 = [A.alloc_top([KC, 384], BF16) for _ in range(2)]
    Rwp2 = [Res(), Res()]
    wtmp = [A.alloc_top([KC, 128], BF16) for _ in range(2)]
    Rwtmp = [Res(), Res()]

    def use_stg128():
        stg.clear()
        Rstg.clear()
        stg.extend(stg128)
        Rstg.extend(Rstg128)

    def load_pair_w(gp):
        wpb = wp2[gp % 2]
        load_w(win_v, C_Q + gp * 128, 128, wpb[:, :, 0:128], Rwp2[gp % 2], gpre)
        load_w(win_v, C_K + gp * 128, 128, wpb[:, :, 128:256], Rwp2[gp % 2], gpre)
        load_w(win_v, C_ZA + gp * 128, 128, wpb[:, :, 256:384], Rwp2[gp % 2], gpre)

    use_stg128()
    load_w(win_v, C_V, 512, wv, Rwv, gpre)
    load_pair_w(0)

    new_stg(256)
    ones16 = A.alloc([512], F32, parts=16)
    memset("pool", ones16, 1.0, wr=[Rc])
    wf = A.alloc([KC, 16], BF16)
    Rwf = Res()
    load_w(win_v, C_F, 16, wf, Rwf, gpre)
    lT = A.alloc([TP], F32, parts=16)
    cT = A.alloc([TP], F32, parts=16)
    c3 = A.alloc([3, TP], BF16, parts=16)
    ebuf = [A.alloc([512], F32, parts=16) for _ in range(2)]
    Reb = [Res(), Res()]
    RlT = Res()
    RcT = Res()
    Rc3 = Res()
    for I in range(NQT):
        t0 = 512 * I
        w = min(512, TP - t0)
        b = I % 2
        for kc in range(KC):
            mm(pbank(b)[0:16, 0:w], wf[:, kc, :], uT[:, kc, t0:t0 + w], start=(kc == 0), stop=(kc == KC - 1),
               rd=[Rwf] + ru(t0, t0 + w), wr=[PR[b]], sig=(kc == KC - 1))
        act(ebuf[b][:, 0:w], pbank(b)[0:16, 0:w], AF.Exp, rd=[PR[b], Rp], wr=[Reb[b]], bias=fbneg[:, 0:1], scale=-1.0)
        act(lT[:, t0:t0 + w], ebuf[b][:, 0:w], AF.Ln, rd=[Reb[b]], wr=[RlT], bias=1.0)
    memset("dve", lT[:, 0:NFILL], 0.0, rd=[RlT], wr=[RlT])
    for I in range(NQT):
        t0 = 512 * I
        w = min(512, TP - t0)
        init = 0.0 if I == 0 else cT[:, t0 - 1:t0]
        P.op("dve", lambda e, o=cT[:, t0:t0 + w], d0=ones16[:, 0:w], d1=lT[:, t0:t0 + w], ini=init:
             e.tensor_tensor_scan(out=o, data0=d0, data1=d1, initial=ini, op0=ALU.mult, op1=ALU.subtract),
             [RlT, RcT, Rc], [RcT])
    cp("dve", c3[:, 0, :], cT, rd=[RcT], wr=[Rc3])
    tt("dve", lT, cT, c3[:, 0, :], ALU.subtract, rd=[RcT, Rc3, RlT], wr=[RlT])
    cp("dve", c3[:, 1, :], lT, rd=[RlT], wr=[Rc3])
    nb_a = min(NB, 32)
    for m in range(NB):
        bank = 2 if m < 32 else 3
        col = (m % 32) * 16
        tr(pbank(bank)[:, col:col + 16], cT[:, 128 * m:128 * m + 128], ident_f[0:16, 0:16],
           rd=[RcT, Rc], wr=[PR[bank]], sig=(m == nb_a - 1 or m == NB - 1))
    ts("dve", negc[:, 0:nb_a, :], pbank(2)[:, 0:nb_a * 16].rearrange("p (a b) -> p a b", b=16), -1.0, None,
       ALU.mult, ALU.bypass, rd=[PR[2]], wr=[Rnegc])
    if NB > 32:
        ts("dve", negc[:, 32:NB, :], pbank(3)[:, 0:(NB - 32) * 16].rearrange("p (a b) -> p a b", b=16), -1.0, None,
           ALU.mult, ALU.bypass, rd=[PR[3]], wr=[Rnegc])
    tt("dve", cT, lT, c3[:, 1, :], ALU.subtract, rd=[RlT, Rc3, RcT, PR[2], PR[3]], wr=[RcT])
    cp("dve", c3[:, 2, :], cT, rd=[RcT], wr=[Rc3])
    Rc3d = Res()
    dma("sp", c3_d, c3, rd=[Rc3], wr=[Rc3d])
    if dbg:
        dma("sp", dbg_d["c3"], c3, rd=[Rc3])
    P.barrier()
    A.reset(m_u)

    if stop == "P1a":
        return finish()
    if stop == "P1b":
        return finish()
    use_stg128()
    Vaug = A.alloc([NB, 8, 65], BF16)
    Rv = [Res() for _ in range(NB)]
    Rvone = Res()
    QT = [A.alloc([TP], BF16, parts=67) for _ in range(2)]
    KT = [A.alloc([TP], BF16, parts=67) for _ in range(2)]
    zs_off = A.mark()
    ZS = [A.alloc([TP], BF16, parts=64) for _ in range(2)]
    Rq = [[Res() for _ in range(NQT)] for _ in range(2)]
    Rk = [[Res() for _ in range(NQT)] for _ in range(2)]
    Rz = [[Res() for _ in range(NQT)] for _ in range(2)]
    Rqc = [Res(), Res()]
    qc_sem = [P.new_sem(), P.new_sem()]
    Rkone = Res()
    NPT = 5
    PTb = [A.alloc([512], BF16) for _ in range(NPT)]
    Rpt = [Res() for _ in range(NPT)]
    rsb = [arena_t[0:65, (zs_off + 2048 * i) // 2:(zs_off + 2048 * i) // 2 + 1024].bitcast(F32) for i in range(2)]
    Rrs = [Res(), Res()]
    etmp = [A.alloc([512], F32, parts=64) for _ in range(2)]
    Ret = [Res(), Res()]
    yT = [A.alloc([512], BF16, parts=64) for _ in range(2)]
    RyT = [Res(), Res()]
    Ryatt = [Res() for _ in range(NQT)]

    memset("pool", Vaug[:, :, :, 64:65], 1.0, wr=[Rvone])
    for s in range(2):
        memset("pool", KT[s][64:67, :], 1.0, wr=[Rkone])
    st_i = 0
    pt_i = 0
    ep_i = 0
    ot_i = 0
    pw_i = [0]

    def pw_slice(npieces):
        for _ in range(npieces):
            k_ = pw_i[0]
            if k_ >= 40:
                return
            pw_i[0] += 1
            g, c = k_ // 4, (k_ % 4) * 128
            i = k_ % 2
            load_w(win_v, C_Z + g * 512 + c, 128, wtmp[i], Rwtmp[i], gpre)
            dma("sp", wssd_d[g].rearrange("p (a b) -> p a b", a=KC)[:, :, c:c + 128], wtmp[i], rd=[Rwtmp[i]],
                wr=[Rwssd[g]])

    for half in range(2):
        for m in range(NB):
            b = m % 2
            for kc in range(KC):
                mm(pbank(b), uT[:, kc, 128 * m:128 * m + 128], wv[:, kc, :], start=(kc == 0), stop=(kc == KC - 1),
                   rd=[Rwv, Ru[m]], wr=[PR[b]], sig=(kc == KC - 1))
            cp("dve" if m % 2 == 0 else "act", Vaug[:, m, :, 0:64], pbank(b).rearrange("p (a b) -> p a b", a=8),
               rd=[PR[b]], wr=[Rv[m]])
            if m == 0:
                memset("pool", Vaug[0:NFILL, 0, :, :], 0.0, rd=[Rv[0], Rvone], wr=[Rv[0], Rvone])
        if half == 0:
            load_w(win_v, C_V + 512, 512, wv, Rwv, gpre)
        for hp in range(4):
            gp = half * 4 + hp
            wp = wp2[gp % 2]
            Rwp = Rwp2[gp % 2]
            for s in range(2):
                dma("sp", QT[s][64:67, :], c3_d[2 * gp + s], rd=[Rc3d], wr=[Rqc[s]], sem=qc_sem[s])
            for I in range(NQT):
                t0 = 512 * I
                w = min(512, TP - t0)
                for (ci, dst, RR, fn, scale) in ((0, QT, Rq, AF.Copy, 0.125), (1, KT, Rk, AF.Copy, None),
                                                 (2, ZS, Rz, AF.Silu, None)):
                    b = (I * 3 + ci) % 2
                    for kc in range(KC):
                        mm(pbank(b)[:, 0:w], wp[:, kc, ci * 128:(ci + 1) * 128], uT[:, kc, t0:t0 + w],
                           start=(kc == 0), stop=(kc == KC - 1), rd=[Rwp] + ru(t0, t0 + w), wr=[PR[b]],
                           sig=(kc == KC - 1))
                    if fn == AF.Silu:
                        act(dst[0][0:64, t0:t0 + w], pbank(b)[0:64, 0:w], fn, rd=[PR[b]], wr=[RR[0][I]])
                    else:
                        ts("dve", dst[0][0:64, t0:t0 + w], pbank(b)[0:64, 0:w], scale if scale else 1.0, None,
                           ALU.mult, ALU.bypass, rd=[PR[b]], wr=[RR[0][I]])
                    act(dst[1][0:64, t0:t0 + w], pbank(b)[64:128, 0:w], fn, rd=[PR[b]], wr=[RR[1][I]], scale=scale)
            if gp + 1 < 8:
                load_pair_w(gp + 1)
            pw_slice(5)
            for s in range(2):
                h = 2 * gp + s
                hl = h % 8
                steps = []
                for I in range(NQT):
                    t0 = 512 * I
                    w = min(512, TP - t0)
                    jmax = (t0 + w) // 128 - 1
                    for j in range(jmax + 1):
                        steps.append((I, j, t0, w, jmax))
                LA = 4
                infl = {}
                obs = {}
                deferred = []

                def emit_st(k):
                    nonlocal st_i, pt_i
                    I, j, t0, w, jmax = steps[k]
                    r = j - 4 * I
                    qlo = 128 * r if r >= 0 else 0
                    N = w - qlo
                    sb_ = st_i % 5
                    st_i += 1
                    pi = pt_i % NPT
                    pt_i += 1
                    mm(pbank(sb_)[:, 0:N], KT[s][0:67, 128 * j:128 * j + 128], QT[s][0:67, t0 + qlo:t0 + w],
                       rd=[Rk[s][j // 4], Rkone, Rq[s][I], Rqc[s]], wr=[PR[sb_]])
                    act(PTb[pi][:, 0:N], pbank(sb_)[:, 0:N], AF.Exp, rd=[PR[sb_], Rnegc], wr=[Rpt[pi]],
                        bias=negc[:, j, h:h + 1])
                    if r >= 0:
                        tt("pool", PTb[pi][:, 0:128], PTb[pi][:, 0:128], tri_bf, ALU.mult,
                           rd=[Rpt[pi], Rc], wr=[Rpt[pi]])
                    infl[k] = (pi, qlo, N)

                def emit_pv(k):
                    nonlocal ot_i, ep_i
                    I, j, t0, w, jmax = steps[k]
                    pi, qlo, N = infl.pop(k)
                    if j == 0:
                        obs[I] = 5 + (ot_i % 2)
                        ot_i += 1
                    ob = obs[I]
                    mm(pbank(ob)[0:65, qlo:w], Vaug[:, j, hl, :], PTb[pi][:, 0:N], start=(j == 0),
                       stop=(j == jmax), rd=[Rv[j], Rvone, Rpt[pi]], wr=[PR[ob]], sig=(j == jmax))
                    if j != jmax:
                        return
                    e_ = ep_i % 2
                    ep_i += 1
                    if I == 0:
                        ts("dve", rsb[e_][64:65, 0:w], pbank(ob)[64:65, 0:w], 1e-30, None, ALU.add, ALU.bypass,
                           rd=[PR[ob]], wr=[Rrs[e_]])
                        recip(rsb[e_][64:65, 0:w], rsb[e_][64:65, 0:w], rd=[Rrs[e_]], wr=[Rrs[e_]])
                    else:
                        recip(rsb[e_][64:65, 0:w], pbank(ob)[64:65, 0:w], rd=[PR[ob]], wr=[Rrs[e_]])
                    tt("dve", etmp[e_][:, 0:w], pbank(ob)[0:64, 0:w], ZS[s][:, t0:t0 + w], ALU.mult,
                       rd=[PR[ob], Rz[s][I]], wr=[Ret[e_]])

                    def part2(e_=e_, w=w, t0=t0, I=I):
                        mm(pbank(7)[0:64, 0:w], ones_f[64:65, 0:64], rsb[e_][64:65, 0:w], rd=[Rrs[e_], Rc],
                           wr=[PR[7]])
                        tt("dve", yT[e_][:, 0:w], etmp[e_][:, 0:w], pbank(7)[0:64, 0:w], ALU.mult,
                           rd=[Ret[e_], PR[7]], wr=[RyT[e_]])
                        nt = w // 128
                        dma("sp", yatt_d[4 * I:4 * I + nt, s * 64:(s + 1) * 64, gp, :].rearrange("m p t -> p m t"),
                            yT[e_][:, 0:w].rearrange("p (m t) -> p m t", t=128), rd=[RyT[e_]], wr=[Ryatt[I]])
                    deferred.append((k + LA + 10, part2))

                k = 0
                while k < len(steps) + LA or deferred:
                    if k < len(steps):
                        emit_st(k)
                    if 0 <= k - LA < len(steps):
                        emit_pv(k - LA)
                    while deferred and (deferred[0][0] <= k or k >= len(steps) + LA):
                        deferred.pop(0)[1]()
                    k += 1
    P.barrier()
    A.reset(m_u)
    if dbg:
        dma("sp", dbg_d["yatt"], yatt_d, rd=Ryatt)

    if stop == "P2":
        return finish()
    A.reset(m_const)
    uT_ = A.alloc([KC, TP], BF16)
    wdt = A.alloc([KC, 32], BF16)
    Rwdt = Res()
    m3 = A.mark()
    new_stg(32)
    load_w(win_v, C_DT, 32, wdt, Rwdt, gpre)
    P.barrier()
    A.reset(m3)
    wring = [A.alloc([KC, 512], BF16) for _ in range(2)]
    Rwr = [Res(), Res()]
    wr_i = [0]
    xbcT = A.alloc([24, 512], BF16)
    Rxbc = [Res() for _ in range(24)]
    convb = [A.alloc([516], F32) for _ in range(2)]
    Rcb = [Res(), Res()]
    cacc = [A.alloc([512], F32) for _ in range(2)]
    Rca = [Res(), Res()]
    halo = A.alloc([24, 4], F32)
    Rhalo = [Res() for _ in range(24)]
    hst = A.alloc([2048], F32)
    hbf = A.alloc([2048], BF16)
    Rhst = [Res() for _ in range(4)]
    Rhbf = [Res() for _ in range(4)]
    sz = [A.alloc([2048], BF16) for _ in range(4)]
    Rsz = [[Res() for _ in range(4)] for _ in range(4)]
    xsD = A.alloc([2048], BF16)
    RxsD = Res()
    xdt = A.alloc([2048], BF16)
    Rxdt = Res()
    xdtS = [A.alloc([2048], BF16) for _ in range(2)]
    RxdtS = [Res(), Res()]
    Btok = [A.alloc([512], BF16) for _ in range(2)]
    RBt = [Res(), Res()]
    sm2 = [A.alloc([10, 32], F32) for _ in range(2)]
    Rsm2 = [[Res() for _ in range(10)] for _ in range(2)]
    DTX, EDT, DT, DTA, ACS, NACS, EACS, DD, DS, CD = range(10)
    CBm = A.alloc([4, 128], BF16)
    RCBm = Res()
    Dg = [A.alloc([4, 128], F32) for _ in range(2)]
    RDg = [Res(), Res()]
    decT = [A.alloc([4, 128], BF16) for _ in range(2)]
    Rdec = [Res(), Res()]
    MT = [A.alloc([4, 128], BF16) for _ in range(2)]
    RMT = [Res(), Res()]
    NYD = 5
    ydg = [A.alloc([512], F32) for _ in range(NYD)]
    Rydg = [Res() for _ in range(NYD)]
    yg = [A.alloc([512], F32) for _ in range(2)]
    Ryg = [Res(), Res()]
    sqj = A.alloc([512], F32)
    gst = [A.alloc([4], F32) for _ in range(2)]
    Rgst = [Res(), Res()]
    yn = A.alloc([2048], BF16)
    Ryn = [Res() for _ in range(4)]
    ynT = A.alloc([16, 128], BF16)
    RynT = Res()
    Ryssd = [Res() for _ in range(NCH)]
    memset("pool", halo, 0.0, wr=Rhalo)

    psA_i = [0]
    psB_i = [0]

    def psA():
        b = psA_i[0] % 4
        psA_i[0] += 1
        return b

    def psB():
        b = 4 + psB_i[0] % 4
        psB_i[0] += 1
        return b

    def get_w(g):
        i = wr_i[0] % 2
        wr_i[0] += 1
        dma("sp", wring[i].rearrange("p a b -> p (a b)"), wssd_d[g], rd=[Rwssd[g]], wr=[Rwr[i]])
        return i

    chunks = [(128 * c, 128) for c in range(NCH)]
    tiles = [list(range(k, min(k + 4, NCH))) for k in range(0, NCH, 4)]
    cnt = {"cv": 0, "yd": 0, "q": 0, "g": 0}

    def z_pass(tl, ps):
        for g in range(4):
            wi = get_w(g)
            for ci, c in enumerate(tl):
                tok0, Lc = chunks[c]
                b = ps()
                for kc in range(KC):
                    mm(pbank(b), uT[:, kc, tok0:tok0 + Lc], wring[wi][:, kc, :], start=(kc == 0),
                       stop=(kc == KC - 1), rd=[Rwr[wi]] + ru(tok0, tok0 + Lc), wr=[PR[b]], sig=(kc == KC - 1))
                act(sz[ci][:, g * 512:(g + 1) * 512], pbank(b), AF.Silu, rd=[PR[b]], wr=[Rsz[ci][g]])
                yield

    def conv_pass(tl, ps):
        ts_ = chunks[tl[0]][0]
        te_ = chunks[tl[-1]][0] + 128
        Wk = te_ - ts_
        for wgi in range(6):
            wi = get_w(4 + wgi)
            for cgl in range(4):
                cg = wgi * 4 + cgl
                b = ps()
                for kc in range(KC):
                    mm(pbank(b)[:, 0:Wk], wring[wi][:, kc, cgl * 128:(cgl + 1) * 128], uT[:, kc, ts_:te_],
                       start=(kc == 0), stop=(kc == KC - 1), rd=[Rwr[wi]] + ru(ts_, te_), wr=[PR[b]],
                       sig=(kc == KC - 1))
                v = cnt["cv"] % 2
                cnt["cv"] += 1
                cp("act", convb[v][:, 3:3 + Wk], pbank(b)[:, 0:Wk], rd=[PR[b]], wr=[Rcb[v]])
                cp("pool", convb[v][:, 0:3], halo[:, cg, 0:3], rd=[Rhalo[cg]], wr=[Rcb[v]])
                cp("pool", halo[:, cg, 0:3], convb[v][:, Wk:Wk + 3], rd=[Rcb[v]], wr=[Rhalo[cg]])
                act(cacc[v][:, 0:Wk], pbank(b)[:, 0:Wk], AF.Identity, rd=[PR[b], Rp], wr=[Rca[v]],
                    bias=cw[:, cg, 4:5], scale=cw[:, cg, 3:4])
                for kk in range(0, 3):
                    stt(cacc[v][:, 0:Wk], convb[v][:, kk:kk + Wk], cw[:, cg, kk:kk + 1], cacc[v][:, 0:Wk],
                        ALU.mult, ALU.add, rd=[Rcb[v], Rca[v], Rp], wr=[Rca[v]])
                act(xbcT[:, cg, 0:Wk], cacc[v][:, 0:Wk], AF.Silu, rd=[Rca[v]], wr=[Rxbc[cg]])
                if tl[0] == 0:
                    memset("pool", xbcT[:, cg, 0:NFILL], 0.0, rd=[Rxbc[cg]], wr=[Rxbc[cg]])
                yield

    def stageA(c, ts_):
        tok0, Lc = chunks[c]
        off = tok0 - ts_
        first = (c == 0)
        p = c % 2
        sm = sm2[p]
        Rsm = Rsm2[p]
        b = psA()
        for kc in range(KC):
            mm(pbank(b)[:, 0:32], uT[:, kc, tok0:tok0 + Lc], wdt[:, kc, :], start=(kc == 0),
               stop=(kc == KC - 1), rd=[Rwdt] + ru(tok0, tok0 + Lc), wr=[PR[b]], sig=(kc == KC - 1))
        tt("dve", sm[:, DTX, :], pbank(b)[:, 0:32], dtb_bc, ALU.add, rd=[PR[b], Rp], wr=[Rsm[DTX]])
        act(sm[:, EDT, :], sm[:, DTX, :], AF.Exp, rd=[Rsm[DTX]], wr=[Rsm[EDT]])
        act(sm[:, DT, :], sm[:, EDT, :], AF.Ln, rd=[Rsm[EDT]], wr=[Rsm[DT]], bias=1.0)
        if first:
            memset("dve", sm[0:NFILL, DT, :], 0.0, rd=[Rsm[DT]], wr=[Rsm[DT]])
        tt("dve", sm[:, DTA, :], sm[:, DT, :], a_bc, ALU.mult, rd=[Rsm[DT], Rp], wr=[Rsm[DTA]])
        yield
        for hb in range(2):
            b = psA()
            pv = pbank_bf(b).rearrange("p (a b) -> p a b", a=8)
            for k8 in range(8):
                cg = hb * 8 + k8
                tr(pv[:, k8, :], xbcT[:, cg, off:off + Lc], ident_bf, rd=[Rxbc[cg], Rc], wr=[PR[b]], sig=(k8 == 7))
            tt("dve", xdt[:, hb * 1024:(hb + 1) * 1024].rearrange("p (a b) -> p a b", a=16),
               pbank_bf(b).rearrange("p (a b) -> p a b", a=16),
               bc_last(sm[:, DT, hb * 16:(hb + 1) * 16], 64), ALU.mult, rd=[PR[b], Rsm[DT]], wr=[Rxdt])
            cp("act", xsD[:, hb * 1024:(hb + 1) * 1024], pbank_bf(b), rd=[PR[b]], wr=[RxsD])
            tt("pool", xsD[:, hb * 1024:(hb + 1) * 1024].rearrange("p (a b) -> p a b", a=16),
               xsD[:, hb * 1024:(hb + 1) * 1024].rearrange("p (a b) -> p a b", a=16),
               bc_last(dsk_bc[:, hb * 16:(hb + 1) * 16], 64), ALU.mult, rd=[RxsD, Rp], wr=[RxsD])
            yield
        b = psA()
        mm(pbank(b)[:, 0:32], tri_f, sm[:, DTA, :], rd=[Rsm[DTA], Rc], wr=[PR[b]])
        cp("act", sm[:, ACS, :], pbank(b)[:, 0:32], rd=[PR[b]], wr=[Rsm[ACS]])
        act(sm[:, EACS, :], pbank(b)[:, 0:32], AF.Exp, rd=[PR[b]], wr=[Rsm[EACS]])
        ts("dve", sm[:, NACS, :], sm[:, ACS, :], -1.0, None, ALU.mult, ALU.bypass, rd=[Rsm[ACS]], wr=[Rsm[NACS]])
        b = psA()
        mm(pbank(b)[:, 0:32], lsel128, sm[:, ACS, :], rd=[Rsm[ACS], Rc], wr=[PR[b]])
        tt("dve", sm[:, DD, :], pbank(b)[:, 0:32], sm[:, ACS, :], ALU.subtract, rd=[PR[b], Rsm[ACS]], wr=[Rsm[DD]])
        act(sm[:, CD, :], pbank(b)[:, 0:32], AF.Exp, rd=[PR[b]], wr=[Rsm[CD]])
        act(sm[:, DS, :], sm[:, DD, :], AF.Exp, rd=[Rsm[DD]], wr=[Rsm[DS]])
        yield
        b = psA()
        pv = pbank_bf(b).rearrange("p (a b) -> p a b", a=8)
        for k4 in range(4):
            tr(pv[:, k4, :], xbcT[:, 16 + k4, off:off + Lc], ident_bf, rd=[Rxbc[16 + k4], Rc], wr=[PR[b]],
               sig=(k4 == 3))
        cp("act", Btok[p], pbank_bf(b)[:, 0:512], rd=[PR[b]], wr=[RBt[p]])
        tt("pool", xdtS[p].rearrange("p (a b) -> p a b", a=32), xdt.rearrange("p (a b) -> p a b", a=32),
           bc_last(sm[:, DS, :], 64), ALU.mult, rd=[Rxdt, Rsm[DS]], wr=[RxdtS[p]])
        b = psA()
        pv = pbank(b).rearrange("p (a b) -> p a b", a=4)
        for g in range(4):
            mm(pv[:, g, :], xbcT[:, 16 + g, off:off + Lc], xbcT[:, 20 + g, off:off + Lc],
               rd=[Rxbc[16 + g], Rxbc[20 + g]], wr=[PR[b]], sig=(g == 3))
        tt("dve", CBm, pv, bc_mid(tri_bf, 4), ALU.mult, rd=[PR[b], Rc], wr=[RCBm])
        yield
        yslots = []
        for g in range(4):
            byd = psA()
            for qq in range(2):
                qd = 2 * g + qq
                qi = cnt["q"] % 2
                cnt["q"] += 1
                tt("pool", Dg[qi], bc_mid(ident_f, 4), bc_last(sm[:, ACS, 4 * qd:4 * qd + 4], 128), ALU.mult,
                   rd=[Rsm[ACS], Rc], wr=[RDg[qi]])
                be = psA()
                pe_ = pbank(be).rearrange("p (a b) -> p a b", a=4)
                mm(pbank(be), ident_bf, maskneg.rearrange("p a b -> p (a b)"), start=True, stop=False,
                   rd=[Rc], wr=[PR[be]], sig=False)
                mm(pbank(be), ones_f, Dg[qi].rearrange("p a b -> p (a b)"), start=False, stop=True,
                   rd=[RDg[qi], Rc], wr=[PR[be]])
                for hh in range(4):
                    h = 4 * qd + hh
                    act(decT[qi][:, hh, :], pe_[:, hh, :], AF.Exp, rd=[PR[be], Rsm[NACS]], wr=[Rdec[qi]],
                        bias=sm[:, NACS, h:h + 1])
                tt("dve", MT[qi], decT[qi], bc_mid(CBm[:, g, :], 4), ALU.mult, rd=[Rdec[qi], RCBm], wr=[RMT[qi]])
                for hh in range(4):
                    h = 4 * qd + hh
                    hl = h % 8
                    mm(pbank(byd)[:, hl * 64:(hl + 1) * 64], MT[qi][:, hh, :], xdt[:, h * 64:(h + 1) * 64],
                       start=True, stop=False, rd=[RMT[qi], Rxdt], wr=[PR[byd]], sig=False)
                    mm(pbank(byd)[:, hl * 64:(hl + 1) * 64], ident_bf, xsD[:, h * 64:(h + 1) * 64],
                       start=False, stop=True, rd=[RxsD, Rc], wr=[PR[byd]], sig=(qq == 1 and hh == 3))
                yield
            ys = cnt["yd"] % NYD
            cnt["yd"] += 1
            cp("act", ydg[ys], pbank(byd), rd=[PR[byd]], wr=[Rydg[ys]])
            yslots.append(ys)
            yield
        stA_out[c] = yslots

    stA_out = {}

    def stageB(c, ts_, ci):
        tok0, Lc = chunks[c]
        off = tok0 - ts_
        first = (c == 0)
        p = c % 2
        sm = sm2[p]
        Rsm = Rsm2[p]
        yslots = stA_out[c]
        for g in range(4):
            gi = cnt["g"] % 2
            cnt["g"] += 1
            gs = slice(g * 512, (g + 1) * 512)
            ys = yslots[g]
            if not first:
                byo = psB()
                mm(pbank(byo), xbcT[:, 20 + g, off:off + Lc], hbf[:, gs], rd=[Rxbc[20 + g], Rhbf[g]], wr=[PR[byo]])
                for hl in range(8):
                    h = 8 * g + hl
                    stt(yg[gi][:, hl * 64:(hl + 1) * 64], pbank(byo)[:, hl * 64:(hl + 1) * 64],
                        sm[:, EACS, h:h + 1], ydg[ys][:, hl * 64:(hl + 1) * 64], ALU.mult, ALU.add,
                        rd=[PR[byo], Rsm[EACS], Rydg[ys]], wr=[Ryg[gi]])
                tt("dve", yg[gi], yg[gi], sz[ci][:, gs], ALU.mult, rd=[Ryg[gi], Rsz[ci][g]], wr=[Ryg[gi]])
            else:
                tt("dve", yg[gi], ydg[ys], sz[ci][:, gs], ALU.mult, rd=[Rydg[ys], Rsz[ci][g]], wr=[Ryg[gi]])
            act(sqj, yg[gi], AF.Square, rd=[Ryg[gi]], wr=[Rgst[gi]], accum=gst[gi][:, 0:1])
            act(gst[gi][:, 1:2], gst[gi][:, 0:1], AF.Ln, rd=[Rgst[gi]], wr=[Rgst[gi]], bias=epsc[:, 0:1],
                scale=1.0 / 512)
            act(gst[gi][:, 2:3], gst[gi][:, 1:2], AF.Exp, rd=[Rgst[gi]], wr=[Rgst[gi]], scale=-0.5)
            ts("dve", yn[:, gs], yg[gi], gst[gi][:, 2:3], None, ALU.mult, ALU.bypass, rd=[Ryg[gi], Rgst[gi]],
               wr=[Ryn[g]])
            yield
            bsn = psB()
            mm(pbank(bsn), Btok[p][:, g * 128:(g + 1) * 128], xdtS[p][:, gs], rd=[RBt[p], RxdtS[p]], wr=[PR[bsn]])
            if first:
                cp("dve", hst[:, gs], pbank(bsn), rd=[PR[bsn]], wr=[Rhst[g]])
            else:
                tt("pool", hst[:, gs].rearrange("p (a b) -> p a b", a=8),
                   hst[:, gs].rearrange("p (a b) -> p a b", a=8), bc_last(sm[:, CD, 8 * g:8 * g + 8], 64),
                   ALU.mult, rd=[Rhst[g], Rsm[CD]], wr=[Rhst[g]])
                tt("dve", hst[:, gs], hst[:, gs], pbank(bsn), ALU.add, rd=[Rhst[g], PR[bsn]], wr=[Rhst[g]])
            cp("pool", hbf[:, gs], hst[:, gs], rd=[Rhst[g]], wr=[Rhbf[g]])
            yield
        for hb in range(2):
            b = psB()
            pv = pbank_bf(b).rearrange("p (a b) -> p a b", a=8)
            for k8 in range(8):
                cg = hb * 8 + k8
                tr(pv[:, k8, :], yn[:, cg * 128:(cg + 1) * 128], ident_bf, rd=[Ryn[cg // 4], Rc], wr=[PR[b]],
                   sig=(k8 == 7))
            cp("act" if hb == 0 else "dve", ynT[:, hb * 8:(hb + 1) * 8, :], pv, rd=[PR[b]], wr=[RynT])
            yield
        dma("sp", yssd_d[c], ynT, rd=[RynT], wr=[Ryssd[c]])

    def interleave(*gens):
        P.begin_capture()
        for g in gens:
            if g is not None:
                for _ in g:
                    pass
        P.end_capture(keep=KEEP[0])

    seq = []
    for ti, tl in enumerate(tiles):
        ts_ = chunks[tl[0]][0]
        if ti == 0:
            seq.append(conv_pass(tl, psA))
            seq.append(z_pass(tl, psB))
            seq.append(stageA(tl[0], ts_))
            seq.append("cut")
        for ci, c in enumerate(tl):
            seq.append(stageB(c, ts_, ci))
            if ci + 1 < len(tl):
                seq.append(stageA(c + 1, ts_))
            elif ti + 1 < len(tiles):
                ntl = tiles[ti + 1]
                seq.append(conv_pass(ntl, psA))
                seq.append(z_pass(ntl, psB))
                seq.append(stageA(ntl[0], chunks[ntl[0]][0]))
        seq.append("cut")
    KEEP = [0.8]
    region = []
    ncut = sum(1 for g in seq if g == "cut")
    icut = 0
    for g in seq:
        if g == "cut":
            icut += 1
            if icut == ncut:
                KEEP[0] = 1.0
            interleave(*region)
            region = []
        else:
            region.append(g)
    assert not region and not P.carry
    P.barrier()
    A.reset(m_const)
    if dbg:
        dma("sp", dbg_d["yssd"], yssd_d, rd=Ryssd)
    if stop == "P3":
        return finish()
    new_stg(256)
    wps = A.alloc([16, D], BF16)
    wpa = A.alloc([KC, D], BF16)
    wo = A.alloc([KC, D], BF16)
    Rw4 = Res()
    npost_bc = A.alloc([D], F32)
    dma("sp", npost_bc, npost_d[0].partition_broadcast(128), wr=[Rw4])
    wps_v = wps_d.rearrange("(kc p) n -> p kc n", p=128)
    wpa_v = wpa_d.rearrange("(kc p) n -> p kc n", p=128)
    wo_v = wo_d.rearrange("(kc p) n -> p kc n", p=128)
    st4 = [A.alloc([2, D], F32) for _ in range(2)]
    Rst4 = [Res(), Res()]
    for k0 in range(0, 16, 2):
        i_ = (k0 // 2) % 2
        dma("sp", st4[i_], wps_v[:, k0:k0 + 2, :], wr=[Rst4[i_]])
        for kk in range(2):
            kc = k0 + kk
            if kk == 0:
                ts("dve", wps[:, kc, :], st4[i_][:, kk, :], snrm[:, kc:kc + 1], None, ALU.mult, ALU.bypass,
                   rd=[Rst4[i_], Rp], wr=[Rw4])
            else:
                act(wps[:, kc, :], st4[i_][:, kk, :], AF.Copy, rd=[Rst4[i_], Rp], wr=[Rw4], scale=snrm[:, kc:kc + 1])
    for k0 in range(0, KC, 4):
        dma("pool", wpa[:, k0:k0 + 4, :], wpa_v[:, k0:k0 + 4, :], wr=[Rw4], sem=P.new_sem())
        dma("pool", wo[:, k0:k0 + 4, :], wo_v[:, k0:k0 + 4, :], wr=[Rw4], sem=P.new_sem())
    ysT = [A.alloc([16, 128], BF16) for _ in range(2)]
    yaT = [A.alloc([KC, 128], BF16) for _ in range(2)]
    gt = [A.alloc([2048], BF16) for _ in range(2)]
    xr = [A.alloc([D], F32) for _ in range(2)]
    Rin = [Res(), Res()]
    t1 = [A.alloc([D], F32) for _ in range(2)]
    Rt1 = [Res(), Res()]
    t2 = [A.alloc([D], F32) for _ in range(2)]
    Rt2 = [Res(), Res()]
    mg = [A.alloc([D], BF16) for _ in range(2)]
    Rmg = [Res(), Res()]
    mgT = [A.alloc([KC, 128], BF16) for _ in range(2)]
    RmgT = [Res(), Res()]
    ob_ = [A.alloc([D], F32) for _ in range(2)]
    Rob = [Res(), Res()]
    pst = [A.alloc([4], F32) for _ in range(2)]
    Rpst = [Res(), Res()]

    def load_tile4(m, i):
        t0 = 128 * m
        n = 128
        dma("sp", ysT[i], yssd_d[m], rd=[Ryssd[m]], wr=[Rin[i]])
        dma("sp", yaT[i], yatt_d[m], rd=[Ryatt[m // 4]], wr=[Rin[i]])
        dma("sp", gt[i][0:n, :], gates_d[t0:t0 + n, :], rd=[Rgates[m]], wr=[Rin[i]])
        load_x_tile(m, xr[i], Rin[i])

    out_toks = []
    load_tile4(1, 1)
    for m in range(1, NB):
        if (m - 1) % 4 == 0:
            if P.cap is not None:
                P.end_capture(keep=0.8)
            P.begin_capture()
        i = m % 2
        t0 = 128 * m
        n = 128
        if m + 1 < NB:
            load_tile4(m + 1, 1 - i)
        ck("p4_load")
        for hf in range(2):
            for kc in range(16):
                mm(pbank(hf)[0:n, :], ysT[i][:, kc, 0:n], wps[:, kc, hf * 512:(hf + 1) * 512], start=(kc == 0),
                   stop=(kc == 15), rd=[Rin[i], Rw4], wr=[PR[hf]], sig=(kc == 15))
        for hf in range(2):
            for kc in range(KC):
                mm(pbank(2 + hf)[0:n, :], yaT[i][:, kc, 0:n], wpa[:, kc, hf * 512:(hf + 1) * 512], start=(kc == 0),
                   stop=(kc == KC - 1), rd=[Rin[i], Rw4], wr=[PR[2 + hf]], sig=(kc == KC - 1))
        ck("p4_ab")
        for hf in range(2):
            hs = slice(hf * 512, (hf + 1) * 512)
            tt("dve", t1[i][:, hs], pbank(hf), gt[i][:, hs], ALU.mult, rd=[PR[hf], Rin[i]], wr=[Rt1[i]])
            tt("dve", t2[i][:, hs], pbank(2 + hf), gt[i][:, D + hf * 512:D + (hf + 1) * 512], ALU.mult,
               rd=[PR[2 + hf], Rin[i]], wr=[Rt2[i]])
        tt("dve", mg[i][0:n, :], t1[i][0:n, :], t2[i][0:n, :], ALU.add, rd=[Rt1[i], Rt2[i]], wr=[Rmg[i]])
        ck("p4_merge")
        pv = pbank_bf(4 + i).rearrange("p (a b) -> p a b", a=KC)
        for kc in range(KC):
            tr(pv[:, kc, 0:n], mg[i][0:n, kc * 128:(kc + 1) * 128], ident_bf[0:n, 0:n], rd=[Rmg[i], Rc],
               wr=[PR[4 + i]], sig=(kc == KC - 1))
        cp("act", mgT[i][:, :, 0:n], pv[:, :, 0:n], rd=[PR[4 + i]], wr=[RmgT[i]])
        for hf in range(2):
            for kc in range(KC):
                mm(pbank(6 + hf)[0:n, :], mgT[i][:, kc, 0:n], wo[:, kc, hf * 512:(hf + 1) * 512], start=(kc == 0),
                   stop=(kc == KC - 1), rd=[RmgT[i], Rw4], wr=[PR[6 + hf]], sig=(kc == KC - 1))
        ck("p4_o")
        for hf in range(2):
            act(t1[i][:, hf * 512:(hf + 1) * 512], pbank(6 + hf), AF.Square, rd=[PR[6 + hf], Rt1[i]],
                wr=[Rt1[i], Rpst[i]], accum=pst[i][:, hf:hf + 1])
        tt("dve", pst[i][:, 0:1], pst[i][:, 0:1], pst[i][:, 1:2], ALU.add, rd=[Rpst[i]], wr=[Rpst[i]])
        act(pst[i][:, 2:3], pst[i][:, 0:1], AF.Sqrt, rd=[Rpst[i]], wr=[Rpst[i]], bias=EPS, scale=1.0 / D)
        recip(pst[i][:, 3:4], pst[i][:, 2:3], rd=[Rpst[i]], wr=[Rpst[i]])
        for hf in range(2):
            hs = slice(hf * 512, (hf + 1) * 512)
            stt(ob_[i][:, hs], pbank(6 + hf), pst[i][:, 3:4], npost_bc[:, hs], ALU.mult, ALU.mult,
                rd=[PR[6 + hf], Rpst[i], Rw4], wr=[Rob[i]])
        tt("pool", ob_[i][0:n, :], ob_[i][0:n, :], xr[i][0:n, :], ALU.add, rd=[Rob[i], Rin[i]], wr=[Rob[i]])
        ck("p4_norm")
        out_toks.append(dma("sp", out_d[128 * (m - 1):128 * m, :], ob_[i][:, :], rd=[Rob[i]]))
        ck("p4_t%d" % m)
    if P.cap is not None:
        P.end_capture()
    P.barrier()
    P.emit(sems)
    es.close()
    return nc, A.peak


def _prep_inputs(inputs, b):
    f = lambda a: np.ascontiguousarray(np.asarray(a, dtype=np.float32))
    conv_wb = np.concatenate([np.asarray(inputs["conv_w"][0]), np.asarray(inputs["conv_b"])], axis=0)
    return {
        "x": f(inputs["x"][b]),
        "meta": f(inputs["meta_tokens"]),
        "norm_pre": f(np.asarray(inputs["norm_pre"]).reshape(KC, 128)),
        "w_in": f(inputs["w_in"][0]),
        "conv_wb": f(conv_wb),
        "dt_bias": f(inputs["dt_bias"]),
        "a_log": f(inputs["a_log"]),
        "d_skip": f(inputs["d_skip"]),
        "ssd_norm": f(np.asarray(inputs["ssd_norm"]).reshape(16, 128)),
        "fgate_bias": f(np.asarray(inputs["fgate_bias"]).reshape(16, 1)),
        "gate_bias": f(inputs["gate_bias"]),
        "w_proj_ssd": f(inputs["w_proj_ssd"][0]),
        "w_proj_att": f(inputs["w_proj_att"][0]),
        "w_out": f(inputs["w_out"][0]),
        "norm_post": f(inputs["norm_post"]),
    }


def kernel(**inputs):
    x = np.asarray(inputs["x"])
    B, SEQ, _ = x.shape
    nc = bass.Bass("TRN2", target_bir_lowering=False)
    build(nc, SEQ)
    in_maps = [_prep_inputs(inputs, b) for b in range(B)]
    res = run_bass_kernel_spmd(nc, in_maps, core_ids=list(range(B)))
    return np.stack([np.asarray(r["out"], dtype=np.float32) for r in res.results], axis=0)
```

```python
import numpy as np
from contextlib import ExitStack
import concourse.bass as bass
import concourse.mybir as mybir
from concourse.bass_utils import run_bass_kernel_spmd

F32 = mybir.dt.float32
BF16 = mybir.dt.bfloat16
AF = mybir.ActivationFunctionType
ALU = mybir.AluOpType

ENGS = ("pe", "act", "dve", "pool", "sp")

D = 1024
KC = 8
NCOLS = 11312
C_Z, C_XS, C_B, C_C, C_DT, C_ZA, C_Q, C_K, C_V, C_F, C_G = (0, 2048, 4096, 4608, 5120, 5152, 6176, 7200,
                                                          8224, 9248, 9264)
EPS = 1e-6
NMETA = 16
ACT_TBL = {AF.Exp: "explog", AF.Ln: "explog", AF.Silu: "silu", AF.Sqrt: "sqrt", AF.Sigmoid: "sigmoid"}
ACT_TBL_COST = 1300.0
P3_KEEP = 0.8
P4_KEEP = 0.8


class Res:
    __slots__ = ("w", "r", "excl")

    def __init__(self, excl=False):
        self.w = None
        self.r = []
        self.excl = excl


class Prog:
    def __init__(self, nc, n_dma_sems=32):
        self.nc = nc
        self.q = {e: [] for e in ENGS}
        self.cnt = {e: 0 for e in ENGS}
        self.seen = {e: {} for e in ENGS}
        self.n_dma = n_dma_sems
        self.dma_cnt = [0] * n_dma_sems
        self.dma_rr = 0
        self.pending = {e: False for e in ENGS}

    def _deps(self, eng, reads, writes):
        toks = []
        for r in reads:
            if r.w is not None:
                toks.append(r.w)
            if r.excl:
                toks.extend(t for t in r.r if t[2] != eng)
        for w in writes:
            if w.w is not None:
                toks.append(w.w)
            toks.extend(w.r)
        need = {}
        for (k, v, e) in toks:
            if e == eng and eng == "pe":
                continue
            if need.get(k, 0) < v:
                need[k] = v
        out = []
        seen = self.seen[eng]
        for k, v in need.items():
            if seen.get(k, 0) >= v:
                continue
            seen[k] = v
            out.append((k, v))
        return out

    def _commit(self, tok, reads, writes):
        for r in reads:
            r.r.append(tok)
            if len(r.r) > 64:
                best = {}
                for t in r.r:
                    if best.get(t[0], (None, 0, None))[1] < t[1]:
                        best[t[0]] = t
                r.r = list(best.values())
        for w in writes:
            w.w = tok
            w.r = []

    def begin_capture(self):
        self.cap = []

    carry = ()

    def end_capture(self, hop=600.0, keep=1.0):
        ops = list(self.carry) + self.cap
        self.carry = ()
        self.cap = None
        n = len(ops)
        lastw = {}
        readers = {}
        preds = [set() for _ in range(n)]
        for i, (kind, eng, fn, reads, writes, sig, cost, lat, tag) in enumerate(ops):
            for r in reads:
                k = id(r)
                if k in lastw:
                    preds[i].add(lastw[k])
            for w in writes:
                k = id(w)
                if k in lastw:
                    preds[i].add(lastw[k])
                for j in readers.get(k, ()):
                    preds[i].add(j)
            for r in reads:
                readers.setdefault(id(r), []).append(i)
            for w in writes:
                lastw[id(w)] = i
                readers[id(w)] = []
            preds[i].discard(i)
        succs = [[] for _ in range(n)]
        npred = [0] * n
        for i in range(n):
            npred[i] = len(preds[i])
            for j in preds[i]:
                succs[j].append(i)
        efree = {e: 0.0 for e in ENGS}
        fin = [0.0] * n
        rdy = [0.0] * n
        ready = [i for i in range(n) if npred[i] == 0]
        order = []
        import heapq
        TBL = ACT_TBL_COST
        while ready:
            best = None
            bt = None
            for i in ready:
                eng = ops[i][1]
                t = max(efree[eng], rdy[i])
                tg = ops[i][8]
                if tg is not None and tg != self.act_tbl:
                    t += TBL
                if bt is None or t < bt - 1e-9 or (abs(t - bt) <= 1e-9 and i < best):
                    bt = t
                    best = i
            i = best
            ready.remove(i)
            kind, eng, fn, reads, writes, sig, cost, lat, tag = ops[i]
            if tag is not None:
                self.act_tbl = tag
            efree[eng] = bt + cost
            fin[i] = bt + lat
            order.append(i)
            for j in succs[i]:
                npred[j] -= 1
                dly = fin[i] + (hop if ops[j][1] != eng or kind == "dma" else 0.0)
                if dly > rdy[j]:
                    rdy[j] = dly
                if npred[j] == 0:
                    ready.append(j)
        assert len(order) == n
        if keep < 1.0:
            n_emit = int(keep * n)
            left = sorted(order[n_emit:])
            self.carry = [ops[i] for i in left]
            order = order[:n_emit]
        for i in order:
            kind, eng, fn, reads, writes, sig, cost, lat, tag = ops[i]
            if kind == "dma":
                self.dma(eng, fn, reads, writes, sem=sig)
            else:
                self.op(eng, fn, reads, writes, True)

    cap = None
    act_tbl = None

    def op(self, eng, fn, reads=(), writes=(), sig=True, cost=300.0, tag=None):
        if self.cap is not None:
            self.cap.append(("op", eng, fn, tuple(reads), tuple(writes), sig, cost, cost, tag))
            return None
        waits = self._deps(eng, reads, writes)
        if sig:
            self.cnt[eng] += 1
            tok = (('e', eng), self.cnt[eng], eng)
            self.q[eng].append((waits, fn, ('e', eng), 1))
            self.pending[eng] = False
        else:
            tok = (('e', eng), self.cnt[eng] + 1, eng)
            self.q[eng].append((waits, fn, None, 0))
            self.pending[eng] = True
        self._commit(tok, reads, writes)
        return tok

    def new_sem(self):
        self.dma_cnt.append(0)
        return len(self.dma_cnt) - 1

    def dma(self, eng, fn, reads=(), writes=(), sem=None, cost=3000.0):
        if self.cap is not None:
            self.cap.append(("dma", eng, fn, tuple(reads), tuple(writes), sem, 100.0, cost, None))
            return None
        waits = self._deps(eng, reads, writes)
        if sem is None:
            i = self.dma_rr
            self.dma_rr = (self.dma_rr + 1) % self.n_dma
        else:
            i = sem
        self.dma_cnt[i] += 16
        tok = (('d', i), self.dma_cnt[i], 'dma')
        self.q[eng].append((waits, fn, ('d', i), 16))
        self._commit(tok, reads, writes)
        return tok

    def all_tokens(self):
        toks = []
        for e in ENGS:
            if self.cnt[e] > 0:
                toks.append((('e', e), self.cnt[e], e))
        for i in range(len(self.dma_cnt)):
            if self.dma_cnt[i] > 0:
                toks.append((('d', i), self.dma_cnt[i], 'dma'))
        return toks

    def wait_all(self, eng, toks):
        waits = []
        for (k, v, e) in toks:
            if self.seen[eng].get(k, 0) >= v:
                continue
            self.seen[eng][k] = v
            waits.append((k, v))
        if waits:
            self.q[eng].append((waits, None, None, 0))

    def barrier(self):
        for e in ENGS:
            assert not self.pending[e], "barrier with unsignalled op on " + e
        toks = self.all_tokens()
        for e in ENGS:
            self.wait_all(e, toks)

    def emit(self, sems):
        nc = self.nc
        q = self.q

        def run(e, name):
            for (waits, fn, sk, n) in q[name]:
                for (k, v) in waits:
                    e.wait_ge(sems[k], v)
                if fn is None:
                    continue
                inst = fn(e)
                if sk is not None:
                    inst.then_inc(sems[sk], n)

        with nc.Block() as block:
            @block.tensor
            def _(e):
                run(e, "pe")

            @block.scalar
            def _(e):
                run(e, "act")

            @block.vector
            def _(e):
                run(e, "dve")

            @block.gpsimd
            def _(e):
                run(e, "pool")

            @block.sync
            def _(e):
                run(e, "sp")


class Arena:
    def __init__(self, t, nbytes):
        self.t = t
        self.cap = nbytes
        self.off = 0
        self.peak = 0
        self.top = nbytes

    def alloc(self, shape, dt, parts=128):
        n = 1
        for s in shape:
            n *= s
        esz = 2 if dt == BF16 else 4
        nb = (n * esz + 63) // 64 * 64
        off = self.off
        self.off += nb
        self.peak = max(self.peak, self.off)
        assert self.off <= self.cap, ("arena overflow", self.off, self.cap)
        ap = self.t[0:parts, off // 2: off // 2 + (n * esz) // 2]
        if dt == F32:
            ap = ap.bitcast(F32)
        if len(shape) == 2:
            ap = ap.rearrange("p (a b) -> p a b", a=shape[0])
        elif len(shape) == 3:
            ap = ap.rearrange("p (a b c) -> p a b c", a=shape[0], b=shape[1])
        return ap

    def alloc_top(self, shape, dt, parts=128):
        n = 1
        for s_ in shape:
            n *= s_
        esz = 2 if dt == BF16 else 4
        nb = (n * esz + 63) // 64 * 64
        self.top -= nb
        save = self.off
        self.off = self.top
        ap = self.alloc(shape, dt, parts)
        self.off = save
        self.peak = max(self.peak, save)
        return ap

    def mark(self):
        return self.off

    def reset(self, m):
        self.off = m


def bc_mid(ap2, n):
    p, f = ap2.shape
    return ap2.unsqueeze(1).to_broadcast([p, n, f])


def bc_last(ap2, n):
    p, f = ap2.shape
    return ap2.unsqueeze(2).to_broadcast([p, f, n])


class StopBuild(Exception):
    pass


_FIN = [None]


def build(nc, SEQ, dbg=False, stop=None):
    try:
        return _build(nc, SEQ, dbg, stop)
    except StopBuild:
        return _FIN[0]()


def _build(nc, SEQ, dbg=False, stop=None):
    assert SEQ % 128 == 0
    T = SEQ + 128
    NB = T // 128
    TP = T
    NCH = NB
    NQT = (TP + 511) // 512
    NFILL = 128 - NMETA

    dram = {}

    def din(name, shape, dt=F32):
        dram[name] = nc.dram_tensor(name, shape, dt, kind="ExternalInput").ap()
        return dram[name]

    x_d = din("x", [SEQ, D])
    meta_d = din("meta", [NMETA, D])
    npre_d = din("norm_pre", [KC, 128])
    win_d = din("w_in", [D, NCOLS])
    convwb_d = din("conv_wb", [5, 3072])
    dtb_d = din("dt_bias", [1, 32])
    alog_d = din("a_log", [1, 32])
    dsk_d = din("d_skip", [1, 32])
    snrm_d = din("ssd_norm", [16, 128])
    fgb_d = din("fgate_bias", [16, 1])
    gb_d = din("gate_bias", [1, 2048])
    wps_d = din("w_proj_ssd", [2048, D])
    wpa_d = din("w_proj_att", [D, D])
    wo_d = din("w_out", [D, D])
    npost_d = din("norm_post", [1, D])
    out_d = nc.dram_tensor("out", [SEQ, D], F32, kind="ExternalOutput").ap()

    c3_d = nc.dram_tensor("c3_s", [16, 3, TP], BF16, kind="Internal").ap()
    gates_d = nc.dram_tensor("gates_s", [TP, 2048], BF16, kind="Internal").ap()
    yatt_d = nc.dram_tensor("yatt_s", [NB, 128, KC, 128], BF16, kind="Internal").ap()
    yssd_d = nc.dram_tensor("yssd_s", [NB, 128, 16, 128], BF16, kind="Internal").ap()
    wssd_d = nc.dram_tensor("wssd_s", [10, 128, KC * 512], BF16, kind="Internal").ap()
    dbg_d = {}
    if dbg:
        dbg_d["uT"] = nc.dram_tensor("dbg_uT", [128, KC, TP], BF16, kind="ExternalOutput").ap()
        dbg_d["yatt"] = nc.dram_tensor("dbg_yatt", [NB, 128, KC, 128], BF16, kind="ExternalOutput").ap()
        dbg_d["yssd"] = nc.dram_tensor("dbg_yssd", [NB, 128, 16, 128], BF16, kind="ExternalOutput").ap()
        dbg_d["gates"] = nc.dram_tensor("dbg_gates", [TP, 2048], BF16, kind="ExternalOutput").ap()
        dbg_d["c3"] = nc.dram_tensor("dbg_c3", [16, 3, TP], BF16, kind="ExternalOutput").ap()

    win_v = win_d.rearrange("(kc p) n -> p kc n", p=128)

    es = ExitStack()
    ARENA_BYTES = 206 * 1024
    arena_t = es.enter_context(nc.sbuf_tensor("arena", [128, ARENA_BYTES // 2], BF16))
    A = Arena(arena_t, ARENA_BYTES)
    PTs = [es.enter_context(nc.psum_tensor("ps%d" % i, [128, 1024], F32)) for i in range(4)]
    PR = [Res(excl=True) for _ in range(8)]

    def pbank(b):
        return PTs[b // 2][:, (b % 2) * 512:(b % 2) * 512 + 512]

    def pbank_bf(b):
        return pbank(b).bitcast(BF16)

    P = Prog(nc)
    sems = {}
    for e in ENGS:
        sems[('e', e)] = es.enter_context(nc.semaphore("s_" + e))
    for i in range(P.n_dma + 16):
        sems[('d', i)] = es.enter_context(nc.semaphore("d_%d" % i))

    def finish():
        if P.cap is not None:
            P.end_capture()
        P.barrier()
        P.emit(sems)
        es.close()
        return nc, A.peak

    _FIN[0] = finish

    def ck(name):
        if stop == name:
            raise StopBuild()

    def fsz(ap):
        n = 1
        for d in ap.shape[1:]:
            n *= d
        return n

    def mm(out, lhsT, rhs, start=True, stop=True, rd=(), wr=(), sig=True):
        c = 64.0 + max(fsz(rhs), 64) / 2.0
        if rhs.dtype == F32:
            c *= 4
        return P.op("pe", lambda e, o=out, l=lhsT, r=rhs, s=start, t=stop:
                    e.matmul(o, lhsT=l, rhs=r, start=s, stop=t), rd, wr, sig, cost=c)

    def tr(out, in_, ident, rd=(), wr=(), sig=True):
        return P.op("pe", lambda e, o=out, i=in_, d=ident: e.transpose(o, i, d), rd, wr, sig, cost=130.0)

    def act(out, in_, func, rd=(), wr=(), bias=None, scale=None, accum=None):
        kw = {}
        if bias is not None:
            kw["bias"] = bias
        if scale is not None:
            kw["scale"] = scale
        if accum is not None:
            kw["accum_out"] = accum
        c = 220.0 + max(fsz(in_), 64) / 1.2 + (100.0 if accum is not None else 0.0)
        tag = ACT_TBL.get(func)
        return P.op("act", lambda e, o=out, i=in_, f=func, kw=kw: e.activation(out=o, in_=i, func=f, **kw), rd, wr,
                    cost=c, tag=tag)

    def ecost(eng, ap):
        n = max(fsz(ap), 64)
        return (120.0 + n / 0.96) if eng == "dve" else (300.0 + n / 0.6)

    def tt(eng, out, in0, in1, op, rd=(), wr=()):
        return P.op(eng, lambda e, o=out, a=in0, b=in1, p=op: e.tensor_tensor(out=o, in0=a, in1=b, op=p), rd, wr,
                    cost=ecost(eng, out))

    def ts(eng, out, in0, s1, s2, op0, op1, rd=(), wr=()):
        return P.op(eng, lambda e, o=out, a=in0, x=s1, y=s2, p0=op0, p1=op1:
                    e.tensor_scalar(out=o, in0=a, scalar1=x, scalar2=y, op0=p0, op1=p1), rd, wr,
                    cost=ecost(eng, out))

    def stt(out, in0, scalar, in1, op0, op1, rd=(), wr=()):
        return P.op("dve", lambda e, o=out, a=in0, s=scalar, b=in1, p0=op0, p1=op1:
                    e.scalar_tensor_tensor(out=o, in0=a, scalar=s, in1=b, op0=p0, op1=p1), rd, wr,
                    cost=ecost("dve", out))

    def cp(eng, out, in_, rd=(), wr=()):
        if eng == "act":
            return act(out, in_, AF.Copy, rd, wr)
        return P.op(eng, lambda e, o=out, i=in_: e.tensor_copy(out=o, in_=i), rd, wr, cost=ecost(eng, out))

    def memset(eng, ap, val, rd=(), wr=()):
        return P.op(eng, lambda e, a=ap, v=val: e.memset(a, v), rd, wr, cost=ecost(eng, ap))

    def dma(eng, out, in_, rd=(), wr=(), sem=None):
        nb = fsz(out) * out.shape[0] * (2 if out.dtype == BF16 else 4)
        return P.dma(eng, lambda e, o=out, i=in_: e.dma_start(out=o, in_=i), rd, wr, sem=sem,
                     cost=2500.0 + nb / 150.0)

    def recip(out, in_, rd=(), wr=()):
        return P.op("dve", lambda e, o=out, i=in_: e.reciprocal(out=o, in_=i), rd, wr,
                    cost=120.0 + 8.0 * max(fsz(out), 8))

    def asel(out, in_, pattern, cmp, fill, base, cm, rd=(), wr=()):
        return P.op("pool", lambda e, o=out, i=in_, p=pattern, c=cmp, f=fill, b=base, m=cm:
                    e.affine_select(out=o, in_=i, pattern=p, compare_op=c, fill=f, base=b, channel_multiplier=m),
                    rd, wr)

    Rc = Res()
    ones_f = A.alloc([128], F32)
    zeros_bf = A.alloc([4, 128], BF16)
    ones_bf = A.alloc([128], BF16)
    ident_bf = A.alloc([128], BF16)
    ident_f = A.alloc([128], F32)
    tri_bf = A.alloc([128], BF16)
    tri_f = A.alloc([128], F32)
    maskneg = A.alloc([4, 128], BF16)
    lsel128 = A.alloc([128], F32)
    negones_f = A.alloc([128], F32)
    memset("pool", ones_f, 1.0, wr=[Rc])
    memset("pool", ones_bf, 1.0, wr=[Rc])
    memset("pool", zeros_bf, 0.0, wr=[Rc])
    asel(ident_bf, ones_bf, [[-1, 128]], ALU.is_equal, 0.0, 0, 1, rd=[Rc], wr=[Rc])
    asel(ident_f, ones_f, [[-1, 128]], ALU.is_equal, 0.0, 0, 1, rd=[Rc], wr=[Rc])
    asel(tri_bf, ones_bf, [[1, 128]], ALU.is_ge, 0.0, 0, -1, rd=[Rc], wr=[Rc])
    asel(tri_f, ones_f, [[1, 128]], ALU.is_ge, 0.0, 0, -1, rd=[Rc], wr=[Rc])
    asel(maskneg, zeros_bf, [[0, 4], [1, 128]], ALU.is_ge, -30000.0, 0, -1, rd=[Rc], wr=[Rc])
    asel(lsel128, ones_f, [[0, 128]], ALU.is_equal, 0.0, -127, 1, rd=[Rc], wr=[Rc])
    memset("pool", negones_f, -1.0, wr=[Rc])
    epsc = A.alloc([1], F32)
    memset("pool", epsc, EPS, wr=[Rc])

    gpre = A.alloc([KC], F32)
    snrm = A.alloc([16], F32)
    cw = A.alloc([24, 5], F32)
    fbneg = A.alloc([1], F32, parts=16)
    dtb_bc = A.alloc([32], F32)
    a_bc = A.alloc([32], F32)
    dsk_bc = A.alloc([32], F32)
    Rp = Res()
    m0 = A.mark()
    t_np = A.alloc([128], F32, parts=KC)
    t_sn = A.alloc([128], F32, parts=16)
    t_cw = A.alloc([3072], F32, parts=5)
    dma("sp", t_np, npre_d, wr=[Rp], sem=P.new_sem())
    dma("sp", t_sn, snrm_d, wr=[Rp], sem=P.new_sem())
    dma("sp", t_cw, convwb_d, wr=[Rp], sem=P.new_sem())
    dma("sp", fbneg, fgb_d, wr=[Rp], sem=P.new_sem())
    dma("sp", dtb_bc, dtb_d[0].partition_broadcast(128), wr=[Rp])
    dma("sp", a_bc, alog_d[0].partition_broadcast(128), wr=[Rp])
    dma("sp", dsk_bc, dsk_d[0].partition_broadcast(128), wr=[Rp])
    tr(pbank(0)[:, 0:KC], t_np, ident_f[0:KC, 0:KC], rd=[Rp, Rc], wr=[PR[0]])
    cp("dve", gpre, pbank(0)[:, 0:KC], rd=[PR[0]], wr=[Rp])
    tr(pbank(1)[:, 0:16], t_sn, ident_f[0:16, 0:16], rd=[Rp, Rc], wr=[PR[1]])
    cp("dve", snrm, pbank(1)[:, 0:16], rd=[PR[1]], wr=[Rp])
    for cg in range(24):
        tr(pbank(2)[:, cg * 5:cg * 5 + 5], t_cw[:, cg * 128:(cg + 1) * 128], ident_f[0:5, 0:5],
           rd=[Rp, Rc], wr=[PR[2]], sig=(cg == 23))
    cp("dve", cw, pbank(2)[:, 0:120].rearrange("p (a b) -> p a b", a=24), rd=[PR[2]], wr=[Rp])
    ts("dve", fbneg, fbneg, -1.0, None, ALU.mult, ALU.bypass, rd=[Rp], wr=[Rp])
    act(a_bc, a_bc, AF.Exp, rd=[Rp], wr=[Rp])
    ts("dve", a_bc, a_bc, -1.0, None, ALU.mult, ALU.bypass, rd=[Rp], wr=[Rp])
    P.barrier()
    A.reset(m0)

    if stop == "C":
        return finish()
    m_const = A.mark()
    uT = A.alloc([KC, TP], BF16)
    Ru = [Res() for _ in range(NB)]
    negc = A.alloc([NB, 16], F32)
    Rnegc = Res()

    def ru(t0, t1):
        return [Ru[b] for b in range(t0 // 128, (t1 - 1) // 128 + 1)]

    m_u = A.mark()

    stg = []
    Rstg = []
    stg_i = [0]

    def new_stg(width):
        stg.clear()
        Rstg.clear()
        for _ in range(2):
            stg.append(A.alloc([KC, width], F32))
            Rstg.append(Res())

    def load_w(src_v, c0, ncols, dst, dstR, scl):
        nk = dst.shape[1]
        SW = stg[0].shape[2]
        for c in range(0, ncols, SW):
            w = min(SW, ncols - c)
            for k0 in range(0, nk, KC):
                k1 = min(nk, k0 + KC)
                i = stg_i[0] % 2
                stg_i[0] += 1
                dma("sp", stg[i][:, 0:k1 - k0, 0:w], src_v[:, k0:k1, c0 + c:c0 + c + w], wr=[Rstg[i]])
                for k in range(k0, k1):
                    if scl is not None:
                        ts("pool", dst[:, k, c:c + w], stg[i][:, k - k0, 0:w], scl[:, k:k + 1], 1.0,
                           ALU.mult, ALU.mult, rd=[Rstg[i], Rp], wr=[dstR])
                    else:
                        cp("pool", dst[:, k, c:c + w], stg[i][:, k - k0, 0:w], rd=[Rstg[i]], wr=[dstR])

    P.begin_capture()
    Rwssd = [Res() for _ in range(10)]

    xt = [A.alloc([D], F32) for _ in range(2)]
    Rxt = [Res(), Res()]
    xn = [A.alloc([D], BF16) for _ in range(2)]
    Rxn = [Res(), Res()]
    junk = A.alloc([D], BF16)
    Rjunk = Res()
    ssq = [A.alloc([4], F32) for _ in range(2)]
    Rss = [Res(), Res()]

    def load_x_tile(m, buf, R):
        if m == 0:
            memset("pool", buf[0:NFILL, :], 0.0, wr=[R])
            dma("sp", buf[NFILL:128, :], meta_d, wr=[R])
        else:
            dma("sp", buf[:, :], x_d[128 * (m - 1):128 * m, :], wr=[R])
        return 128

    load_x_tile(0, xt[0], Rxt[0])
    for m in range(NB):
        i = m % 2
        t0 = 128 * m
        n = 128
        if m + 1 < NB:
            load_x_tile(m + 1, xt[1 - i], Rxt[1 - i])
        act(junk[0:n, :], xt[i][0:n, :], AF.Square, rd=[Rxt[i]], wr=[Rss[i], Rjunk], accum=ssq[i][0:n, 0:1])
        act(ssq[i][0:n, 1:2], ssq[i][0:n, 0:1], AF.Sqrt, rd=[Rss[i]], wr=[Rss[i]], bias=EPS, scale=1.0 / D)
        recip(ssq[i][0:n, 2:3], ssq[i][0:n, 1:2], rd=[Rss[i]], wr=[Rss[i]])
        ts("dve", xn[i][0:n, :], xt[i][0:n, :], ssq[i][0:n, 2:3], None, ALU.mult, ALU.bypass,
           rd=[Rxt[i], Rss[i]], wr=[Rxn[i]])
        pb = pbank_bf(i).rearrange("p (a b) -> p a b", a=KC)
        for kc in range(KC):
            tr(pb[:, kc, 0:n], xn[i][0:n, kc * 128:(kc + 1) * 128], ident_bf[0:n, 0:n],
               rd=[Rxn[i], Rc], wr=[PR[i]], sig=(kc == KC - 1))
        cp("act" if m % 2 == 0 else "dve", uT[:, :, t0:t0 + n], pb[:, :, 0:n], rd=[PR[i]], wr=[Ru[m]])
    new_stg(256)
    wg = A.alloc([KC, 2048], BF16)
    Rwg = Res()
    gb_bc = A.alloc([2048], F32)
    Rgb = Res()
    dma("sp", gb_bc, gb_d[0].partition_broadcast(128), wr=[Rgb])
    load_w(win_v, C_G, 2048, wg, Rwg, gpre)
    gtmp = [A.alloc([512], F32) for _ in range(2)]
    Rgt = [Res(), Res()]
    gsb = [A.alloc([2048], BF16) for _ in range(2)]
    Rgs = [Res(), Res()]
    Rgates = [Res() for _ in range(NB)]
    k = 0
    for m in range(1, NB):
        t0 = 128 * m
        n = 128
        i = m % 2
        for qd in range(4):
            b = 4 + k % 4
            j = k % 2
            k += 1
            for kc in range(KC):
                mm(pbank(b)[0:n, :], uT[:, kc, t0:t0 + n], wg[:, kc, qd * 512:(qd + 1) * 512],
                   start=(kc == 0), stop=(kc == KC - 1), rd=[Rwg, Ru[m]], wr=[PR[b]], sig=(kc == KC - 1))
            tt("dve", gtmp[j][0:n, :], pbank(b)[0:n, :], gb_bc[0:n, qd * 512:(qd + 1) * 512], ALU.add,
               rd=[PR[b], Rgb], wr=[Rgt[j]])
            act(gsb[i][0:n, qd * 512:(qd + 1) * 512], gtmp[j][0:n, :], AF.Sigmoid, rd=[Rgt[j]], wr=[Rgs[i]])
        dma("sp", gates_d[t0:t0 + n, :], gsb[i][0:n, :], rd=[Rgs[i]], wr=[Rgates[m]])

    P.end_capture()
    P.barrier()
    A.reset(m_u)
    if dbg:
        dma("sp", dbg_d["uT"], uT, rd=Ru)
        dma("sp", dbg_d["gates"][128:T, :], gates_d[128:T, :], rd=Rgates)

    if stop == "P0":
        return finish()
    stg128 = [A.alloc_top([KC, 128], F32) for _ in range(2)]
    Rstg128 = [Res(), Res()]
    wv = A.alloc_top([KC, 512], BF16)
    Rwv = Res()
    wp2 = [A.alloc_top([KC, 384], BF16) for _ in range(2)]
    Rwp2 = [Res(), Res()]
    wtmp = [A.alloc_top([KC, 128], BF16) for _ in range(2)]
    Rwtmp = [Res(), Res()]

    def use_stg128():
        stg.clear()
        Rstg.clear()
        stg.extend(stg128)
        Rstg.extend(Rstg128)

    def load_pair_w(gp):
        wpb = wp2[gp % 2]
        load_w(win_v, C_Q + gp * 128, 128, wpb[:, :, 0:128], Rwp2[gp % 2], gpre)
        load_w(win_v, C_K + gp * 128, 128, wpb[:, :, 128:256], Rwp2[gp % 2], gpre)
        load_w(win_v, C_ZA + gp * 128, 128, wpb[:, :, 256:384], Rwp2[gp % 2], gpre)

    use_stg128()
    load_w(win_v, C_V, 512, wv, Rwv, gpre)
    load_pair_w(0)

    new_stg(256)
    ones16 = A.alloc([512], F32, parts=16)
    memset("pool", ones16, 1.0, wr=[Rc])
    wf = A.alloc([KC, 16], BF16)
    Rwf = Res()
    load_w(win_v, C_F, 16, wf, Rwf, gpre)
    lT = A.alloc([TP], F32, parts=16)
    cT = A.alloc([TP], F32, parts=16)
    c3 = A.alloc([3, TP], BF16, parts=16)
    ebuf = [A.alloc([512], F32, parts=16) for _ in range(2)]
    Reb = [Res(), Res()]
    RlT = Res()
    RcT = Res()
    Rc3 = Res()
    for I in range(NQT):
        t0 = 512 * I
        w = min(512, TP - t0)
        b = I % 2
        for kc in range(KC):
            mm(pbank(b)[0:16, 0:w], wf[:, kc, :], uT[:, kc, t0:t0 + w], start=(kc == 0), stop=(kc == KC - 1),
               rd=[Rwf] + ru(t0, t0 + w), wr=[PR[b]], sig=(kc == KC - 1))
        act(ebuf[b][:, 0:w], pbank(b)[0:16, 0:w], AF.Exp, rd=[PR[b], Rp], wr=[Reb[b]], bias=fbneg[:, 0:1], scale=-1.0)
        act(lT[:, t0:t0 + w], ebuf[b][:, 0:w], AF.Ln, rd=[Reb[b]], wr=[RlT], bias=1.0)
    memset("dve", lT[:, 0:NFILL], 0.0, rd=[RlT], wr=[RlT])
    for I in range(NQT):
        t0 = 512 * I
        w = min(512, TP - t0)
        init = 0.0 if I == 0 else cT[:, t0 - 1:t0]
        P.op("dve", lambda e, o=cT[:, t0:t0 + w], d0=ones16[:, 0:w], d1=lT[:, t0:t0 + w], ini=init:
             e.tensor_tensor_scan(out=o, data0=d0, data1=d1, initial=ini, op0=ALU.mult, op1=ALU.subtract),
             [RlT, RcT, Rc], [RcT])
    cp("dve", c3[:, 0, :], cT, rd=[RcT], wr=[Rc3])
    tt("dve", lT, cT, c3[:, 0, :], ALU.subtract, rd=[RcT, Rc3, RlT], wr=[RlT])
    cp("dve", c3[:, 1, :], lT, rd=[RlT], wr=[Rc3])
    nb_a = min(NB, 32)
    for m in range(NB):
        bank = 2 if m < 32 else 3
        col = (m % 32) * 16
        tr(pbank(bank)[:, col:col + 16], cT[:, 128 * m:128 * m + 128], ident_f[0:16, 0:16],
           rd=[RcT, Rc], wr=[PR[bank]], sig=(m == nb_a - 1 or m == NB - 1))
    ts("dve", negc[:, 0:nb_a, :], pbank(2)[:, 0:nb_a * 16].rearrange("p (a b) -> p a b", b=16), -1.0, None,
       ALU.mult, ALU.bypass, rd=[PR[2]], wr=[Rnegc])
    if NB > 32:
        ts("dve", negc[:, 32:NB, :], pbank(3)[:, 0:(NB - 32) * 16].rearrange("p (a b) -> p a b", b=16), -1.0, None,
           ALU.mult, ALU.bypass, rd=[PR[3]], wr=[Rnegc])
    tt("dve", cT, lT, c3[:, 1, :], ALU.subtract, rd=[RlT, Rc3, RcT, PR[2], PR[3]], wr=[RcT])
    cp("dve", c3[:, 2, :], cT, rd=[RcT], wr=[Rc3])
    Rc3d = Res()
    dma("sp", c3_d, c3, rd=[Rc3], wr=[Rc3d])
    if dbg:
        dma("sp", dbg_d["c3"], c3, rd=[Rc3])
    P.barrier()
    A.reset(m_u)

    if stop == "P1a":
        return finish()
    if stop == "P1b":
        return finish()
    use_stg128()
    Vaug = A.alloc([NB, 8, 65], BF16)
    Rv = [Res() for _ in range(NB)]
    Rvone = Res()
    QT = [A.alloc([TP], BF16, parts=67) for _ in range(2)]
    KT = [A.alloc([TP], BF16, parts=67) for _ in range(2)]
    zs_off = A.mark()
    ZS = [A.alloc([TP], BF16, parts=64) for _ in range(2)]
    Rq = [[Res() for _ in range(NQT)] for _ in range(2)]
    Rk = [[Res() for _ in range(NQT)] for _ in range(2)]
    Rz = [[Res() for _ in range(NQT)] for _ in range(2)]
    Rqc = [Res(), Res()]
    qc_sem = [P.new_sem(), P.new_sem()]
    Rkone = Res()
    NPT = 5
    PTb = [A.alloc([512], BF16) for _ in range(NPT)]
    Rpt = [Res() for _ in range(NPT)]
    rsb = [arena_t[0:65, (zs_off + 2048 * i) // 2:(zs_off + 2048 * i) // 2 + 1024].bitcast(F32) for i in range(2)]
    Rrs = [Res(), Res()]
    etmp = [A.alloc([512], F32, parts=64) for _ in range(2)]
    Ret = [Res(), Res()]
    yT = [A.alloc([512], BF16, parts=64) for _ in range(2)]
    RyT = [Res(), Res()]
    Ryatt = [Res() for _ in range(NQT)]

    memset("pool", Vaug[:, :, :, 64:65], 1.0, wr=[Rvone])
    for s in range(2):
        memset("pool", KT[s][64:67, :], 1.0, wr=[Rkone])
    st_i = 0
    pt_i = 0
    ep_i = 0
    ot_i = 0
    pw_i = [0]

    def pw_slice(npieces):
        for _ in range(npieces):
            k_ = pw_i[0]
            if k_ >= 40:
                return
            pw_i[0] += 1
            g, c = k_ // 4, (k_ % 4) * 128
            i = k_ % 2
            load_w(win_v, C_Z + g * 512 + c, 128, wtmp[i], Rwtmp[i], gpre)
            dma("sp", wssd_d[g].rearrange("p (a b) -> p a b", a=KC)[:, :, c:c + 128], wtmp[i], rd=[Rwtmp[i]],
                wr=[Rwssd[g]])

    for half in range(2):
        for m in range(NB):
            b = m % 2
            for kc in range(KC):
                mm(pbank(b), uT[:, kc, 128 * m:128 * m + 128], wv[:, kc, :], start=(kc == 0), stop=(kc == KC - 1),
                   rd=[Rwv, Ru[m]], wr=[PR[b]], sig=(kc == KC - 1))
            cp("dve" if m % 2 == 0 else "act", Vaug[:, m, :, 0:64], pbank(b).rearrange("p (a b) -> p a b", a=8),
               rd=[PR[b]], wr=[Rv[m]])
            if m == 0:
                memset("pool", Vaug[0:NFILL, 0, :, :], 0.0, rd=[Rv[0], Rvone], wr=[Rv[0], Rvone])
        if half == 0:
            load_w(win_v, C_V + 512, 512, wv, Rwv, gpre)
        for hp in range(4):
            gp = half * 4 + hp
            wp = wp2[gp % 2]
            Rwp = Rwp2[gp % 2]
            for s in range(2):
                dma("sp", QT[s][64:67, :], c3_d[2 * gp + s], rd=[Rc3d], wr=[Rqc[s]], sem=qc_sem[s])
            for I in range(NQT):
                t0 = 512 * I
                w = min(512, TP - t0)
                for (ci, dst, RR, fn, scale) in ((0, QT, Rq, AF.Copy, 0.125), (1, KT, Rk, AF.Copy, None),
                                                 (2, ZS, Rz, AF.Silu, None)):
                    b = (I * 3 + ci) % 2
                    for kc in range(KC):
                        mm(pbank(b)[:, 0:w], wp[:, kc, ci * 128:(ci + 1) * 128], uT[:, kc, t0:t0 + w],
                           start=(kc == 0), stop=(kc == KC - 1), rd=[Rwp] + ru(t0, t0 + w), wr=[PR[b]],
                           sig=(kc == KC - 1))
                    if fn == AF.Silu:
                        act(dst[0][0:64, t0:t0 + w], pbank(b)[0:64, 0:w], fn, rd=[PR[b]], wr=[RR[0][I]])
                    else:
                        ts("dve", dst[0][0:64, t0:t0 + w], pbank(b)[0:64, 0:w], scale if scale else 1.0, None,
                           ALU.mult, ALU.bypass, rd=[PR[b]], wr=[RR[0][I]])
                    act(dst[1][0:64, t0:t0 + w], pbank(b)[64:128, 0:w], fn, rd=[PR[b]], wr=[RR[1][I]], scale=scale)
            if gp + 1 < 8:
                load_pair_w(gp + 1)
            pw_slice(5)
            for s in range(2):
                h = 2 * gp + s
                hl = h % 8
                steps = []
                for I in range(NQT):
                    t0 = 512 * I
                    w = min(512, TP - t0)
                    jmax = (t0 + w) // 128 - 1
                    for j in range(jmax + 1):
                        steps.append((I, j, t0, w, jmax))
                LA = 4
                infl = {}
                obs = {}
                deferred = []

                def emit_st(k):
                    nonlocal st_i, pt_i
                    I, j, t0, w, jmax = steps[k]
                    r = j - 4 * I
                    qlo = 128 * r if r >= 0 else 0
                    N = w - qlo
                    sb_ = st_i % 5
                    st_i += 1
                    pi = pt_i % NPT
                    pt_i += 1
                    mm(pbank(sb_)[:, 0:N], KT[s][0:67, 128 * j:128 * j + 128], QT[s][0:67, t0 + qlo:t0 + w],
                       rd=[Rk[s][j // 4], Rkone, Rq[s][I], Rqc[s]], wr=[PR[sb_]])
                    act(PTb[pi][:, 0:N], pbank(sb_)[:, 0:N], AF.Exp, rd=[PR[sb_], Rnegc], wr=[Rpt[pi]],
                        bias=negc[:, j, h:h + 1])
                    if r >= 0:
                        tt("pool", PTb[pi][:, 0:128], PTb[pi][:, 0:128], tri_bf, ALU.mult,
                           rd=[Rpt[pi], Rc], wr=[Rpt[pi]])
                    infl[k] = (pi, qlo, N)

                def emit_pv(k):
                    nonlocal ot_i, ep_i
                    I, j, t0, w, jmax = steps[k]
                    pi, qlo, N = infl.pop(k)
                    if j == 0:
                        obs[I] = 5 + (ot_i % 2)
                        ot_i += 1
                    ob = obs[I]
                    mm(pbank(ob)[0:65, qlo:w], Vaug[:, j, hl, :], PTb[pi][:, 0:N], start=(j == 0),
                       stop=(j == jmax), rd=[Rv[j], Rvone, Rpt[pi]], wr=[PR[ob]], sig=(j == jmax))
                    if j != jmax:
                        return
                    e_ = ep_i % 2
                    ep_i += 1
                    if I == 0:
                        ts("dve", rsb[e_][64:65, 0:w], pbank(ob)[64:65, 0:w], 1e-30, None, ALU.add, ALU.bypass,
                           rd=[PR[ob]], wr=[Rrs[e_]])
                        recip(rsb[e_][64:65, 0:w], rsb[e_][64:65, 0:w], rd=[Rrs[e_]], wr=[Rrs[e_]])
                    else:
                        recip(rsb[e_][64:65, 0:w], pbank(ob)[64:65, 0:w], rd=[PR[ob]], wr=[Rrs[e_]])
                    tt("dve", etmp[e_][:, 0:w], pbank(ob)[0:64, 0:w], ZS[s][:, t0:t0 + w], ALU.mult,
                       rd=[PR[ob], Rz[s][I]], wr=[Ret[e_]])

                    def part2(e_=e_, w=w, t0=t0, I=I):
                        mm(pbank(7)[0:64, 0:w], ones_f[64:65, 0:64], rsb[e_][64:65, 0:w], rd=[Rrs[e_], Rc],
                           wr=[PR[7]])
                        tt("dve", yT[e_][:, 0:w], etmp[e_][:, 0:w], pbank(7)[0:64, 0:w], ALU.mult,
                           rd=[Ret[e_], PR[7]], wr=[RyT[e_]])
                        nt = w // 128
                        dma("sp", yatt_d[4 * I:4 * I + nt, s * 64:(s + 1) * 64, gp, :].rearrange("m p t -> p m t"),
                            yT[e_][:, 0:w].rearrange("p (m t) -> p m t", t=128), rd=[RyT[e_]], wr=[Ryatt[I]])
                    deferred.append((k + LA + 10, part2))

                k = 0
                while k < len(steps) + LA or deferred:
                    if k < len(steps):
                        emit_st(k)
                    if 0 <= k - LA < len(steps):
                        emit_pv(k - LA)
                    while deferred and (deferred[0][0] <= k or k >= len(steps) + LA):
                        deferred.pop(0)[1]()
                    k += 1
    P.barrier()
    A.reset(m_u)
    if dbg:
        dma("sp", dbg_d["yatt"], yatt_d, rd=Ryatt)

    if stop == "P2":
        return finish()
    A.reset(m_const)
    uT_ = A.alloc([KC, TP], BF16)
    wdt = A.alloc([KC, 32], BF16)
    Rwdt = Res()
    m3 = A.mark()
    new_stg(32)
    load_w(win_v, C_DT, 32, wdt, Rwdt, gpre)
    P.barrier()
    A.reset(m3)
    wring = [A.alloc([KC, 512], BF16) for _ in range(2)]
    Rwr = [Res(), Res()]
    wr_i = [0]
    xbcT = A.alloc([24, 512], BF16)
    Rxbc = [Res() for _ in range(24)]
    convb = [A.alloc([516], F32) for _ in range(2)]
    Rcb = [Res(), Res()]
    cacc = [A.alloc([512], F32) for _ in range(2)]
    Rca = [Res(), Res()]
    halo = A.alloc([24, 4], F32)
    Rhalo = [Res() for _ in range(24)]
    hst = A.alloc([2048], F32)
    hbf = A.alloc([2048], BF16)
    Rhst = [Res() for _ in range(4)]
    Rhbf = [Res() for _ in range(4)]
    sz = [A.alloc([2048], BF16) for _ in range(4)]
    Rsz = [[Res() for _ in range(4)] for _ in range(4)]
    xsD = A.alloc([2048], BF16)
    RxsD = Res()
    xdt = A.alloc([2048], BF16)
    Rxdt = Res()
    xdtS = [A.alloc([2048], BF16) for _ in range(2)]
    RxdtS = [Res(), Res()]
    Btok = [A.alloc([512], BF16) for _ in range(2)]
    RBt = [Res(), Res()]
    sm2 = [A.alloc([10, 32], F32) for _ in range(2)]
    Rsm2 = [[Res() for _ in range(10)] for _ in range(2)]
    DTX, EDT, DT, DTA, ACS, NACS, EACS, DD, DS, CD = range(10)
    CBm = A.alloc([4, 128], BF16)
    RCBm = Res()
    Dg = [A.alloc([4, 128], F32) for _ in range(2)]
    RDg = [Res(), Res()]
    decT = [A.alloc([4, 128], BF16) for _ in range(2)]
    Rdec = [Res(), Res()]
    MT = [A.alloc([4, 128], BF16) for _ in range(2)]
    RMT = [Res(), Res()]
    NYD = 5
    ydg = [A.alloc([512], F32) for _ in range(NYD)]
    Rydg = [Res() for _ in range(NYD)]
    yg = [A.alloc([512], F32) for _ in range(2)]
    Ryg = [Res(), Res()]
    sqj = A.alloc([512], F32)
    Rsqj = Res()
    gst = [A.alloc([4], F32) for _ in range(2)]
    Rgst = [Res(), Res()]
    yn = A.alloc([2048], BF16)
    Ryn = [Res() for _ in range(4)]
    ynT = A.alloc([16, 128], BF16)
    RynT = Res()
    Ryssd = [Res() for _ in range(NCH)]
    memset("pool", halo, 0.0, wr=Rhalo)

    psA_i = [0]
    psB_i = [0]

    def psA():
        b = psA_i[0] % 4
        psA_i[0] += 1
        return b

    def psB():
        b = 4 + psB_i[0] % 4
        psB_i[0] += 1
        return b

    def get_w(g):
        i = wr_i[0] % 2
        wr_i[0] += 1
        dma("sp", wring[i].rearrange("p a b -> p (a b)"), wssd_d[g], rd=[Rwssd[g]], wr=[Rwr[i]])
        return i

    chunks = [(128 * c, 128) for c in range(NCH)]
    tiles = [list(range(k, min(k + 4, NCH))) for k in range(0, NCH, 4)]
    cnt = {"cv": 0, "yd": 0, "q": 0, "g": 0}

    def z_pass(tl, ps):
        for g in range(4):
            wi = get_w(g)
            for ci, c in enumerate(tl):
                tok0, Lc = chunks[c]
                b = ps()
                for kc in range(KC):
                    mm(pbank(b), uT[:, kc, tok0:tok0 + Lc], wring[wi][:, kc, :], start=(kc == 0),
                       stop=(kc == KC - 1), rd=[Rwr[wi]] + ru(tok0, tok0 + Lc), wr=[PR[b]], sig=(kc == KC - 1))
                act(sz[ci][:, g * 512:(g + 1) * 512], pbank(b), AF.Silu, rd=[PR[b]], wr=[Rsz[ci][g]])
                yield

    def conv_pass(tl, ps):
        ts_ = chunks[tl[0]][0]
        te_ = chunks[tl[-1]][0] + 128
        Wk = te_ - ts_
        for wgi in range(6):
            wi = get_w(4 + wgi)
            for cgl in range(4):
                cg = wgi * 4 + cgl
                b = ps()
                for kc in range(KC):
                    mm(pbank(b)[:, 0:Wk], wring[wi][:, kc, cgl * 128:(cgl + 1) * 128], uT[:, kc, ts_:te_],
                       start=(kc == 0), stop=(kc == KC - 1), rd=[Rwr[wi]] + ru(ts_, te_), wr=[PR[b]],
                       sig=(kc == KC - 1))
                v = cnt["cv"] % 2
                cnt["cv"] += 1
                cp("act", convb[v][:, 3:3 + Wk], pbank(b)[:, 0:Wk], rd=[PR[b]], wr=[Rcb[v]])
                cp("pool", convb[v][:, 0:3], halo[:, cg, 0:3], rd=[Rhalo[cg]], wr=[Rcb[v]])
                cp("pool", halo[:, cg, 0:3], convb[v][:, Wk:Wk + 3], rd=[Rcb[v]], wr=[Rhalo[cg]])
                act(cacc[v][:, 0:Wk], pbank(b)[:, 0:Wk], AF.Identity, rd=[PR[b], Rp], wr=[Rca[v]],
                    bias=cw[:, cg, 4:5], scale=cw[:, cg, 3:4])
                for kk in range(0, 3):
                    stt(cacc[v][:, 0:Wk], convb[v][:, kk:kk + Wk], cw[:, cg, kk:kk + 1], cacc[v][:, 0:Wk],
                        ALU.mult, ALU.add, rd=[Rcb[v], Rca[v], Rp], wr=[Rca[v]])
                act(xbcT[:, cg, 0:Wk], cacc[v][:, 0:Wk], AF.Silu, rd=[Rca[v]], wr=[Rxbc[cg]])
                if tl[0] == 0:
                    memset("pool", xbcT[:, cg, 0:NFILL], 0.0, rd=[Rxbc[cg]], wr=[Rxbc[cg]])
                yield

    def stageA(c, ts_):
        tok0, Lc = chunks[c]
        off = tok0 - ts_
        first = (c == 0)
        p = c % 2
        sm = sm2[p]
        Rsm = Rsm2[p]
        b = psA()
        for kc in range(KC):
            mm(pbank(b)[:, 0:32], uT[:, kc, tok0:tok0 + Lc], wdt[:, kc, :], start=(kc == 0),
               stop=(kc == KC - 1), rd=[Rwdt] + ru(tok0, tok0 + Lc), wr=[PR[b]], sig=(kc == KC - 1))
        tt("dve", sm[:, DTX, :], pbank(b)[:, 0:32], dtb_bc, ALU.add, rd=[PR[b], Rp], wr=[Rsm[DTX]])
        act(sm[:, EDT, :], sm[:, DTX, :], AF.Exp, rd=[Rsm[DTX]], wr=[Rsm[EDT]])
        act(sm[:, DT, :], sm[:, EDT, :], AF.Ln, rd=[Rsm[EDT]], wr=[Rsm[DT]], bias=1.0)
        if first:
            memset("dve", sm[0:NFILL, DT, :], 0.0, rd=[Rsm[DT]], wr=[Rsm[DT]])
        tt("dve", sm[:, DTA, :], sm[:, DT, :], a_bc, ALU.mult, rd=[Rsm[DT], Rp], wr=[Rsm[DTA]])
        yield
        for hb in range(2):
            b = psA()
            pv = pbank_bf(b).rearrange("p (a b) -> p a b", a=8)
            for k8 in range(8):
                cg = hb * 8 + k8
                tr(pv[:, k8, :], xbcT[:, cg, off:off + Lc], ident_bf, rd=[Rxbc[cg], Rc], wr=[PR[b]], sig=(k8 == 7))
            tt("dve", xdt[:, hb * 1024:(hb + 1) * 1024].rearrange("p (a b) -> p a b", a=16),
               pbank_bf(b).rearrange("p (a b) -> p a b", a=16),
               bc_last(sm[:, DT, hb * 16:(hb + 1) * 16], 64), ALU.mult, rd=[PR[b], Rsm[DT]], wr=[Rxdt])
            cp("act", xsD[:, hb * 1024:(hb + 1) * 1024], pbank_bf(b), rd=[PR[b]], wr=[RxsD])
            tt("pool", xsD[:, hb * 1024:(hb + 1) * 1024].rearrange("p (a b) -> p a b", a=16),
               xsD[:, hb * 1024:(hb + 1) * 1024].rearrange("p (a b) -> p a b", a=16),
               bc_last(dsk_bc[:, hb * 16:(hb + 1) * 16], 64), ALU.mult, rd=[RxsD, Rp], wr=[RxsD])
            yield
        b = psA()
        mm(pbank(b)[:, 0:32], tri_f, sm[:, DTA, :], rd=[Rsm[DTA], Rc], wr=[PR[b]])
        cp("act", sm[:, ACS, :], pbank(b)[:, 0:32], rd=[PR[b]], wr=[Rsm[ACS]])
        act(sm[:, EACS, :], pbank(b)[:, 0:32], AF.Exp, rd=[PR[b]], wr=[Rsm[EACS]])
        ts("dve", sm[:, NACS, :], sm[:, ACS, :], -1.0, None, ALU.mult, ALU.bypass, rd=[Rsm[ACS]], wr=[Rsm[NACS]])
        b = psA()
        mm(pbank(b)[:, 0:32], lsel128, sm[:, ACS, :], rd=[Rsm[ACS], Rc], wr=[PR[b]])
        tt("dve", sm[:, DD, :], pbank(b)[:, 0:32], sm[:, ACS, :], ALU.subtract, rd=[PR[b], Rsm[ACS]], wr=[Rsm[DD]])
        act(sm[:, CD, :], pbank(b)[:, 0:32], AF.Exp, rd=[PR[b]], wr=[Rsm[CD]])
        act(sm[:, DS, :], sm[:, DD, :], AF.Exp, rd=[Rsm[DD]], wr=[Rsm[DS]])
        yield
        b = psA()
        pv = pbank_bf(b).rearrange("p (a b) -> p a b", a=8)
        for k4 in range(4):
            tr(pv[:, k4, :], xbcT[:, 16 + k4, off:off + Lc], ident_bf, rd=[Rxbc[16 + k4], Rc], wr=[PR[b]],
               sig=(k4 == 3))
        cp("act", Btok[p], pbank_bf(b)[:, 0:512], rd=[PR[b]], wr=[RBt[p]])
        tt("pool", xdtS[p].rearrange("p (a b) -> p a b", a=32), xdt.rearrange("p (a b) -> p a b", a=32),
           bc_last(sm[:, DS, :], 64), ALU.mult, rd=[Rxdt, Rsm[DS]], wr=[RxdtS[p]])
        b = psA()
        pv = pbank(b).rearrange("p (a b) -> p a b", a=4)
        for g in range(4):
            mm(pv[:, g, :], xbcT[:, 16 + g, off:off + Lc], xbcT[:, 20 + g, off:off + Lc],
               rd=[Rxbc[16 + g], Rxbc[20 + g]], wr=[PR[b]], sig=(g == 3))
        tt("dve", CBm, pv, bc_mid(tri_bf, 4), ALU.mult, rd=[PR[b], Rc], wr=[RCBm])
        yield
        yslots = []
        for g in range(4):
            byd = psA()
            for qq in range(2):
                qd = 2 * g + qq
                qi = cnt["q"] % 2
                cnt["q"] += 1
                tt("pool", Dg[qi], bc_mid(ident_f, 4), bc_last(sm[:, ACS, 4 * qd:4 * qd + 4], 128), ALU.mult,
                   rd=[Rsm[ACS], Rc], wr=[RDg[qi]])
                be = psA()
                pe_ = pbank(be).rearrange("p (a b) -> p a b", a=4)
                mm(pbank(be), ident_bf, maskneg.rearrange("p a b -> p (a b)"), start=True, stop=False,
                   rd=[Rc], wr=[PR[be]], sig=False)
                mm(pbank(be), ones_f, Dg[qi].rearrange("p a b -> p (a b)"), start=False, stop=True,
                   rd=[RDg[qi], Rc], wr=[PR[be]])
                for hh in range(4):
                    h = 4 * qd + hh
                    act(decT[qi][:, hh, :], pe_[:, hh, :], AF.Exp, rd=[PR[be], Rsm[NACS]], wr=[Rdec[qi]],
                        bias=sm[:, NACS, h:h + 1])
                tt("dve", MT[qi], decT[qi], bc_mid(CBm[:, g, :], 4), ALU.mult, rd=[Rdec[qi], RCBm], wr=[RMT[qi]])
                for hh in range(4):
                    h = 4 * qd + hh
                    hl = h % 8
                    mm(pbank(byd)[:, hl * 64:(hl + 1) * 64], MT[qi][:, hh, :], xdt[:, h * 64:(h + 1) * 64],
                       start=True, stop=False, rd=[RMT[qi], Rxdt], wr=[PR[byd]], sig=False)
                    mm(pbank(byd)[:, hl * 64:(hl + 1) * 64], ident_bf, xsD[:, h * 64:(h + 1) * 64],
                       start=False, stop=True, rd=[RxsD, Rc], wr=[PR[byd]], sig=(qq == 1 and hh == 3))
                yield
            ys = cnt["yd"] % NYD
            cnt["yd"] += 1
            cp("act", ydg[ys], pbank(byd), rd=[PR[byd]], wr=[Rydg[ys]])
            yslots.append(ys)
            yield
        stA_out[c] = yslots

    stA_out = {}

    def stageB(c, ts_, ci):
        tok0, Lc = chunks[c]
        off = tok0 - ts_
        first = (c == 0)
        p = c % 2
        sm = sm2[p]
        Rsm = Rsm2[p]
        yslots = stA_out[c]
        for g in range(4):
            gi = cnt["g"] % 2
            cnt["g"] += 1
            gs = slice(g * 512, (g + 1) * 512)
            ys = yslots[g]
            if not first:
                byo = psB()
                mm(pbank(byo), xbcT[:, 20 + g, off:off + Lc], hbf[:, gs], rd=[Rxbc[20 + g], Rhbf[g]], wr=[PR[byo]])
                for hl in range(8):
                    h = 8 * g + hl
                    stt(yg[gi][:, hl * 64:(hl + 1) * 64], pbank(byo)[:, hl * 64:(hl + 1) * 64],
                        sm[:, EACS, h:h + 1], ydg[ys][:, hl * 64:(hl + 1) * 64], ALU.mult, ALU.add,
                        rd=[PR[byo], Rsm[EACS], Rydg[ys]], wr=[Ryg[gi]])
                tt("dve", yg[gi], yg[gi], sz[ci][:, gs], ALU.mult, rd=[Ryg[gi], Rsz[ci][g]], wr=[Ryg[gi]])
            else:
                tt("dve", yg[gi], ydg[ys], sz[ci][:, gs], ALU.mult, rd=[Rydg[ys], Rsz[ci][g]], wr=[Ryg[gi]])
            act(sqj, yg[gi], AF.Square, rd=[Ryg[gi]], wr=[Rgst[gi], Rsqj], accum=gst[gi][:, 0:1])
            act(gst[gi][:, 1:2], gst[gi][:, 0:1], AF.Ln, rd=[Rgst[gi]], wr=[Rgst[gi]], bias=epsc[:, 0:1],
                scale=1.0 / 512)
            act(gst[gi][:, 2:3], gst[gi][:, 1:2], AF.Exp, rd=[Rgst[gi]], wr=[Rgst[gi]], scale=-0.5)
            ts("dve", yn[:, gs], yg[gi], gst[gi][:, 2:3], None, ALU.mult, ALU.bypass, rd=[Ryg[gi], Rgst[gi]],
               wr=[Ryn[g]])
            yield
            bsn = psB()
            mm(pbank(bsn), Btok[p][:, g * 128:(g + 1) * 128], xdtS[p][:, gs], rd=[RBt[p], RxdtS[p]], wr=[PR[bsn]])
            if first:
                cp("dve", hst[:, gs], pbank(bsn), rd=[PR[bsn]], wr=[Rhst[g]])
            else:
                tt("pool", hst[:, gs].rearrange("p (a b) -> p a b", a=8),
                   hst[:, gs].rearrange("p (a b) -> p a b", a=8), bc_last(sm[:, CD, 8 * g:8 * g + 8], 64),
                   ALU.mult, rd=[Rhst[g], Rsm[CD]], wr=[Rhst[g]])
                tt("dve", hst[:, gs], hst[:, gs], pbank(bsn), ALU.add, rd=[Rhst[g], PR[bsn]], wr=[Rhst[g]])
            cp("pool", hbf[:, gs], hst[:, gs], rd=[Rhst[g]], wr=[Rhbf[g]])
            yield
        for hb in range(2):
            b = psB()
            pv = pbank_bf(b).rearrange("p (a b) -> p a b", a=8)
            for k8 in range(8):
                cg = hb * 8 + k8
                tr(pv[:, k8, :], yn[:, cg * 128:(cg + 1) * 128], ident_bf, rd=[Ryn[cg // 4], Rc], wr=[PR[b]],
                   sig=(k8 == 7))
            cp("act" if hb == 0 else "dve", ynT[:, hb * 8:(hb + 1) * 8, :], pv, rd=[PR[b]], wr=[RynT])
            yield
        dma("sp", yssd_d[c], ynT, rd=[RynT], wr=[Ryssd[c]])

    def interleave(*gens):
        P.begin_capture()
        for g in gens:
            if g is not None:
                for _ in g:
                    pass
        P.end_capture(keep=KEEP[0])

    seq = []
    for ti, tl in enumerate(tiles):
        ts_ = chunks[tl[0]][0]
        if ti == 0:
            seq.append(conv_pass(tl, psA))
            seq.append(z_pass(tl, psB))
            seq.append(stageA(tl[0], ts_))
            seq.append("cut")
        for ci, c in enumerate(tl):
            seq.append(stageB(c, ts_, ci))
            if ci + 1 < len(tl):
                seq.append(stageA(c + 1, ts_))
            elif ti + 1 < len(tiles):
                ntl = tiles[ti + 1]
                seq.append(conv_pass(ntl, psA))
                seq.append(z_pass(ntl, psB))
                seq.append(stageA(ntl[0], chunks[ntl[0]][0]))
        seq.append("cut")
    KEEP = [P3_KEEP]
    region = []
    ncut = sum(1 for g in seq if g == "cut")
    icut = 0
    for g in seq:
        if g == "cut":
            icut += 1
            if icut == ncut:
                KEEP[0] = 1.0
            interleave(*region)
            region = []
        else:
            region.append(g)
    assert not region and not P.carry
    P.barrier()
    A.reset(m_const)
    if dbg:
        dma("sp", dbg_d["yssd"], yssd_d, rd=Ryssd)
    if stop == "P3":
        return finish()
    new_stg(256)
    wps = A.alloc([16, D], BF16)
    wpa = A.alloc([KC, D], BF16)
    wo = A.alloc([KC, D], BF16)
    Rw4 = Res()
    npost_bc = A.alloc([D], F32)
    dma("sp", npost_bc, npost_d[0].partition_broadcast(128), wr=[Rw4])
    wps_v = wps_d.rearrange("(kc p) n -> p kc n", p=128)
    wpa_v = wpa_d.rearrange("(kc p) n -> p kc n", p=128)
    wo_v = wo_d.rearrange("(kc p) n -> p kc n", p=128)
    st4 = [A.alloc([2, D], F32) for _ in range(2)]
    Rst4 = [Res(), Res()]
    for k0 in range(0, 16, 2):
        i_ = (k0 // 2) % 2
        dma("sp", st4[i_], wps_v[:, k0:k0 + 2, :], wr=[Rst4[i_]])
        for kk in range(2):
            kc = k0 + kk
            if kk == 0:
                ts("dve", wps[:, kc, :], st4[i_][:, kk, :], snrm[:, kc:kc + 1], None, ALU.mult, ALU.bypass,
                   rd=[Rst4[i_], Rp], wr=[Rw4])
            else:
                act(wps[:, kc, :], st4[i_][:, kk, :], AF.Copy, rd=[Rst4[i_], Rp], wr=[Rw4], scale=snrm[:, kc:kc + 1])
    for k0 in range(0, KC, 4):
        dma("pool", wpa[:, k0:k0 + 4, :], wpa_v[:, k0:k0 + 4, :], wr=[Rw4], sem=P.new_sem())
        dma("pool", wo[:, k0:k0 + 4, :], wo_v[:, k0:k0 + 4, :], wr=[Rw4], sem=P.new_sem())
    ysT = [A.alloc([16, 128], BF16) for _ in range(2)]
    yaT = [A.alloc([KC, 128], BF16) for _ in range(2)]
    gt = [A.alloc([2048], BF16) for _ in range(2)]
    xr = [A.alloc([D], F32) for _ in range(2)]
    Rin = [Res(), Res()]
    t1 = [A.alloc([D], F32) for _ in range(2)]
    Rt1 = [Res(), Res()]
    t2 = [A.alloc([D], F32) for _ in range(2)]
    Rt2 = [Res(), Res()]
    mg = [A.alloc([D], BF16) for _ in range(2)]
    Rmg = [Res(), Res()]
    mgT = [A.alloc([KC, 128], BF16) for _ in range(2)]
    RmgT = [Res(), Res()]
    ob_ = [A.alloc([D], F32) for _ in range(2)]
    Rob = [Res(), Res()]
    pst = [A.alloc([4], F32) for _ in range(2)]
    Rpst = [Res(), Res()]

    def load_tile4(m, i):
        t0 = 128 * m
        n = 128
        dma("sp", ysT[i], yssd_d[m], rd=[Ryssd[m]], wr=[Rin[i]])
        dma("sp", yaT[i], yatt_d[m], rd=[Ryatt[m // 4]], wr=[Rin[i]])
        dma("sp", gt[i][0:n, :], gates_d[t0:t0 + n, :], rd=[Rgates[m]], wr=[Rin[i]])
        load_x_tile(m, xr[i], Rin[i])

    out_toks = []
    load_tile4(1, 1)
    for m in range(1, NB):
        if (m - 1) % 4 == 0:
            if P.cap is not None:
                P.end_capture(keep=P4_KEEP)
            P.begin_capture()
        i = m % 2
        t0 = 128 * m
        n = 128
        if m + 1 < NB:
            load_tile4(m + 1, 1 - i)
        ck("p4_load")
        for hf in range(2):
            for kc in range(16):
                mm(pbank(hf)[0:n, :], ysT[i][:, kc, 0:n], wps[:, kc, hf * 512:(hf + 1) * 512], start=(kc == 0),
                   stop=(kc == 15), rd=[Rin[i], Rw4], wr=[PR[hf]], sig=(kc == 15))
        for hf in range(2):
            for kc in range(KC):
                mm(pbank(2 + hf)[0:n, :], yaT[i][:, kc, 0:n], wpa[:, kc, hf * 512:(hf + 1) * 512], start=(kc == 0),
                   stop=(kc == KC - 1), rd=[Rin[i], Rw4], wr=[PR[2 + hf]], sig=(kc == KC - 1))
        ck("p4_ab")
        for hf in range(2):
            hs = slice(hf * 512, (hf + 1) * 512)
            tt("dve", t1[i][:, hs], pbank(hf), gt[i][:, hs], ALU.mult, rd=[PR[hf], Rin[i]], wr=[Rt1[i]])
            tt("dve", t2[i][:, hs], pbank(2 + hf), gt[i][:, D + hf * 512:D + (hf + 1) * 512], ALU.mult,
               rd=[PR[2 + hf], Rin[i]], wr=[Rt2[i]])
        tt("dve", mg[i][0:n, :], t1[i][0:n, :], t2[i][0:n, :], ALU.add, rd=[Rt1[i], Rt2[i]], wr=[Rmg[i]])
        ck("p4_merge")
        pv = pbank_bf(4 + i).rearrange("p (a b) -> p a b", a=KC)
        for kc in range(KC):
            tr(pv[:, kc, 0:n], mg[i][0:n, kc * 128:(kc + 1) * 128], ident_bf[0:n, 0:n], rd=[Rmg[i], Rc],
               wr=[PR[4 + i]], sig=(kc == KC - 1))
        cp("act", mgT[i][:, :, 0:n], pv[:, :, 0:n], rd=[PR[4 + i]], wr=[RmgT[i]])
        for hf in range(2):
            for kc in range(KC):
                mm(pbank(6 + hf)[0:n, :], mgT[i][:, kc, 0:n], wo[:, kc, hf * 512:(hf + 1) * 512], start=(kc == 0),
                   stop=(kc == KC - 1), rd=[RmgT[i], Rw4], wr=[PR[6 + hf]], sig=(kc == KC - 1))
        ck("p4_o")
        for hf in range(2):
            act(t1[i][:, hf * 512:(hf + 1) * 512], pbank(6 + hf), AF.Square, rd=[PR[6 + hf], Rt1[i]],
                wr=[Rt1[i], Rpst[i]], accum=pst[i][:, hf:hf + 1])
        tt("dve", pst[i][:, 0:1], pst[i][:, 0:1], pst[i][:, 1:2], ALU.add, rd=[Rpst[i]], wr=[Rpst[i]])
        act(pst[i][:, 2:3], pst[i][:, 0:1], AF.Sqrt, rd=[Rpst[i]], wr=[Rpst[i]], bias=EPS, scale=1.0 / D)
        recip(pst[i][:, 3:4], pst[i][:, 2:3], rd=[Rpst[i]], wr=[Rpst[i]])
        for hf in range(2):
            hs = slice(hf * 512, (hf + 1) * 512)
            stt(ob_[i][:, hs], pbank(6 + hf), pst[i][:, 3:4], npost_bc[:, hs], ALU.mult, ALU.mult,
                rd=[PR[6 + hf], Rpst[i], Rw4], wr=[Rob[i]])
        tt("pool", ob_[i][0:n, :], ob_[i][0:n, :], xr[i][0:n, :], ALU.add, rd=[Rob[i], Rin[i]], wr=[Rob[i]])
        ck("p4_norm")
        out_toks.append(dma("sp", out_d[128 * (m - 1):128 * m, :], ob_[i][:, :], rd=[Rob[i]]))
        ck("p4_t%d" % m)
    if P.cap is not None:
        P.end_capture()
    P.barrier()
    P.emit(sems)
    es.close()
    return nc, A.peak


def _prep_inputs(inputs, b):
    f = lambda a: np.ascontiguousarray(np.asarray(a, dtype=np.float32))
    conv_wb = np.concatenate([np.asarray(inputs["conv_w"][0]), np.asarray(inputs["conv_b"])], axis=0)
    return {
        "x": f(inputs["x"][b]),
        "meta": f(inputs["meta_tokens"]),
        "norm_pre": f(np.asarray(inputs["norm_pre"]).reshape(KC, 128)),
        "w_in": f(inputs["w_in"][0]),
        "conv_wb": f(conv_wb),
        "dt_bias": f(inputs["dt_bias"]),
        "a_log": f(inputs["a_log"]),
        "d_skip": f(inputs["d_skip"]),
        "ssd_norm": f(np.asarray(inputs["ssd_norm"]).reshape(16, 128)),
        "fgate_bias": f(np.asarray(inputs["fgate_bias"]).reshape(16, 1)),
        "gate_bias": f(inputs["gate_bias"]),
        "w_proj_ssd": f(inputs["w_proj_ssd"][0]),
        "w_proj_att": f(inputs["w_proj_att"][0]),
        "w_out": f(inputs["w_out"][0]),
        "norm_post": f(inputs["norm_post"]),
    }


def kernel(**inputs):
    x = np.asarray(inputs["x"])
    B, SEQ, _ = x.shape
    nc = bass.Bass("TRN2", target_bir_lowering=False)
    build(nc, SEQ)
    in_maps = [_prep_inputs(inputs, b) for b in range(B)]
    res = run_bass_kernel_spmd(nc, in_maps, core_ids=list(range(B)))
    return np.stack([np.asarray(r["out"], dtype=np.float32) for r in res.results], axis=0)
```

```python
import numpy as np
from contextlib import ExitStack
import concourse.bass as bass
import concourse.mybir as mybir
from concourse.bass_utils import run_bass_kernel_spmd

F32 = mybir.dt.float32
BF16 = mybir.dt.bfloat16
AF = mybir.ActivationFunctionType
ALU = mybir.AluOpType

ENGS = ("pe", "act", "dve", "pool", "sp")

D = 1024
KC = 8
NCOLS = 11312
C_Z, C_XS, C_B, C_C, C_DT, C_ZA, C_Q, C_K, C_V, C_F, C_G = (0, 2048, 4096, 4608, 5120, 5152, 6176, 7200,
                                                          8224, 9248, 9264)
EPS = 1e-6
NMETA = 16
ACT_TBL = {AF.Exp: "explog", AF.Ln: "explog", AF.Silu: "silu", AF.Sqrt: "sqrt", AF.Sigmoid: "sigmoid"}
ACT_TBL_COST = 1300.0
P3_KEEP = 0.8
P4_KEEP = 0.8


class Res:
    __slots__ = ("w", "r", "excl")

    def __init__(self, excl=False):
        self.w = None
        self.r = []
        self.excl = excl


class Prog:
    def __init__(self, nc, n_dma_sems=32):
        self.nc = nc
        self.q = {e: [] for e in ENGS}
        self.cnt = {e: 0 for e in ENGS}
        self.seen = {e: {} for e in ENGS}
        self.n_dma = n_dma_sems
        self.dma_cnt = [0] * n_dma_sems
        self.dma_rr = 0
        self.pending = {e: False for e in ENGS}

    def _deps(self, eng, reads, writes):
        toks = []
        for r in reads:
            if r.w is not None:
                toks.append(r.w)
            if r.excl:
                toks.extend(t for t in r.r if t[2] != eng)
        for w in writes:
            if w.w is not None:
                toks.append(w.w)
            toks.extend(w.r)
        need = {}
        for (k, v, e) in toks:
            if e == eng and eng == "pe":
                continue
            if need.get(k, 0) < v:
                need[k] = v
        out = []
        seen = self.seen[eng]
        for k, v in need.items():
            if seen.get(k, 0) >= v:
                continue
            seen[k] = v
            out.append((k, v))
        return out

    def _commit(self, tok, reads, writes):
        for r in reads:
            r.r.append(tok)
            if len(r.r) > 64:
                best = {}
                for t in r.r:
                    if best.get(t[0], (None, 0, None))[1] < t[1]:
                        best[t[0]] = t
                r.r = list(best.values())
        for w in writes:
            w.w = tok
            w.r = []

    def begin_capture(self):
        self.cap = []

    carry = ()

    def end_capture(self, hop=600.0, keep=1.0):
        ops = list(self.carry) + self.cap
        self.carry = ()
        self.cap = None
        n = len(ops)
        lastw = {}
        readers = {}
        preds = [set() for _ in range(n)]
        for i, (kind, eng, fn, reads, writes, sig, cost, lat, tag) in enumerate(ops):
            for r in reads:
                k = id(r)
                if k in lastw:
                    preds[i].add(lastw[k])
            for w in writes:
                k = id(w)
                if k in lastw:
                    preds[i].add(lastw[k])
                for j in readers.get(k, ()):
                    preds[i].add(j)
            for r in reads:
                readers.setdefault(id(r), []).append(i)
            for w in writes:
                lastw[id(w)] = i
                readers[id(w)] = []
            preds[i].discard(i)
        succs = [[] for _ in range(n)]
        npred = [0] * n
        for i in range(n):
            npred[i] = len(preds[i])
            for j in preds[i]:
                succs[j].append(i)
        efree = {e: 0.0 for e in ENGS}
        fin = [0.0] * n
        rdy = [0.0] * n
        ready = [i for i in range(n) if npred[i] == 0]
        order = []
        import heapq
        TBL = ACT_TBL_COST
        while ready:
            best = None
            bt = None
            for i in ready:
                eng = ops[i][1]
                t = max(efree[eng], rdy[i])
                tg = ops[i][8]
                if tg is not None and tg != self.act_tbl:
                    t += TBL
                if bt is None or t < bt - 1e-9 or (abs(t - bt) <= 1e-9 and i < best):
                    bt = t
                    best = i
            i = best
            ready.remove(i)
            kind, eng, fn, reads, writes, sig, cost, lat, tag = ops[i]
            if tag is not None:
                self.act_tbl = tag
            efree[eng] = bt + cost
            fin[i] = bt + lat
            order.append(i)
            for j in succs[i]:
                npred[j] -= 1
                dly = fin[i] + (hop if ops[j][1] != eng or kind == "dma" else 0.0)
                if dly > rdy[j]:
                    rdy[j] = dly
                if npred[j] == 0:
                    ready.append(j)
        assert len(order) == n
        if keep < 1.0:
            n_emit = int(keep * n)
            left = sorted(order[n_emit:])
            self.carry = [ops[i] for i in left]
            order = order[:n_emit]
        for i in order:
            kind, eng, fn, reads, writes, sig, cost, lat, tag = ops[i]
            if kind == "dma":
                self.dma(eng, fn, reads, writes, sem=sig)
            else:
                self.op(eng, fn, reads, writes, True)

    cap = None
    act_tbl = None

    def op(self, eng, fn, reads=(), writes=(), sig=True, cost=300.0, tag=None):
        if self.cap is not None:
            self.cap.append(("op", eng, fn, tuple(reads), tuple(writes), sig, cost, cost, tag))
            return None
        waits = self._deps(eng, reads, writes)
        if sig:
            self.cnt[eng] += 1
            tok = (('e', eng), self.cnt[eng], eng)
            self.q[eng].append((waits, fn, ('e', eng), 1))
            self.pending[eng] = False
        else:
            tok = (('e', eng), self.cnt[eng] + 1, eng)
            self.q[eng].append((waits, fn, None, 0))
            self.pending[eng] = True
        self._commit(tok, reads, writes)
        return tok

    def new_sem(self):
        self.dma_cnt.append(0)
        return len(self.dma_cnt) - 1

    def dma(self, eng, fn, reads=(), writes=(), sem=None, cost=3000.0):
        if self.cap is not None:
            self.cap.append(("dma", eng, fn, tuple(reads), tuple(writes), sem, 100.0, cost, None))
            return None
        waits = self._deps(eng, reads, writes)
        if sem is None:
            i = self.dma_rr
            self.dma_rr = (self.dma_rr + 1) % self.n_dma
        else:
            i = sem
        self.dma_cnt[i] += 16
        tok = (('d', i), self.dma_cnt[i], 'dma')
        self.q[eng].append((waits, fn, ('d', i), 16))
        self._commit(tok, reads, writes)
        return tok

    def all_tokens(self):
        toks = []
        for e in ENGS:
            if self.cnt[e] > 0:
                toks.append((('e', e), self.cnt[e], e))
        for i in range(len(self.dma_cnt)):
            if self.dma_cnt[i] > 0:
                toks.append((('d', i), self.dma_cnt[i], 'dma'))
        return toks

    def wait_all(self, eng, toks):
        waits = []
        for (k, v, e) in toks:
            if self.seen[eng].get(k, 0) >= v:
                continue
            self.seen[eng][k] = v
            waits.append((k, v))
        if waits:
            self.q[eng].append((waits, None, None, 0))

    def barrier(self):
        for e in ENGS:
            assert not self.pending[e], "barrier with unsignalled op on " + e
        toks = self.all_tokens()
        for e in ENGS:
            self.wait_all(e, toks)

    def emit(self, sems):
        nc = self.nc
        q = self.q

        def run(e, name):
            for (waits, fn, sk, n) in q[name]:
                for (k, v) in waits:
                    e.wait_ge(sems[k], v)
                if fn is None:
                    continue
                inst = fn(e)
                if sk is not None:
                    inst.then_inc(sems[sk], n)

        with nc.Block() as block:
            @block.tensor
            def _(e):
                run(e, "pe")

            @block.scalar
            def _(e):
                run(e, "act")

            @block.vector
            def _(e):
                run(e, "dve")

            @block.gpsimd
            def _(e):
                run(e, "pool")

            @block.sync
            def _(e):
                run(e, "sp")


class Arena:
    def __init__(self, t, nbytes):
        self.t = t
        self.cap = nbytes
        self.off = 0
        self.peak = 0
        self.top = nbytes

    def alloc(self, shape, dt, parts=128):
        n = 1
        for s in shape:
            n *= s
        esz = 2 if dt == BF16 else 4
        nb = (n * esz + 63) // 64 * 64
        off = self.off
        self.off += nb
        self.peak = max(self.peak, self.off)
        assert self.off <= self.cap, ("arena overflow", self.off, self.cap)
        ap = self.t[0:parts, off // 2: off // 2 + (n * esz) // 2]
        if dt == F32:
            ap = ap.bitcast(F32)
        if len(shape) == 2:
            ap = ap.rearrange("p (a b) -> p a b", a=shape[0])
        elif len(shape) == 3:
            ap = ap.rearrange("p (a b c) -> p a b c", a=shape[0], b=shape[1])
        return ap

    def alloc_top(self, shape, dt, parts=128):
        n = 1
        for s_ in shape:
            n *= s_
        esz = 2 if dt == BF16 else 4
        nb = (n * esz + 63) // 64 * 64
        self.top -= nb
        save = self.off
        self.off = self.top
        ap = self.alloc(shape, dt, parts)
        self.off = save
        self.peak = max(self.peak, save)
        return ap

    def mark(self):
        return self.off

    def reset(self, m):
        self.off = m


def bc_mid(ap2, n):
    p, f = ap2.shape
    return ap2.unsqueeze(1).to_broadcast([p, n, f])


def bc_last(ap2, n):
    p, f = ap2.shape
    return ap2.unsqueeze(2).to_broadcast([p, f, n])


class StopBuild(Exception):
    pass


_FIN = [None]


def build(nc, SEQ, dbg=False, stop=None):
    try:
        return _build(nc, SEQ, dbg, stop)
    except StopBuild:
        return _FIN[0]()


def _build(nc, SEQ, dbg=False, stop=None):
    assert SEQ % 128 == 0
    T = SEQ + 128
    NB = T // 128
    TP = T
    NCH = NB
    NQT = (TP + 511) // 512
    NFILL = 128 - NMETA

    dram = {}

    def din(name, shape, dt=F32):
        dram[name] = nc.dram_tensor(name, shape, dt, kind="ExternalInput").ap()
        return dram[name]

    x_d = din("x", [SEQ, D])
    meta_d = din("meta", [NMETA, D])
    npre_d = din("norm_pre", [KC, 128])
    win_d = din("w_in", [D, NCOLS])
    convwb_d = din("conv_wb", [5, 3072])
    dtb_d = din("dt_bias", [1, 32])
    alog_d = din("a_log", [1, 32])
    dsk_d = din("d_skip", [1, 32])
    snrm_d = din("ssd_norm", [16, 128])
    fgb_d = din("fgate_bias", [16, 1])
    gb_d = din("gate_bias", [1, 2048])
    wps_d = din("w_proj_ssd", [2048, D])
    wpa_d = din("w_proj_att", [D, D])
    wo_d = din("w_out", [D, D])
    npost_d = din("norm_post", [1, D])
    out_d = nc.dram_tensor("out", [SEQ, D], F32, kind="ExternalOutput").ap()

    c3_d = nc.dram_tensor("c3_s", [16, 3, TP], BF16, kind="Internal").ap()
    gates_d = nc.dram_tensor("gates_s", [TP, 2048], BF16, kind="Internal").ap()
    yatt_d = nc.dram_tensor("yatt_s", [NB, 128, KC, 128], BF16, kind="Internal").ap()
    yssd_d = nc.dram_tensor("yssd_s", [NB, 128, 16, 128], BF16, kind="Internal").ap()
    wssd_d = nc.dram_tensor("wssd_s", [10, 128, KC * 512], BF16, kind="Internal").ap()
    dbg_d = {}
    if dbg:
        dbg_d["uT"] = nc.dram_tensor("dbg_uT", [128, KC, TP], BF16, kind="ExternalOutput").ap()
        dbg_d["yatt"] = nc.dram_tensor("dbg_yatt", [NB, 128, KC, 128], BF16, kind="ExternalOutput").ap()
        dbg_d["yssd"] = nc.dram_tensor("dbg_yssd", [NB, 128, 16, 128], BF16, kind="ExternalOutput").ap()
        dbg_d["gates"] = nc.dram_tensor("dbg_gates", [TP, 2048], BF16, kind="ExternalOutput").ap()
        dbg_d["c3"] = nc.dram_tensor("dbg_c3", [16, 3, TP], BF16, kind="ExternalOutput").ap()

    win_v = win_d.rearrange("(kc p) n -> p kc n", p=128)

    es = ExitStack()
    ARENA_BYTES = 206 * 1024
    arena_t = es.enter_context(nc.sbuf_tensor("arena", [128, ARENA_BYTES // 2], BF16))
    A = Arena(arena_t, ARENA_BYTES)
    PTs = [es.enter_context(nc.psum_tensor("ps%d" % i, [128, 1024], F32)) for i in range(4)]
    PR = [Res(excl=True) for _ in range(8)]

    def pbank(b):
        return PTs[b // 2][:, (b % 2) * 512:(b % 2) * 512 + 512]

    def pbank_bf(b):
        return pbank(b).bitcast(BF16)

    P = Prog(nc)
    sems = {}
    for e in ENGS:
        sems[('e', e)] = es.enter_context(nc.semaphore("s_" + e))
    for i in range(P.n_dma + 16):
        sems[('d', i)] = es.enter_context(nc.semaphore("d_%d" % i))

    def finish():
        if P.cap is not None:
            P.end_capture()
        P.barrier()
        P.emit(sems)
        es.close()
        return nc, A.peak

    _FIN[0] = finish

    def ck(name):
        if stop == name:
            raise StopBuild()

    def fsz(ap):
        n = 1
        for d in ap.shape[1:]:
            n *= d
        return n

    def mm(out, lhsT, rhs, start=True, stop=True, rd=(), wr=(), sig=True):
        c = 64.0 + max(fsz(rhs), 64) / 2.0
        if rhs.dtype == F32:
            c *= 4
        return P.op("pe", lambda e, o=out, l=lhsT, r=rhs, s=start, t=stop:
                    e.matmul(o, lhsT=l, rhs=r, start=s, stop=t), rd, wr, sig, cost=c)

    def tr(out, in_, ident, rd=(), wr=(), sig=True):
        return P.op("pe", lambda e, o=out, i=in_, d=ident: e.transpose(o, i, d), rd, wr, sig, cost=130.0)

    def act(out, in_, func, rd=(), wr=(), bias=None, scale=None, accum=None):
        kw = {}
        if bias is not None:
            kw["bias"] = bias
        if scale is not None:
            kw["scale"] = scale
        if accum is not None:
            kw["accum_out"] = accum
        c = 220.0 + max(fsz(in_), 64) / 1.2 + (100.0 if accum is not None else 0.0)
        tag = ACT_TBL.get(func)
        return P.op("act", lambda e, o=out, i=in_, f=func, kw=kw: e.activation(out=o, in_=i, func=f, **kw), rd, wr,
                    cost=c, tag=tag)

    def ecost(eng, ap):
        n = max(fsz(ap), 64)
        return (120.0 + n / 0.96) if eng == "dve" else (300.0 + n / 0.6)

    def tt(eng, out, in0, in1, op, rd=(), wr=()):
        return P.op(eng, lambda e, o=out, a=in0, b=in1, p=op: e.tensor_tensor(out=o, in0=a, in1=b, op=p), rd, wr,
                    cost=ecost(eng, out))

    def ts(eng, out, in0, s1, s2, op0, op1, rd=(), wr=()):
        return P.op(eng, lambda e, o=out, a=in0, x=s1, y=s2, p0=op0, p1=op1:
                    e.tensor_scalar(out=o, in0=a, scalar1=x, scalar2=y, op0=p0, op1=p1), rd, wr,
                    cost=ecost(eng, out))

    def stt(out, in0, scalar, in1, op0, op1, rd=(), wr=()):
        return P.op("dve", lambda e, o=out, a=in0, s=scalar, b=in1, p0=op0, p1=op1:
                    e.scalar_tensor_tensor(out=o, in0=a, scalar=s, in1=b, op0=p0, op1=p1), rd, wr,
                    cost=ecost("dve", out))

    def cp(eng, out, in_, rd=(), wr=()):
        if eng == "act":
            return act(out, in_, AF.Copy, rd, wr)
        return P.op(eng, lambda e, o=out, i=in_: e.tensor_copy(out=o, in_=i), rd, wr, cost=ecost(eng, out))

    def memset(eng, ap, val, rd=(), wr=()):
        return P.op(eng, lambda e, a=ap, v=val: e.memset(a, v), rd, wr, cost=ecost(eng, ap))

    def dma(eng, out, in_, rd=(), wr=(), sem=None):
        nb = fsz(out) * out.shape[0] * (2 if out.dtype == BF16 else 4)
        return P.dma(eng, lambda e, o=out, i=in_: e.dma_start(out=o, in_=i), rd, wr, sem=sem,
                     cost=2500.0 + nb / 150.0)

    def recip(out, in_, rd=(), wr=()):
        return P.op("dve", lambda e, o=out, i=in_: e.reciprocal(out=o, in_=i), rd, wr,
                    cost=120.0 + 8.0 * max(fsz(out), 8))

    def asel(out, in_, pattern, cmp, fill, base, cm, rd=(), wr=()):
        return P.op("pool", lambda e, o=out, i=in_, p=pattern, c=cmp, f=fill, b=base, m=cm:
                    e.affine_select(out=o, in_=i, pattern=p, compare_op=c, fill=f, base=b, channel_multiplier=m),
                    rd, wr)

    Rc = Res()
    ones_f = A.alloc([128], F32)
    zeros_bf = A.alloc([4, 128], BF16)
    ones_bf = A.alloc([128], BF16)
    ident_bf = A.alloc([128], BF16)
    ident_f = A.alloc([128], F32)
    tri_bf = A.alloc([128], BF16)
    tri_f = A.alloc([128], F32)
    maskneg = A.alloc([4, 128], BF16)
    lsel128 = A.alloc([128], F32)
    negones_f = A.alloc([128], F32)
    memset("pool", ones_f, 1.0, wr=[Rc])
    memset("pool", ones_bf, 1.0, wr=[Rc])
    memset("pool", zeros_bf, 0.0, wr=[Rc])
    asel(ident_bf, ones_bf, [[-1, 128]], ALU.is_equal, 0.0, 0, 1, rd=[Rc], wr=[Rc])
    asel(ident_f, ones_f, [[-1, 128]], ALU.is_equal, 0.0, 0, 1, rd=[Rc], wr=[Rc])
    asel(tri_bf, ones_bf, [[1, 128]], ALU.is_ge, 0.0, 0, -1, rd=[Rc], wr=[Rc])
    asel(tri_f, ones_f, [[1, 128]], ALU.is_ge, 0.0, 0, -1, rd=[Rc], wr=[Rc])
    asel(maskneg, zeros_bf, [[0, 4], [1, 128]], ALU.is_ge, -30000.0, 0, -1, rd=[Rc], wr=[Rc])
    asel(lsel128, ones_f, [[0, 128]], ALU.is_equal, 0.0, -127, 1, rd=[Rc], wr=[Rc])
    memset("pool", negones_f, -1.0, wr=[Rc])
    epsc = A.alloc([1], F32)
    memset("pool", epsc, EPS, wr=[Rc])

    gpre = A.alloc([KC], F32)
    snrm = A.alloc([16], F32)
    cw = A.alloc([24, 5], F32)
    fbneg = A.alloc([1], F32, parts=16)
    dtb_bc = A.alloc([32], F32)
    a_bc = A.alloc([32], F32)
    dsk_bc = A.alloc([32], F32)
    Rp = Res()
    m0 = A.mark()
    t_np = A.alloc([128], F32, parts=KC)
    t_sn = A.alloc([128], F32, parts=16)
    t_cw = A.alloc([3072], F32, parts=5)
    dma("sp", t_np, npre_d, wr=[Rp], sem=P.new_sem())
    dma("sp", t_sn, snrm_d, wr=[Rp], sem=P.new_sem())
    dma("sp", t_cw, convwb_d, wr=[Rp], sem=P.new_sem())
    dma("sp", fbneg, fgb_d, wr=[Rp], sem=P.new_sem())
    dma("sp", dtb_bc, dtb_d[0].partition_broadcast(128), wr=[Rp])
    dma("sp", a_bc, alog_d[0].partition_broadcast(128), wr=[Rp])
    dma("sp", dsk_bc, dsk_d[0].partition_broadcast(128), wr=[Rp])
    tr(pbank(0)[:, 0:KC], t_np, ident_f[0:KC, 0:KC], rd=[Rp, Rc], wr=[PR[0]])
    cp("dve", gpre, pbank(0)[:, 0:KC], rd=[PR[0]], wr=[Rp])
    tr(pbank(1)[:, 0:16], t_sn, ident_f[0:16, 0:16], rd=[Rp, Rc], wr=[PR[1]])
    cp("dve", snrm, pbank(1)[:, 0:16], rd=[PR[1]], wr=[Rp])
    for cg in range(24):
        tr(pbank(2)[:, cg * 5:cg * 5 + 5], t_cw[:, cg * 128:(cg + 1) * 128], ident_f[0:5, 0:5],
           rd=[Rp, Rc], wr=[PR[2]], sig=(cg == 23))
    cp("dve", cw, pbank(2)[:, 0:120].rearrange("p (a b) -> p a b", a=24), rd=[PR[2]], wr=[Rp])
    ts("dve", fbneg, fbneg, -1.0, None, ALU.mult, ALU.bypass, rd=[Rp], wr=[Rp])
    act(a_bc, a_bc, AF.Exp, rd=[Rp], wr=[Rp])
    ts("dve", a_bc, a_bc, -1.0, None, ALU.mult, ALU.bypass, rd=[Rp], wr=[Rp])
    P.barrier()
    A.reset(m0)

    if stop == "C":
        return finish()
    m_const = A.mark()
    uT = A.alloc([KC, TP], BF16)
    Ru = [Res() for _ in range(NB)]
    negc = A.alloc([NB, 16], F32)
    Rnegc = Res()

    def ru(t0, t1):
        return [Ru[b] for b in range(t0 // 128, (t1 - 1) // 128 + 1)]

    m_u = A.mark()

    stg = []
    Rstg = []
    stg_i = [0]

    def new_stg(width):
        stg.clear()
        Rstg.clear()
        for _ in range(2):
            stg.append(A.alloc([KC, width], F32))
            Rstg.append(Res())

    def load_w(src_v, c0, ncols, dst, dstR, scl):
        nk = dst.shape[1]
        SW = stg[0].shape[2]
        for c in range(0, ncols, SW):
            w = min(SW, ncols - c)
            for k0 in range(0, nk, KC):
                k1 = min(nk, k0 + KC)
                i = stg_i[0] % 2
                stg_i[0] += 1
                dma("sp", stg[i][:, 0:k1 - k0, 0:w], src_v[:, k0:k1, c0 + c:c0 + c + w], wr=[Rstg[i]])
                for k in range(k0, k1):
                    if scl is not None:
                        ts("pool", dst[:, k, c:c + w], stg[i][:, k - k0, 0:w], scl[:, k:k + 1], 1.0,
                           ALU.mult, ALU.mult, rd=[Rstg[i], Rp], wr=[dstR])
                    else:
                        cp("pool", dst[:, k, c:c + w], stg[i][:, k - k0, 0:w], rd=[Rstg[i]], wr=[dstR])

    P.begin_capture()
    Rwssd = [Res() for _ in range(10)]

    xt = [A.alloc([D], F32) for _ in range(2)]
    Rxt = [Res(), Res()]
    xn = [A.alloc([D], BF16) for _ in range(2)]
    Rxn = [Res(), Res()]
    junk2 = [A.alloc([D], BF16) for _ in range(2)]
    Rjunk2 = [Res(), Res()]
    ssq = [A.alloc([4], F32) for _ in range(2)]
    Rss = [Res(), Res()]

    def load_x_tile(m, buf, R):
        if m == 0:
            memset("pool", buf[0:NFILL, :], 0.0, wr=[R])
            dma("sp", buf[NFILL:128, :], meta_d, wr=[R])
        else:
            dma("sp", buf[:, :], x_d[128 * (m - 1):128 * m, :], wr=[R])
        return 128

    load_x_tile(0, xt[0], Rxt[0])
    for m in range(NB):
        i = m % 2
        t0 = 128 * m
        n = 128
        if m + 1 < NB:
            load_x_tile(m + 1, xt[1 - i], Rxt[1 - i])
        act(junk2[i][0:n, :], xt[i][0:n, :], AF.Square, rd=[Rxt[i]], wr=[Rss[i], Rjunk2[i]],
            accum=ssq[i][0:n, 0:1])
        act(ssq[i][0:n, 1:2], ssq[i][0:n, 0:1], AF.Sqrt, rd=[Rss[i]], wr=[Rss[i]], bias=EPS, scale=1.0 / D)
        recip(ssq[i][0:n, 2:3], ssq[i][0:n, 1:2], rd=[Rss[i]], wr=[Rss[i]])
        ts("dve", xn[i][0:n, :], xt[i][0:n, :], ssq[i][0:n, 2:3], None, ALU.mult, ALU.bypass,
           rd=[Rxt[i], Rss[i]], wr=[Rxn[i]])
        pb = pbank_bf(i).rearrange("p (a b) -> p a b", a=KC)
        for kc in range(KC):
            tr(pb[:, kc, 0:n], xn[i][0:n, kc * 128:(kc + 1) * 128], ident_bf[0:n, 0:n],
               rd=[Rxn[i], Rc], wr=[PR[i]], sig=(kc == KC - 1))
        cp("act" if m % 2 == 0 else "dve", uT[:, :, t0:t0 + n], pb[:, :, 0:n], rd=[PR[i]], wr=[Ru[m]])
    new_stg(256)
    wg = A.alloc([KC, 2048], BF16)
    Rwg = Res()
    gb_bc = A.alloc([2048], F32)
    Rgb = Res()
    dma("sp", gb_bc, gb_d[0].partition_broadcast(128), wr=[Rgb])
    load_w(win_v, C_G, 2048, wg, Rwg, gpre)
    gtmp = [A.alloc([512], F32) for _ in range(2)]
    Rgt = [Res(), Res()]
    gsb = [A.alloc([2048], BF16) for _ in range(2)]
    Rgs = [Res(), Res()]
    Rgates = [Res() for _ in range(NB)]
    k = 0
    for m in range(1, NB):
        t0 = 128 * m
        n = 128
        i = m % 2
        for qd in range(4):
            b = 4 + k % 4
            j = k % 2
            k += 1
            for kc in range(KC):
                mm(pbank(b)[0:n, :], uT[:, kc, t0:t0 + n], wg[:, kc, qd * 512:(qd + 1) * 512],
                   start=(kc == 0), stop=(kc == KC - 1), rd=[Rwg, Ru[m]], wr=[PR[b]], sig=(kc == KC - 1))
            tt("dve", gtmp[j][0:n, :], pbank(b)[0:n, :], gb_bc[0:n, qd * 512:(qd + 1) * 512], ALU.add,
               rd=[PR[b], Rgb], wr=[Rgt[j]])
            act(gsb[i][0:n, qd * 512:(qd + 1) * 512], gtmp[j][0:n, :], AF.Sigmoid, rd=[Rgt[j]], wr=[Rgs[i]])
        dma("sp", gates_d[t0:t0 + n, :], gsb[i][0:n, :], rd=[Rgs[i]], wr=[Rgates[m]])

    P.end_capture()
    P.barrier()
    A.reset(m_u)
    if dbg:
        dma("sp", dbg_d["uT"], uT, rd=Ru)
        dma("sp", dbg_d["gates"][128:T, :], gates_d[128:T, :], rd=Rgates)

    if stop == "P0":
        return finish()
    stg128 = [A.alloc_top([KC, 128], F32) for _ in range(2)]
    Rstg128 = [Res(), Res()]
    wv = A.alloc_top([KC, 512], BF16)
    Rwv = Res()
    wp2 = [A.alloc_top([KC, 384], BF16) for _ in range(2)]
    Rwp2 = [Res(), Res()]
    wtmp = [A.alloc_top([KC, 128], BF16) for _ in range(2)]
    Rwtmp = [Res(), Res()]

    def use_stg128():
        stg.clear()
        Rstg.clear()
        stg.extend(stg128)
        Rstg.extend(Rstg128)

    def load_pair_w(gp):
        wpb = wp2[gp % 2]
        load_w(win_v, C_Q + gp * 128, 128, wpb[:, :, 0:128], Rwp2[gp % 2], gpre)
        load_w(win_v, C_K + gp * 128, 128, wpb[:, :, 128:256], Rwp2[gp % 2], gpre)
        load_w(win_v, C_ZA + gp * 128, 128, wpb[:, :, 256:384], Rwp2[gp % 2], gpre)

    use_stg128()
    load_w(win_v, C_V, 512, wv, Rwv, gpre)
    load_pair_w(0)

    new_stg(256)
    ones16 = A.alloc([512], F32, parts=16)
    memset("pool", ones16, 1.0, wr=[Rc])
    wf = A.alloc([KC, 16], BF16)
    Rwf = Res()
    load_w(win_v, C_F, 16, wf, Rwf, gpre)
    lT = A.alloc([TP], F32, parts=16)
    cT = A.alloc([TP], F32, parts=16)
    c3 = A.alloc([3, TP], BF16, parts=16)
    ebuf = [A.alloc([512], F32, parts=16) for _ in range(2)]
    Reb = [Res(), Res()]
    RlT = Res()
    RcT = Res()
    Rc3 = Res()
    for I in range(NQT):
        t0 = 512 * I
        w = min(512, TP - t0)
        b = I % 2
        for kc in range(KC):
            mm(pbank(b)[0:16, 0:w], wf[:, kc, :], uT[:, kc, t0:t0 + w], start=(kc == 0), stop=(kc == KC - 1),
               rd=[Rwf] + ru(t0, t0 + w), wr=[PR[b]], sig=(kc == KC - 1))
        act(ebuf[b][:, 0:w], pbank(b)[0:16, 0:w], AF.Exp, rd=[PR[b], Rp], wr=[Reb[b]], bias=fbneg[:, 0:1], scale=-1.0)
        act(lT[:, t0:t0 + w], ebuf[b][:, 0:w], AF.Ln, rd=[Reb[b]], wr=[RlT], bias=1.0)
    memset("dve", lT[:, 0:NFILL], 0.0, rd=[RlT], wr=[RlT])
    for I in range(NQT):
        t0 = 512 * I
        w = min(512, TP - t0)
        init = 0.0 if I == 0 else cT[:, t0 - 1:t0]
        P.op("dve", lambda e, o=cT[:, t0:t0 + w], d0=ones16[:, 0:w], d1=lT[:, t0:t0 + w], ini=init:
             e.tensor_tensor_scan(out=o, data0=d0, data1=d1, initial=ini, op0=ALU.mult, op1=ALU.subtract),
             [RlT, RcT, Rc], [RcT])
    cp("dve", c3[:, 0, :], cT, rd=[RcT], wr=[Rc3])
    tt("dve", lT, cT, c3[:, 0, :], ALU.subtract, rd=[RcT, Rc3, RlT], wr=[RlT])
    cp("dve", c3[:, 1, :], lT, rd=[RlT], wr=[Rc3])
    nb_a = min(NB, 32)
    for m in range(NB):
        bank = 2 if m < 32 else 3
        col = (m % 32) * 16
        tr(pbank(bank)[:, col:col + 16], cT[:, 128 * m:128 * m + 128], ident_f[0:16, 0:16],
           rd=[RcT, Rc], wr=[PR[bank]], sig=(m == nb_a - 1 or m == NB - 1))
    ts("dve", negc[:, 0:nb_a, :], pbank(2)[:, 0:nb_a * 16].rearrange("p (a b) -> p a b", b=16), -1.0, None,
       ALU.mult, ALU.bypass, rd=[PR[2]], wr=[Rnegc])
    if NB > 32:
        ts("dve", negc[:, 32:NB, :], pbank(3)[:, 0:(NB - 32) * 16].rearrange("p (a b) -> p a b", b=16), -1.0, None,
           ALU.mult, ALU.bypass, rd=[PR[3]], wr=[Rnegc])
    tt("dve", cT, lT, c3[:, 1, :], ALU.subtract, rd=[RlT, Rc3, RcT, PR[2], PR[3]], wr=[RcT])
    cp("dve", c3[:, 2, :], cT, rd=[RcT], wr=[Rc3])
    Rc3d = Res()
    dma("sp", c3_d, c3, rd=[Rc3], wr=[Rc3d])
    if dbg:
        dma("sp", dbg_d["c3"], c3, rd=[Rc3])
    P.barrier()
    A.reset(m_u)

    if stop == "P1a":
        return finish()
    if stop == "P1b":
        return finish()
    use_stg128()
    Vaug = A.alloc([NB, 8, 65], BF16)
    Rv = [Res() for _ in range(NB)]
    Rvone = Res()
    QT = [A.alloc([TP], BF16, parts=67) for _ in range(2)]
    KT = [A.alloc([TP], BF16, parts=67) for _ in range(2)]
    zs_off = A.mark()
    ZS = [A.alloc([TP], BF16, parts=64) for _ in range(2)]
    Rq = [[Res() for _ in range(NQT)] for _ in range(2)]
    Rk = [[Res() for _ in range(NQT)] for _ in range(2)]
    Rz = [[Res() for _ in range(NQT)] for _ in range(2)]
    Rqc = [Res(), Res()]
    qc_sem = [P.new_sem(), P.new_sem()]
    Rkone = Res()
    NPT = 5
    PTb = [A.alloc([512], BF16) for _ in range(NPT)]
    Rpt = [Res() for _ in range(NPT)]
    rsb = [arena_t[0:65, (zs_off + 2048 * i) // 2:(zs_off + 2048 * i) // 2 + 1024].bitcast(F32) for i in range(2)]
    Rrs = [Res(), Res()]
    etmp = [A.alloc([512], F32, parts=64) for _ in range(2)]
    Ret = [Res(), Res()]
    yT = [A.alloc([512], BF16, parts=64) for _ in range(2)]
    RyT = [Res(), Res()]
    Ryatt = [Res() for _ in range(NQT)]

    memset("pool", Vaug[:, :, :, 64:65], 1.0, wr=[Rvone])
    for s in range(2):
        memset("pool", KT[s][64:67, :], 1.0, wr=[Rkone])
    st_i = 0
    pt_i = 0
    ep_i = 0
    ot_i = 0
    pw_i = [0]

    def pw_slice(npieces):
        for _ in range(npieces):
            k_ = pw_i[0]
            if k_ >= 40:
                return
            pw_i[0] += 1
            g, c = k_ // 4, (k_ % 4) * 128
            i = k_ % 2
            load_w(win_v, C_Z + g * 512 + c, 128, wtmp[i], Rwtmp[i], gpre)
            dma("sp", wssd_d[g].rearrange("p (a b) -> p a b", a=KC)[:, :, c:c + 128], wtmp[i], rd=[Rwtmp[i]],
                wr=[Rwssd[g]])

    for half in range(2):
        for m in range(NB):
            b = m % 2
            for kc in range(KC):
                mm(pbank(b), uT[:, kc, 128 * m:128 * m + 128], wv[:, kc, :], start=(kc == 0), stop=(kc == KC - 1),
                   rd=[Rwv, Ru[m]], wr=[PR[b]], sig=(kc == KC - 1))
            cp("dve" if m % 2 == 0 else "act", Vaug[:, m, :, 0:64], pbank(b).rearrange("p (a b) -> p a b", a=8),
               rd=[PR[b]], wr=[Rv[m]])
            if m == 0:
                memset("pool", Vaug[0:NFILL, 0, :, :], 0.0, rd=[Rv[0], Rvone], wr=[Rv[0], Rvone])
        if half == 0:
            load_w(win_v, C_V + 512, 512, wv, Rwv, gpre)
        for hp in range(4):
            gp = half * 4 + hp
            wp = wp2[gp % 2]
            Rwp = Rwp2[gp % 2]
            for s in range(2):
                dma("sp", QT[s][64:67, :], c3_d[2 * gp + s], rd=[Rc3d], wr=[Rqc[s]], sem=qc_sem[s])
            for I in range(NQT):
                t0 = 512 * I
                w = min(512, TP - t0)
                for (ci, dst, RR, fn, scale) in ((0, QT, Rq, AF.Copy, 0.125), (1, KT, Rk, AF.Copy, None),
                                                 (2, ZS, Rz, AF.Silu, None)):
                    b = (I * 3 + ci) % 2
                    for kc in range(KC):
                        mm(pbank(b)[:, 0:w], wp[:, kc, ci * 128:(ci + 1) * 128], uT[:, kc, t0:t0 + w],
                           start=(kc == 0), stop=(kc == KC - 1), rd=[Rwp] + ru(t0, t0 + w), wr=[PR[b]],
                           sig=(kc == KC - 1))
                    if fn == AF.Silu:
                        act(dst[0][0:64, t0:t0 + w], pbank(b)[0:64, 0:w], fn, rd=[PR[b]], wr=[RR[0][I]])
                    else:
                        ts("dve", dst[0][0:64, t0:t0 + w], pbank(b)[0:64, 0:w], scale if scale else 1.0, None,
                           ALU.mult, ALU.bypass, rd=[PR[b]], wr=[RR[0][I]])
                    act(dst[1][0:64, t0:t0 + w], pbank(b)[64:128, 0:w], fn, rd=[PR[b]], wr=[RR[1][I]], scale=scale)
            if gp + 1 < 8:
                load_pair_w(gp + 1)
            pw_slice(5)
            for s in range(2):
                h = 2 * gp + s
                hl = h % 8
                steps = []
                for I in range(NQT):
                    t0 = 512 * I
                    w = min(512, TP - t0)
                    jmax = (t0 + w) // 128 - 1
                    for j in range(jmax + 1):
                        steps.append((I, j, t0, w, jmax))
                LA = 4
                infl = {}
                obs = {}
                deferred = []

                def emit_st(k):
                    nonlocal st_i, pt_i
                    I, j, t0, w, jmax = steps[k]
                    r = j - 4 * I
                    qlo = 128 * r if r >= 0 else 0
                    N = w - qlo
                    sb_ = st_i % 5
                    st_i += 1
                    pi = pt_i % NPT
                    pt_i += 1
                    mm(pbank(sb_)[:, 0:N], KT[s][0:67, 128 * j:128 * j + 128], QT[s][0:67, t0 + qlo:t0 + w],
                       rd=[Rk[s][j // 4], Rkone, Rq[s][I], Rqc[s]], wr=[PR[sb_]])
                    act(PTb[pi][:, 0:N], pbank(sb_)[:, 0:N], AF.Exp, rd=[PR[sb_], Rnegc], wr=[Rpt[pi]],
                        bias=negc[:, j, h:h + 1])
                    if r >= 0:
                        tt("pool", PTb[pi][:, 0:128], PTb[pi][:, 0:128], tri_bf, ALU.mult,
                           rd=[Rpt[pi], Rc], wr=[Rpt[pi]])
                    infl[k] = (pi, qlo, N)

                def emit_pv(k):
                    nonlocal ot_i, ep_i
                    I, j, t0, w, jmax = steps[k]
                    pi, qlo, N = infl.pop(k)
                    if j == 0:
                        obs[I] = 5 + (ot_i % 2)
                        ot_i += 1
                    ob = obs[I]
                    mm(pbank(ob)[0:65, qlo:w], Vaug[:, j, hl, :], PTb[pi][:, 0:N], start=(j == 0),
                       stop=(j == jmax), rd=[Rv[j], Rvone, Rpt[pi]], wr=[PR[ob]], sig=(j == jmax))
                    if j != jmax:
                        return
                    e_ = ep_i % 2
                    ep_i += 1
                    if I == 0:
                        ts("dve", rsb[e_][64:65, 0:w], pbank(ob)[64:65, 0:w], 1e-30, None, ALU.add, ALU.bypass,
                           rd=[PR[ob]], wr=[Rrs[e_]])
                        recip(rsb[e_][64:65, 0:w], rsb[e_][64:65, 0:w], rd=[Rrs[e_]], wr=[Rrs[e_]])
                    else:
                        recip(rsb[e_][64:65, 0:w], pbank(ob)[64:65, 0:w], rd=[PR[ob]], wr=[Rrs[e_]])
                    tt("dve", etmp[e_][:, 0:w], pbank(ob)[0:64, 0:w], ZS[s][:, t0:t0 + w], ALU.mult,
                       rd=[PR[ob], Rz[s][I]], wr=[Ret[e_]])

                    def part2(e_=e_, w=w, t0=t0, I=I):
                        mm(pbank(7)[0:64, 0:w], ones_f[64:65, 0:64], rsb[e_][64:65, 0:w], rd=[Rrs[e_], Rc],
                           wr=[PR[7]])
                        tt("dve", yT[e_][:, 0:w], etmp[e_][:, 0:w], pbank(7)[0:64, 0:w], ALU.mult,
                           rd=[Ret[e_], PR[7]], wr=[RyT[e_]])
                        nt = w // 128
                        dma("sp", yatt_d[4 * I:4 * I + nt, s * 64:(s + 1) * 64, gp, :].rearrange("m p t -> p m t"),
                            yT[e_][:, 0:w].rearrange("p (m t) -> p m t", t=128), rd=[RyT[e_]], wr=[Ryatt[I]])
                    deferred.append((k + LA + 10, part2))

                k = 0
                while k < len(steps) + LA or deferred:
                    if k < len(steps):
                        emit_st(k)
                    if 0 <= k - LA < len(steps):
                        emit_pv(k - LA)
                    while deferred and (deferred[0][0] <= k or k >= len(steps) + LA):
                        deferred.pop(0)[1]()
                    k += 1
    P.barrier()
    A.reset(m_u)
    if dbg:
        dma("sp", dbg_d["yatt"], yatt_d, rd=Ryatt)

    if stop == "P2":
        return finish()
    A.reset(m_const)
    uT_ = A.alloc([KC, TP], BF16)
    wdt = A.alloc([KC, 32], BF16)
    Rwdt = Res()
    m3 = A.mark()
    new_stg(32)
    load_w(win_v, C_DT, 32, wdt, Rwdt, gpre)
    P.barrier()
    A.reset(m3)
    wring = [A.alloc([KC, 512], BF16) for _ in range(2)]
    Rwr = [Res(), Res()]
    wr_i = [0]
    xbcT = A.alloc([24, 512], BF16)
    Rxbc = [Res() for _ in range(24)]
    convb = [A.alloc([516], F32) for _ in range(2)]
    Rcb = [Res(), Res()]
    cacc = [A.alloc([512], F32) for _ in range(2)]
    Rca = [Res(), Res()]
    halo = A.alloc([24, 4], F32)
    Rhalo = [Res() for _ in range(24)]
    hst = A.alloc([2048], F32)
    hbf = A.alloc([2048], BF16)
    Rhst = [Res() for _ in range(4)]
    Rhbf = [Res() for _ in range(4)]
    sz = [A.alloc([2048], BF16) for _ in range(4)]
    Rsz = [[Res() for _ in range(4)] for _ in range(4)]
    xsD = A.alloc([2048], BF16)
    RxsD = Res()
    xdt = A.alloc([2048], BF16)
    Rxdt = Res()
    xdtS = [A.alloc([2048], BF16) for _ in range(2)]
    RxdtS = [Res(), Res()]
    Btok = [A.alloc([512], BF16) for _ in range(2)]
    RBt = [Res(), Res()]
    sm2 = [A.alloc([10, 32], F32) for _ in range(2)]
    Rsm2 = [[Res() for _ in range(10)] for _ in range(2)]
    DTX, EDT, DT, DTA, ACS, NACS, EACS, DD, DS, CD = range(10)
    CBm = A.alloc([4, 128], BF16)
    RCBm = Res()
    Dg = [A.alloc([4, 128], F32) for _ in range(2)]
    RDg = [Res(), Res()]
    decT = [A.alloc([4, 128], BF16) for _ in range(2)]
    Rdec = [Res(), Res()]
    MT = [A.alloc([4, 128], BF16) for _ in range(2)]
    RMT = [Res(), Res()]
    NYD = 5
    ydg = [A.alloc([512], F32) for _ in range(NYD)]
    Rydg = [Res() for _ in range(NYD)]
    yg = [A.alloc([512], F32) for _ in range(2)]
    Ryg = [Res(), Res()]
    sqj2 = [A.alloc([512], BF16) for _ in range(2)]
    Rsqj2 = [Res(), Res()]
    gst = [A.alloc([4], F32) for _ in range(2)]
    Rgst = [Res(), Res()]
    yn = A.alloc([2048], BF16)
    Ryn = [Res() for _ in range(4)]
    ynT = A.alloc([16, 128], BF16)
    RynT = Res()
    Ryssd = [Res() for _ in range(NCH)]
    memset("pool", halo, 0.0, wr=Rhalo)

    psA_i = [0]
    psB_i = [0]

    def psA():
        b = psA_i[0] % 4
        psA_i[0] += 1
        return b

    def psB():
        b = 4 + psB_i[0] % 4
        psB_i[0] += 1
        return b

    def get_w(g):
        i = wr_i[0] % 2
        wr_i[0] += 1
        dma("sp", wring[i].rearrange("p a b -> p (a b)"), wssd_d[g], rd=[Rwssd[g]], wr=[Rwr[i]])
        return i

    chunks = [(128 * c, 128) for c in range(NCH)]
    tiles = [list(range(k, min(k + 4, NCH))) for k in range(0, NCH, 4)]
    cnt = {"cv": 0, "yd": 0, "q": 0, "g": 0}

    def z_pass(tl, ps):
        for g in range(4):
            wi = get_w(g)
            for ci, c in enumerate(tl):
                tok0, Lc = chunks[c]
                b = ps()
                for kc in range(KC):
                    mm(pbank(b), uT[:, kc, tok0:tok0 + Lc], wring[wi][:, kc, :], start=(kc == 0),
                       stop=(kc == KC - 1), rd=[Rwr[wi]] + ru(tok0, tok0 + Lc), wr=[PR[b]], sig=(kc == KC - 1))
                act(sz[ci][:, g * 512:(g + 1) * 512], pbank(b), AF.Silu, rd=[PR[b]], wr=[Rsz[ci][g]])
                yield

    def conv_pass(tl, ps):
        ts_ = chunks[tl[0]][0]
        te_ = chunks[tl[-1]][0] + 128
        Wk = te_ - ts_
        for wgi in range(6):
            wi = get_w(4 + wgi)
            for cgl in range(4):
                cg = wgi * 4 + cgl
                b = ps()
                for kc in range(KC):
                    mm(pbank(b)[:, 0:Wk], wring[wi][:, kc, cgl * 128:(cgl + 1) * 128], uT[:, kc, ts_:te_],
                       start=(kc == 0), stop=(kc == KC - 1), rd=[Rwr[wi]] + ru(ts_, te_), wr=[PR[b]],
                       sig=(kc == KC - 1))
                v = cnt["cv"] % 2
                cnt["cv"] += 1
                cp("act", convb[v][:, 3:3 + Wk], pbank(b)[:, 0:Wk], rd=[PR[b]], wr=[Rcb[v]])
                cp("pool", convb[v][:, 0:3], halo[:, cg, 0:3], rd=[Rhalo[cg]], wr=[Rcb[v]])
                cp("pool", halo[:, cg, 0:3], convb[v][:, Wk:Wk + 3], rd=[Rcb[v]], wr=[Rhalo[cg]])
                act(cacc[v][:, 0:Wk], pbank(b)[:, 0:Wk], AF.Identity, rd=[PR[b], Rp], wr=[Rca[v]],
                    bias=cw[:, cg, 4:5], scale=cw[:, cg, 3:4])
                for kk in range(0, 3):
                    stt(cacc[v][:, 0:Wk], convb[v][:, kk:kk + Wk], cw[:, cg, kk:kk + 1], cacc[v][:, 0:Wk],
                        ALU.mult, ALU.add, rd=[Rcb[v], Rca[v], Rp], wr=[Rca[v]])
                act(xbcT[:, cg, 0:Wk], cacc[v][:, 0:Wk], AF.Silu, rd=[Rca[v]], wr=[Rxbc[cg]])
                if tl[0] == 0:
                    memset("pool", xbcT[:, cg, 0:NFILL], 0.0, rd=[Rxbc[cg]], wr=[Rxbc[cg]])
                yield

    def stageA(c, ts_):
        tok0, Lc = chunks[c]
        off = tok0 - ts_
        first = (c == 0)
        p = c % 2
        sm = sm2[p]
        Rsm = Rsm2[p]
        b = psA()
        for kc in range(KC):
            mm(pbank(b)[:, 0:32], uT[:, kc, tok0:tok0 + Lc], wdt[:, kc, :], start=(kc == 0),
               stop=(kc == KC - 1), rd=[Rwdt] + ru(tok0, tok0 + Lc), wr=[PR[b]], sig=(kc == KC - 1))
        tt("dve", sm[:, DTX, :], pbank(b)[:, 0:32], dtb_bc, ALU.add, rd=[PR[b], Rp], wr=[Rsm[DTX]])
        act(sm[:, EDT, :], sm[:, DTX, :], AF.Exp, rd=[Rsm[DTX]], wr=[Rsm[EDT]])
        act(sm[:, DT, :], sm[:, EDT, :], AF.Ln, rd=[Rsm[EDT]], wr=[Rsm[DT]], bias=1.0)
        if first:
            memset("dve", sm[0:NFILL, DT, :], 0.0, rd=[Rsm[DT]], wr=[Rsm[DT]])
        tt("dve", sm[:, DTA, :], sm[:, DT, :], a_bc, ALU.mult, rd=[Rsm[DT], Rp], wr=[Rsm[DTA]])
        yield
        for hb in range(2):
            b = psA()
            pv = pbank_bf(b).rearrange("p (a b) -> p a b", a=8)
            for k8 in range(8):
                cg = hb * 8 + k8
                tr(pv[:, k8, :], xbcT[:, cg, off:off + Lc], ident_bf, rd=[Rxbc[cg], Rc], wr=[PR[b]], sig=(k8 == 7))
            tt("dve", xdt[:, hb * 1024:(hb + 1) * 1024].rearrange("p (a b) -> p a b", a=16),
               pbank_bf(b).rearrange("p (a b) -> p a b", a=16),
               bc_last(sm[:, DT, hb * 16:(hb + 1) * 16], 64), ALU.mult, rd=[PR[b], Rsm[DT]], wr=[Rxdt])
            cp("act", xsD[:, hb * 1024:(hb + 1) * 1024], pbank_bf(b), rd=[PR[b]], wr=[RxsD])
            tt("pool", xsD[:, hb * 1024:(hb + 1) * 1024].rearrange("p (a b) -> p a b", a=16),
               xsD[:, hb * 1024:(hb + 1) * 1024].rearrange("p (a b) -> p a b", a=16),
               bc_last(dsk_bc[:, hb * 16:(hb + 1) * 16], 64), ALU.mult, rd=[RxsD, Rp], wr=[RxsD])
            yield
        b = psA()
        mm(pbank(b)[:, 0:32], tri_f, sm[:, DTA, :], rd=[Rsm[DTA], Rc], wr=[PR[b]])
        cp("act", sm[:, ACS, :], pbank(b)[:, 0:32], rd=[PR[b]], wr=[Rsm[ACS]])
        act(sm[:, EACS, :], pbank(b)[:, 0:32], AF.Exp, rd=[PR[b]], wr=[Rsm[EACS]])
        ts("dve", sm[:, NACS, :], sm[:, ACS, :], -1.0, None, ALU.mult, ALU.bypass, rd=[Rsm[ACS]], wr=[Rsm[NACS]])
        b = psA()
        mm(pbank(b)[:, 0:32], lsel128, sm[:, ACS, :], rd=[Rsm[ACS], Rc], wr=[PR[b]])
        tt("dve", sm[:, DD, :], pbank(b)[:, 0:32], sm[:, ACS, :], ALU.subtract, rd=[PR[b], Rsm[ACS]], wr=[Rsm[DD]])
        act(sm[:, CD, :], pbank(b)[:, 0:32], AF.Exp, rd=[PR[b]], wr=[Rsm[CD]])
        act(sm[:, DS, :], sm[:, DD, :], AF.Exp, rd=[Rsm[DD]], wr=[Rsm[DS]])
        yield
        b = psA()
        pv = pbank_bf(b).rearrange("p (a b) -> p a b", a=8)
        for k4 in range(4):
            tr(pv[:, k4, :], xbcT[:, 16 + k4, off:off + Lc], ident_bf, rd=[Rxbc[16 + k4], Rc], wr=[PR[b]],
               sig=(k4 == 3))
        cp("act", Btok[p], pbank_bf(b)[:, 0:512], rd=[PR[b]], wr=[RBt[p]])
        tt("pool", xdtS[p].rearrange("p (a b) -> p a b", a=32), xdt.rearrange("p (a b) -> p a b", a=32),
           bc_last(sm[:, DS, :], 64), ALU.mult, rd=[Rxdt, Rsm[DS]], wr=[RxdtS[p]])
        b = psA()
        pv = pbank(b).rearrange("p (a b) -> p a b", a=4)
        for g in range(4):
            mm(pv[:, g, :], xbcT[:, 16 + g, off:off + Lc], xbcT[:, 20 + g, off:off + Lc],
               rd=[Rxbc[16 + g], Rxbc[20 + g]], wr=[PR[b]], sig=(g == 3))
        tt("dve", CBm, pv, bc_mid(tri_bf, 4), ALU.mult, rd=[PR[b], Rc], wr=[RCBm])
        yield
        yslots = []
        for g in range(4):
            byd = psA()
            for qq in range(2):
                qd = 2 * g + qq
                qi = cnt["q"] % 2
                cnt["q"] += 1
                tt("pool", Dg[qi], bc_mid(ident_f, 4), bc_last(sm[:, ACS, 4 * qd:4 * qd + 4], 128), ALU.mult,
                   rd=[Rsm[ACS], Rc], wr=[RDg[qi]])
                be = psA()
                pe_ = pbank(be).rearrange("p (a b) -> p a b", a=4)
                mm(pbank(be), ident_bf, maskneg.rearrange("p a b -> p (a b)"), start=True, stop=False,
                   rd=[Rc], wr=[PR[be]], sig=False)
                mm(pbank(be), ones_f, Dg[qi].rearrange("p a b -> p (a b)"), start=False, stop=True,
                   rd=[RDg[qi], Rc], wr=[PR[be]])
                for hh in range(4):
                    h = 4 * qd + hh
                    act(decT[qi][:, hh, :], pe_[:, hh, :], AF.Exp, rd=[PR[be], Rsm[NACS]], wr=[Rdec[qi]],
                        bias=sm[:, NACS, h:h + 1])
                tt("dve", MT[qi], decT[qi], bc_mid(CBm[:, g, :], 4), ALU.mult, rd=[Rdec[qi], RCBm], wr=[RMT[qi]])
                for hh in range(4):
                    h = 4 * qd + hh
                    hl = h % 8
                    mm(pbank(byd)[:, hl * 64:(hl + 1) * 64], MT[qi][:, hh, :], xdt[:, h * 64:(h + 1) * 64],
                       start=True, stop=False, rd=[RMT[qi], Rxdt], wr=[PR[byd]], sig=False)
                    mm(pbank(byd)[:, hl * 64:(hl + 1) * 64], ident_bf, xsD[:, h * 64:(h + 1) * 64],
                       start=False, stop=True, rd=[RxsD, Rc], wr=[PR[byd]], sig=(qq == 1 and hh == 3))
                yield
            ys = cnt["yd"] % NYD
            cnt["yd"] += 1
            cp("act", ydg[ys], pbank(byd), rd=[PR[byd]], wr=[Rydg[ys]])
            yslots.append(ys)
            yield
        stA_out[c] = yslots

    stA_out = {}

    def stageB(c, ts_, ci):
        tok0, Lc = chunks[c]
        off = tok0 - ts_
        first = (c == 0)
        p = c % 2
        sm = sm2[p]
        Rsm = Rsm2[p]
        yslots = stA_out[c]
        for g in range(4):
            gi = cnt["g"] % 2
            cnt["g"] += 1
            gs = slice(g * 512, (g + 1) * 512)
            ys = yslots[g]
            if not first:
                byo = psB()
                mm(pbank(byo), xbcT[:, 20 + g, off:off + Lc], hbf[:, gs], rd=[Rxbc[20 + g], Rhbf[g]], wr=[PR[byo]])
                for hl in range(8):
                    h = 8 * g + hl
                    stt(yg[gi][:, hl * 64:(hl + 1) * 64], pbank(byo)[:, hl * 64:(hl + 1) * 64],
                        sm[:, EACS, h:h + 1], ydg[ys][:, hl * 64:(hl + 1) * 64], ALU.mult, ALU.add,
                        rd=[PR[byo], Rsm[EACS], Rydg[ys]], wr=[Ryg[gi]])
                tt("dve", yg[gi], yg[gi], sz[ci][:, gs], ALU.mult, rd=[Ryg[gi], Rsz[ci][g]], wr=[Ryg[gi]])
            else:
                tt("dve", yg[gi], ydg[ys], sz[ci][:, gs], ALU.mult, rd=[Rydg[ys], Rsz[ci][g]], wr=[Ryg[gi]])
            act(sqj2[gi], yg[gi], AF.Square, rd=[Ryg[gi]], wr=[Rgst[gi], Rsqj2[gi]], accum=gst[gi][:, 0:1])
            act(gst[gi][:, 1:2], gst[gi][:, 0:1], AF.Ln, rd=[Rgst[gi]], wr=[Rgst[gi]], bias=epsc[:, 0:1],
                scale=1.0 / 512)
            act(gst[gi][:, 2:3], gst[gi][:, 1:2], AF.Exp, rd=[Rgst[gi]], wr=[Rgst[gi]], scale=-0.5)
            ts("dve", yn[:, gs], yg[gi], gst[gi][:, 2:3], None, ALU.mult, ALU.bypass, rd=[Ryg[gi], Rgst[gi]],
               wr=[Ryn[g]])
            yield
            bsn = psB()
            mm(pbank(bsn), Btok[p][:, g * 128:(g + 1) * 128], xdtS[p][:, gs], rd=[RBt[p], RxdtS[p]], wr=[PR[bsn]])
            if first:
                cp("dve", hst[:, gs], pbank(bsn), rd=[PR[bsn]], wr=[Rhst[g]])
            else:
                tt("pool", hst[:, gs].rearrange("p (a b) -> p a b", a=8),
                   hst[:, gs].rearrange("p (a b) -> p a b", a=8), bc_last(sm[:, CD, 8 * g:8 * g + 8], 64),
                   ALU.mult, rd=[Rhst[g], Rsm[CD]], wr=[Rhst[g]])
                tt("dve", hst[:, gs], hst[:, gs], pbank(bsn), ALU.add, rd=[Rhst[g], PR[bsn]], wr=[Rhst[g]])
            cp("pool", hbf[:, gs], hst[:, gs], rd=[Rhst[g]], wr=[Rhbf[g]])
            yield
        for hb in range(2):
            b = psB()
            pv = pbank_bf(b).rearrange("p (a b) -> p a b", a=8)
            for k8 in range(8):
                cg = hb * 8 + k8
                tr(pv[:, k8, :], yn[:, cg * 128:(cg + 1) * 128], ident_bf, rd=[Ryn[cg // 4], Rc], wr=[PR[b]],
                   sig=(k8 == 7))
            cp("act" if hb == 0 else "dve", ynT[:, hb * 8:(hb + 1) * 8, :], pv, rd=[PR[b]], wr=[RynT])
            yield
        dma("sp", yssd_d[c], ynT, rd=[RynT], wr=[Ryssd[c]])

    def interleave(*gens):
        P.begin_capture()
        for g in gens:
            if g is not None:
                for _ in g:
                    pass
        P.end_capture(keep=KEEP[0])

    seq = []
    for ti, tl in enumerate(tiles):
        ts_ = chunks[tl[0]][0]
        if ti == 0:
            seq.append(conv_pass(tl, psA))
            seq.append(z_pass(tl, psB))
            seq.append(stageA(tl[0], ts_))
            seq.append("cut")
        for ci, c in enumerate(tl):
            seq.append(stageB(c, ts_, ci))
            if ci + 1 < len(tl):
                seq.append(stageA(c + 1, ts_))
            elif ti + 1 < len(tiles):
                ntl = tiles[ti + 1]
                seq.append(conv_pass(ntl, psA))
                seq.append(z_pass(ntl, psB))
                seq.append(stageA(ntl[0], chunks[ntl[0]][0]))
        seq.append("cut")
    KEEP = [P3_KEEP]
    region = []
    ncut = sum(1 for g in seq if g == "cut")
    icut = 0
    for g in seq:
        if g == "cut":
            icut += 1
            if icut == ncut:
                KEEP[0] = 1.0
            interleave(*region)
            region = []
        else:
            region.append(g)
    assert not region and not P.carry
    P.barrier()
    A.reset(m_const)
    if dbg:
        dma("sp", dbg_d["yssd"], yssd_d, rd=Ryssd)
    if stop == "P3":
        return finish()
    new_stg(256)
    wps = A.alloc([16, D], BF16)
    wpa = A.alloc([KC, D], BF16)
    wo = A.alloc([KC, D], BF16)
    Rw4 = Res()
    npost_bc = A.alloc([D], F32)
    dma("sp", npost_bc, npost_d[0].partition_broadcast(128), wr=[Rw4])
    wps_v = wps_d.rearrange("(kc p) n -> p kc n", p=128)
    wpa_v = wpa_d.rearrange("(kc p) n -> p kc n", p=128)
    wo_v = wo_d.rearrange("(kc p) n -> p kc n", p=128)
    st4 = [A.alloc([2, D], F32) for _ in range(2)]
    Rst4 = [Res(), Res()]
    for k0 in range(0, 16, 2):
        i_ = (k0 // 2) % 2
        dma("sp", st4[i_], wps_v[:, k0:k0 + 2, :], wr=[Rst4[i_]])
        for kk in range(2):
            kc = k0 + kk
            if kk == 0:
                ts("dve", wps[:, kc, :], st4[i_][:, kk, :], snrm[:, kc:kc + 1], None, ALU.mult, ALU.bypass,
                   rd=[Rst4[i_], Rp], wr=[Rw4])
            else:
                act(wps[:, kc, :], st4[i_][:, kk, :], AF.Copy, rd=[Rst4[i_], Rp], wr=[Rw4], scale=snrm[:, kc:kc + 1])
    for k0 in range(0, KC, 4):
        dma("pool", wpa[:, k0:k0 + 4, :], wpa_v[:, k0:k0 + 4, :], wr=[Rw4], sem=P.new_sem())
        dma("pool", wo[:, k0:k0 + 4, :], wo_v[:, k0:k0 + 4, :], wr=[Rw4], sem=P.new_sem())
    ysT = [A.alloc([16, 128], BF16) for _ in range(2)]
    yaT = [A.alloc([KC, 128], BF16) for _ in range(2)]
    gt = [A.alloc([2048], BF16) for _ in range(2)]
    xr = [A.alloc([D], F32) for _ in range(2)]
    Rin = [Res(), Res()]
    t1 = [A.alloc([D], F32) for _ in range(2)]
    Rt1 = [Res(), Res()]
    t2 = [A.alloc([D], F32) for _ in range(2)]
    Rt2 = [Res(), Res()]
    mg = [A.alloc([D], BF16) for _ in range(2)]
    Rmg = [Res(), Res()]
    mgT = [A.alloc([KC, 128], BF16) for _ in range(2)]
    RmgT = [Res(), Res()]
    ob_ = [A.alloc([D], F32) for _ in range(2)]
    Rob = [Res(), Res()]
    pst = [A.alloc([4], F32) for _ in range(2)]
    Rpst = [Res(), Res()]

    def load_tile4(m, i):
        t0 = 128 * m
        n = 128
        dma("sp", ysT[i], yssd_d[m], rd=[Ryssd[m]], wr=[Rin[i]])
        dma("sp", yaT[i], yatt_d[m], rd=[Ryatt[m // 4]], wr=[Rin[i]])
        dma("sp", gt[i][0:n, :], gates_d[t0:t0 + n, :], rd=[Rgates[m]], wr=[Rin[i]])
        load_x_tile(m, xr[i], Rin[i])

    out_toks = []
    load_tile4(1, 1)
    for m in range(1, NB):
        if (m - 1) % 4 == 0:
            if P.cap is not None:
                P.end_capture(keep=P4_KEEP)
            P.begin_capture()
        i = m % 2
        t0 = 128 * m
        n = 128
        if m + 1 < NB:
            load_tile4(m + 1, 1 - i)
        ck("p4_load")
        for hf in range(2):
            for kc in range(16):
                mm(pbank(hf)[0:n, :], ysT[i][:, kc, 0:n], wps[:, kc, hf * 512:(hf + 1) * 512], start=(kc == 0),
                   stop=(kc == 15), rd=[Rin[i], Rw4], wr=[PR[hf]], sig=(kc == 15))
        for hf in range(2):
            for kc in range(KC):
                mm(pbank(2 + hf)[0:n, :], yaT[i][:, kc, 0:n], wpa[:, kc, hf * 512:(hf + 1) * 512], start=(kc == 0),
                   stop=(kc == KC - 1), rd=[Rin[i], Rw4], wr=[PR[2 + hf]], sig=(kc == KC - 1))
        ck("p4_ab")
        for hf in range(2):
            hs = slice(hf * 512, (hf + 1) * 512)
            tt("dve", t1[i][:, hs], pbank(hf), gt[i][:, hs], ALU.mult, rd=[PR[hf], Rin[i]], wr=[Rt1[i]])
            tt("dve", t2[i][:, hs], pbank(2 + hf), gt[i][:, D + hf * 512:D + (hf + 1) * 512], ALU.mult,
               rd=[PR[2 + hf], Rin[i]], wr=[Rt2[i]])
        tt("dve", mg[i][0:n, :], t1[i][0:n, :], t2[i][0:n, :], ALU.add, rd=[Rt1[i], Rt2[i]], wr=[Rmg[i]])
        ck("p4_merge")
        pv = pbank_bf(4 + i).rearrange("p (a b) -> p a b", a=KC)
        for kc in range(KC):
            tr(pv[:, kc, 0:n], mg[i][0:n, kc * 128:(kc + 1) * 128], ident_bf[0:n, 0:n], rd=[Rmg[i], Rc],
               wr=[PR[4 + i]], sig=(kc == KC - 1))
        cp("act", mgT[i][:, :, 0:n], pv[:, :, 0:n], rd=[PR[4 + i]], wr=[RmgT[i]])
        for hf in range(2):
            for kc in range(KC):
                mm(pbank(6 + hf)[0:n, :], mgT[i][:, kc, 0:n], wo[:, kc, hf * 512:(hf + 1) * 512], start=(kc == 0),
                   stop=(kc == KC - 1), rd=[RmgT[i], Rw4], wr=[PR[6 + hf]], sig=(kc == KC - 1))
        ck("p4_o")
        for hf in range(2):
            act(t1[i][:, hf * 512:(hf + 1) * 512], pbank(6 + hf), AF.Square, rd=[PR[6 + hf], Rt1[i]],
                wr=[Rt1[i], Rpst[i]], accum=pst[i][:, hf:hf + 1])
        tt("dve", pst[i][:, 0:1], pst[i][:, 0:1], pst[i][:, 1:2], ALU.add, rd=[Rpst[i]], wr=[Rpst[i]])
        act(pst[i][:, 2:3], pst[i][:, 0:1], AF.Sqrt, rd=[Rpst[i]], wr=[Rpst[i]], bias=EPS, scale=1.0 / D)
        recip(pst[i][:, 3:4], pst[i][:, 2:3], rd=[Rpst[i]], wr=[Rpst[i]])
        for hf in range(2):
            hs = slice(hf * 512, (hf + 1) * 512)
            stt(ob_[i][:, hs], pbank(6 + hf), pst[i][:, 3:4], npost_bc[:, hs], ALU.mult, ALU.mult,
                rd=[PR[6 + hf], Rpst[i], Rw4], wr=[Rob[i]])
        tt("pool", ob_[i][0:n, :], ob_[i][0:n, :], xr[i][0:n, :], ALU.add, rd=[Rob[i], Rin[i]], wr=[Rob[i]])
        ck("p4_norm")
        out_toks.append(dma("sp", out_d[128 * (m - 1):128 * m, :], ob_[i][:, :], rd=[Rob[i]]))
        ck("p4_t%d" % m)
    if P.cap is not None:
        P.end_capture()
    P.barrier()
    P.emit(sems)
    es.close()
    return nc, A.peak


def _prep_inputs(inputs, b):
    f = lambda a: np.ascontiguousarray(np.asarray(a, dtype=np.float32))
    conv_wb = np.concatenate([np.asarray(inputs["conv_w"][0]), np.asarray(inputs["conv_b"])], axis=0)
    return {
        "x": f(inputs["x"][b]),
        "meta": f(inputs["meta_tokens"]),
        "norm_pre": f(np.asarray(inputs["norm_pre"]).reshape(KC, 128)),
        "w_in": f(inputs["w_in"][0]),
        "conv_wb": f(conv_wb),
        "dt_bias": f(inputs["dt_bias"]),
        "a_log": f(inputs["a_log"]),
        "d_skip": f(inputs["d_skip"]),
        "ssd_norm": f(np.asarray(inputs["ssd_norm"]).reshape(16, 128)),
        "fgate_bias": f(np.asarray(inputs["fgate_bias"]).reshape(16, 1)),
        "gate_bias": f(inputs["gate_bias"]),
        "w_proj_ssd": f(inputs["w_proj_ssd"][0]),
        "w_proj_att": f(inputs["w_proj_att"][0]),
        "w_out": f(inputs["w_out"][0]),
        "norm_post": f(inputs["norm_post"]),
    }


def kernel(**inputs):
    x = np.asarray(inputs["x"])
    B, SEQ, _ = x.shape
    nc = bass.Bass("TRN2", target_bir_lowering=False)
    build(nc, SEQ)
    in_maps = [_prep_inputs(inputs, b) for b in range(B)]
    res = run_bass_kernel_spmd(nc, in_maps, core_ids=list(range(B)))
    return np.stack([np.asarray(r["out"], dtype=np.float32) for r in res.results], axis=0)
```
